# Optimizing a Trainium2 kernel written in Bass

```python
import numpy as np
import jax
import jax.numpy as jnp
from jax import lax

D_MODEL = 2048
BATCH = 16
SEQ = 2048
DEPTH = 2

HEAD_DIM = 128
A_HEADS = D_MODEL // (2 * HEAD_DIM)
B_HEADS = D_MODEL // (2 * HEAD_DIM)
C_HEADS = D_MODEL // (2 * HEAD_DIM)
C_KV_HEADS = C_HEADS // 4
D_HEADS = D_MODEL // (2 * HEAD_DIM)
A_WIDTH = A_HEADS * HEAD_DIM
B_WIDTH = B_HEADS * HEAD_DIM
C_WIDTH = C_HEADS * HEAD_DIM
D_WIDTH = D_HEADS * HEAD_DIM
KV_WIDTH = C_KV_HEADS * HEAD_DIM
A_CONV = 4
MLSTM_CHUNK = 64
SB_BLOCK = 128
CMP_LEN = 32
CMP_STRIDE = 16
SEL_LEN = 64
N_SEL = 8
N_LOCAL = 2
FORCE_BONUS = 1000.0
SEL_Q_BLOCK = 32
WINDOW = 256
WIN_BLOCK = 128
HGRN_CHUNK = 32
FFN_DIM = 256 * ((8 * D_MODEL // 3 + 255) // 256)
FFN_CONV = 3
ROPE_THETA = 500000.0
ROPE_DIM = HEAD_DIM // 4
RMS_EPS = 1e-6

kernel_name = 'hybrid_mlstm_stickbreak_nsa_hgrn2_block'


def rms_norm(x, g):
    xf = x.astype(jnp.float32)
    y = xf * lax.rsqrt(jnp.mean(xf * xf, axis=-1, keepdims=True) + RMS_EPS)
    return (y * g.astype(jnp.float32)).astype(x.dtype)


def head_rms_norm(a, g):
    return rms_norm(a, g.reshape(a.shape[1], 1, a.shape[3]))


def split_heads(a, n):
    b, s, _ = a.shape
    return a.reshape(b, s, n, -1).transpose(0, 2, 1, 3)


def merge_heads(a):
    b, h, s, d = a.shape
    return a.transpose(0, 2, 1, 3).reshape(b, s, h * d)


def split_cols(p, sizes):
    return jnp.split(p, [int(c) for c in np.cumsum(sizes)[:-1]], axis=-1)


def causal_dwconv(x, w, b):
    k = w.shape[0]
    y = lax.conv_general_dilated(x, w[:, None, :].astype(x.dtype), window_strides=(1,),
                                 padding=[(k - 1, 0)], dimension_numbers=('NWC', 'WIO', 'NWC'),
                                 feature_group_count=x.shape[-1])
    return y + b.astype(x.dtype)


def rope_partial(x, pos):
    half = ROPE_DIM // 2
    freqs = ROPE_THETA ** (-jnp.arange(half, dtype=jnp.float32) / half)
    ang = jnp.asarray(pos).astype(jnp.float32)[:, None] * freqs
    cos, sin = jnp.cos(ang), jnp.sin(ang)
    xr = x[..., :ROPE_DIM].astype(jnp.float32)
    x1, x2 = xr[..., :half], xr[..., half:]
    rot = jnp.concatenate([x1 * cos - x2 * sin, x2 * cos + x1 * sin], axis=-1).astype(x.dtype)
    return jnp.concatenate([rot, x[..., ROPE_DIM:]], axis=-1)


def masked_softmax(s, mask):
    s = jnp.where(mask, s.astype(jnp.float32), -jnp.inf)
    m = jnp.max(s, axis=-1, keepdims=True)
    m = jnp.where(jnp.isfinite(m), m, 0.0)
    e = jnp.exp(s - m)
    return e / jnp.maximum(jnp.sum(e, axis=-1, keepdims=True), 1e-30)


def mlstm_chunkwise(q, k, v, i_pre, f_pre):
    b_, h_, s_, d = q.shape
    L = min(MLSTM_CHUNK, s_)
    nc = s_ // L
    f32 = jnp.float32
    q = q.astype(f32)
    k = k.astype(f32) * (d ** -0.5)
    v = v.astype(f32)
    log_i = i_pre.astype(f32)
    log_f = jax.nn.log_sigmoid(f_pre.astype(f32))

    def chunk(a):
        return jnp.moveaxis(a.reshape((b_, h_, nc, L) + a.shape[3:]), 2, 0)

    causal = np.tril(np.ones((L, L), dtype=bool))

    def step(carry, inp):
        C, n, m = carry
        qc, kc, vc, ic, fc = inp
        b = jnp.cumsum(fc, axis=-1)
        g = b + m[..., None]
        dmat = jnp.where(causal, b[..., :, None] - b[..., None, :] + ic[..., None, :], -jnp.inf)
        m_row = jnp.maximum(g, jnp.max(dmat, axis=-1))
        w = jnp.exp(dmat - m_row[..., None]) * jnp.einsum('bhtd,bhsd->bhts', qc, kc)
        w_state = jnp.exp(g - m_row)
        num = (w_state[..., None] * jnp.einsum('bhtd,bhde->bhte', qc, C)
               + jnp.einsum('bhts,bhse->bhte', w, vc))
        den = w_state * jnp.einsum('bhtd,bhd->bht', qc, n) + jnp.sum(w, axis=-1)
        h = num / jnp.maximum(jnp.abs(den), jnp.exp(-m_row))[..., None]
        b_last = b[..., -1]
        w_new = b_last[..., None] - b + ic
        m_new = jnp.maximum(b_last + m, jnp.max(w_new, axis=-1))
        decay = jnp.exp(b_last + m - m_new)
        kw = jnp.exp(w_new - m_new[..., None])[..., None] * kc
        C = decay[..., None, None] * C + jnp.einsum('bhsd,bhse->bhde', kw, vc)
        n = decay[..., None] * n + jnp.sum(kw, axis=-2)
        return (C, n, m_new), h

    init = (jnp.zeros((b_, h_, d, v.shape[-1]), f32), jnp.zeros((b_, h_, d), f32),
            jnp.zeros((b_, h_), f32))
    _, hs = lax.scan(step, init, (chunk(q), chunk(k), chunk(v), chunk(log_i), chunk(log_f)))
    return jnp.moveaxis(hs, 0, 2).reshape(b_, h_, s_, -1)


def stick_breaking(q, k, v):
    s_, d = q.shape[2], q.shape[3]
    qb_size = min(SB_BLOCK, s_)
    outs = []
    for blk in range(s_ // qb_size):
        t0, t1 = blk * qb_size, (blk + 1) * qb_size
        z = jnp.einsum('bhtd,bhsd->bhts', q[:, :, t0:t1], k[:, :, :t1]).astype(jnp.float32) * (d ** -0.5)
        strict = np.arange(t1)[None, :] < np.arange(t0, t1)[:, None]
        log_keep = jnp.where(strict, jax.nn.log_sigmoid(-z), 0.0)
        later = lax.cumsum(log_keep, axis=3, reverse=True) - log_keep
        a = jnp.where(strict, jnp.exp(jax.nn.log_sigmoid(z) + later), 0.0)
        outs.append(jnp.einsum('bhts,bhse->bhte', a.astype(v.dtype), v[:, :, :t1]))
    return jnp.concatenate(outs, axis=2)


def nsa_attention(q, k_cmp, v_cmp, k_slc, v_slc, k_win, v_win, gates, cmp_pos, cmp_w1, cmp_w2):
    b_, g_, r_, s_, d = q.shape
    scale = d ** -0.5
    t = np.arange(s_)
    n_cmp = (s_ - CMP_LEN) // CMP_STRIDE + 1
    cidx = np.arange(n_cmp)[:, None] * CMP_STRIDE + np.arange(CMP_LEN)[None, :]
    cend = cidx[:, -1]

    def compress(a, pos_emb, w1, w2):
        blocks = a[:, :, cidx] + pos_emb
        hid = jax.nn.gelu(jnp.einsum('bgnld,lde->bgne', blocks, w1))
        return jnp.einsum('bgne,ef->bgnf', hid, w2)

    kc = rope_partial(compress(k_cmp, cmp_pos[0], cmp_w1[0], cmp_w2[0]), cend)
    vc = compress(v_cmp, cmp_pos[1], cmp_w1[1], cmp_w2[1])
    s_c = jnp.einsum('bgrtd,bgnd->bgrtn', q, kc).astype(jnp.float32) * scale
    p_cmp = masked_softmax(s_c, cend[None, :] <= t[:, None])
    o_cmp = jnp.einsum('bgrtn,bgnd->bgrtd', p_cmp.astype(vc.dtype), vc)

    n_sel = s_ // SEL_LEN
    cs = np.arange(n_cmp) * CMP_STRIDE
    ss = np.arange(n_sel) * SEL_LEN
    cover = ((cs[:, None] < ss[None, :] + SEL_LEN) & (cs[:, None] + CMP_LEN > ss[None, :])).astype(np.float32)
    p_slc = jnp.einsum('bgrtn,nj->bgtj', p_cmp, cover)
    blk_t = t // SEL_LEN
    jb = np.arange(n_sel)
    started = jb[None, :] <= blk_t[:, None]
    forced = (jb[None, :] == 0) | (started & (blk_t[:, None] - jb[None, :] < N_LOCAL))
    score = jnp.where(started, p_slc + jnp.where(forced, FORCE_BONUS, 0.0), -jnp.inf)
    n_top = min(N_SEL, n_sel)
    _, sel_idx = lax.top_k(score, n_top)

    qb_size = min(SEL_Q_BLOCK, s_)
    nqb = s_ // qb_size
    kb = k_slc.reshape(b_, g_, n_sel, SEL_LEN, d)
    vb = v_slc.reshape(b_, g_, n_sel, SEL_LEN, d)
    gather = jax.vmap(jax.vmap(lambda blocks, ix: blocks[ix]))
    m_tok = n_top * SEL_LEN

    def sel_block(args):
        qq, ix, tt = args
        kg = gather(kb, ix).reshape(b_, g_, qb_size, m_tok, d)
        vg = gather(vb, ix).reshape(b_, g_, qb_size, m_tok, d)
        tok = (ix[..., None] * SEL_LEN + jnp.arange(SEL_LEN)).reshape(b_, g_, 1, qb_size, m_tok)
        mask = tok <= tt[:, None]
        s = jnp.einsum('bgrqd,bgqmd->bgrqm', qq, kg).astype(jnp.float32) * scale
        p = masked_softmax(s, mask)
        return jnp.einsum('bgrqm,bgqmd->bgrqd', p.astype(vg.dtype), vg)

    q_blk = jnp.moveaxis(q.reshape(b_, g_, r_, nqb, qb_size, d), 3, 0)
    ix_blk = jnp.moveaxis(sel_idx.reshape(b_, g_, nqb, qb_size, n_top), 2, 0)
    t_blk = jnp.arange(s_).reshape(nqb, qb_size)
    o_slc = lax.map(sel_block, (q_blk, ix_blk, t_blk))
    o_slc = jnp.moveaxis(o_slc, 0, 3).reshape(b_, g_, r_, s_, d)

    nwb = s_ // WIN_BLOCK
    nback = WINDOW // WIN_BLOCK
    band_len = (nback + 1) * WIN_BLOCK

    def band(a):
        ab = a.reshape(b_, g_, nwb, WIN_BLOCK, d)
        ab = jnp.pad(ab, ((0, 0), (0, 0), (nback, 0), (0, 0), (0, 0)))
        return jnp.concatenate([ab[:, :, i:i + nwb] for i in range(nback + 1)], axis=3)

    rr = np.arange(WIN_BLOCK)[:, None]
    jj = np.arange(band_len)[None, :]
    diff = rr + nback * WIN_BLOCK - jj
    s_abs = (np.arange(nwb)[:, None, None] - nback) * WIN_BLOCK + jj[None]
    mask_w = (diff >= 0) & (diff < WINDOW) & (s_abs >= 0)
    qw = q.reshape(b_, g_, r_, nwb, WIN_BLOCK, d)
    sw = jnp.einsum('bgrcqd,bgckd->bgrcqk', qw, band(k_win)).astype(jnp.float32) * scale
    pw = masked_softmax(sw, mask_w)
    o_win = jnp.einsum('bgrcqk,bgckd->bgrcqd', pw.astype(v_win.dtype), band(v_win)).reshape(b_, g_, r_, s_, d)

    return gates[..., 0:1] * o_cmp + gates[..., 1:2] * o_slc + gates[..., 2:3] * o_win


def hgrn2_chunkwise(q, log_f, k, v):
    b_, h_, s_, dk = q.shape
    L = min(HGRN_CHUNK, s_)
    nc = s_ // L

    def chunk(a):
        return jnp.moveaxis(a.reshape(b_, h_, nc, L, a.shape[-1]), 2, 0)

    causal = np.tril(np.ones((L, L), dtype=bool))[:, :, None]

    def step(state, inp):
        qc, gc, kc, vc = inp
        bc = jnp.cumsum(gc, axis=2)
        o_state = jnp.einsum('bhtd,bhde->bhte', qc * jnp.exp(bc), state)
        rel = jnp.exp(jnp.where(causal, bc[:, :, :, None, :] - bc[:, :, None, :, :], -jnp.inf))
        att = jnp.einsum('bhtd,bhtsd,bhsd->bhts', qc, rel, kc)
        o = o_state + jnp.einsum('bhts,bhse->bhte', att, vc)
        b_last = bc[:, :, -1:]
        state = (jnp.exp(b_last[:, :, 0])[..., None] * state
                 + jnp.einsum('bhsd,bhse->bhde', kc * jnp.exp(b_last - bc), vc))
        return state, o

    init = jnp.zeros((b_, h_, dk, v.shape[-1]), jnp.float32)
    _, os_ = lax.scan(step, init, (chunk(q), chunk(log_f), chunk(k), chunk(v)))
    return jnp.moveaxis(os_, 0, 2).reshape(b_, h_, s_, -1)


def mixer_ab(u, w_in, conv_w, conv_b, gate_b, head_norm, w_out):
    aq, ak, av, ao, ai, af, bq, bk, bv = split_cols(
        u @ w_in, [A_WIDTH] * 4 + [A_HEADS, A_HEADS] + [B_WIDTH] * 3)
    qk = jax.nn.silu(causal_dwconv(jnp.concatenate([aq, ak], axis=-1), conv_w, conv_b))
    aq, ak = jnp.split(qk, 2, axis=-1)
    i_pre = (ai + gate_b[:A_HEADS]).transpose(0, 2, 1)
    f_pre = (af + gate_b[A_HEADS:]).transpose(0, 2, 1)
    h_a = mlstm_chunkwise(split_heads(aq, A_HEADS), split_heads(ak, A_HEADS),
                          split_heads(av, A_HEADS), i_pre, f_pre)
    h_a = merge_heads(head_rms_norm(h_a, head_norm)) * jax.nn.sigmoid(ao.astype(jnp.float32))
    h_b = merge_heads(stick_breaking(split_heads(bq, B_HEADS), split_heads(bk, B_HEADS),
                                     split_heads(bv, B_HEADS)))
    mixed = jnp.concatenate([h_a.astype(u.dtype), h_b.astype(u.dtype)], axis=-1)
    return mixed @ w_out


def mixer_cd(u, w_in, cmp_pos, cmp_w1, cmp_w2, lower_bound, head_norm, w_out):
    b_, s_, _ = u.shape
    (cq, ck_c, cv_c, ck_s, cv_s, ck_w, cv_w, c_gate, dq, df, di, dg) = split_cols(
        u @ w_in, [C_WIDTH] + [KV_WIDTH] * 6 + [3 * C_HEADS] + [D_WIDTH] * 4)
    pos = jnp.arange(s_)
    r_ = C_HEADS // C_KV_HEADS
    q = rope_partial(split_heads(cq, C_HEADS), pos).reshape(b_, C_KV_HEADS, r_, s_, HEAD_DIM)
    gates = jax.nn.sigmoid(c_gate.astype(jnp.float32)).reshape(b_, s_, C_KV_HEADS, r_, 3).transpose(0, 2, 3, 1, 4)
    o_c = nsa_attention(q,
                        split_heads(ck_c, C_KV_HEADS), split_heads(cv_c, C_KV_HEADS),
                        rope_partial(split_heads(ck_s, C_KV_HEADS), pos), split_heads(cv_s, C_KV_HEADS),
                        rope_partial(split_heads(ck_w, C_KV_HEADS), pos), split_heads(cv_w, C_KV_HEADS),
                        gates, cmp_pos, cmp_w1, cmp_w2)
    o_c = merge_heads(o_c.reshape(b_, C_HEADS, s_, HEAD_DIM))
    lb = lower_bound.astype(jnp.float32).reshape(1, D_HEADS, 1, HEAD_DIM)
    fp = split_heads(df, D_HEADS).astype(jnp.float32)
    log_f = jnp.logaddexp(jnp.log(lb), jnp.log1p(-lb) + jax.nn.log_sigmoid(fp))
    k_in = (1.0 - lb) * jax.nn.sigmoid(-fp)
    o_d = hgrn2_chunkwise(split_heads(dq, D_HEADS).astype(jnp.float32), log_f, k_in,
                          split_heads(di, D_HEADS).astype(jnp.float32))
    o_d = merge_heads(head_rms_norm(o_d, head_norm)) * jax.nn.silu(dg.astype(jnp.float32))
    mixed = jnp.concatenate([o_c.astype(u.dtype), o_d.astype(u.dtype)], axis=-1)
    return mixed @ w_out


def conv_ffn(u, w_up, conv_w, conv_b, w_down):
    gate, up = jnp.split(u @ w_up, 2, axis=-1)
    gate = causal_dwconv(gate, conv_w, conv_b)
    return (jax.nn.silu(gate) * up) @ w_down


def setup_inputs(seed: int = 0) -> dict:
    key = jax.random.key(seed)
    ks = iter(jax.random.split(key, 32))
    ne = (DEPTH + 1) // 2
    no = DEPTH // 2

    def nrm(shape, scale):
        return scale * jax.random.normal(next(ks), shape, jnp.float32)

    def gain(shape):
        return 1.0 + nrm(shape, 0.1)

    ab_in = 4 * A_WIDTH + 2 * A_HEADS + 3 * B_WIDTH
    cd_in = C_WIDTH + 6 * KV_WIDTH + 3 * C_HEADS + 4 * D_WIDTH
    x = nrm((BATCH, SEQ, D_MODEL), 1.0)
    norm_mix = gain((DEPTH, D_MODEL))
    norm_ffn = gain((DEPTH, D_MODEL))
    norm_final = gain((D_MODEL,))
    ab_w_in = nrm((ne, D_MODEL, ab_in), D_MODEL ** -0.5)
    ab_conv_w = nrm((ne, A_CONV, 2 * A_WIDTH), A_CONV ** -0.5)
    ab_conv_b = nrm((ne, 2 * A_WIDTH), 0.02)
    ab_gate_b = jnp.concatenate([nrm((ne, A_HEADS), 0.1), 3.0 + nrm((ne, A_HEADS), 0.5)], axis=-1)
    ab_head_norm = gain((ne, A_WIDTH))
    ab_w_out = nrm((ne, A_WIDTH + B_WIDTH, D_MODEL), (A_WIDTH + B_WIDTH) ** -0.5)
    cd_w_in = nrm((no, D_MODEL, cd_in), D_MODEL ** -0.5)
    cd_cmp_pos = nrm((no, 2, CMP_LEN, HEAD_DIM), 0.1)
    cd_cmp_w1 = nrm((no, 2, CMP_LEN, HEAD_DIM, HEAD_DIM), (CMP_LEN * HEAD_DIM) ** -0.5)
    cd_cmp_w2 = nrm((no, 2, HEAD_DIM, HEAD_DIM), HEAD_DIM ** -0.5)
    hgrn_gamma = nrm((DEPTH, D_WIDTH), 0.5)
    cd_head_norm = gain((no, D_WIDTH))
    cd_w_out = nrm((no, C_WIDTH + D_WIDTH, D_MODEL), (C_WIDTH + D_WIDTH) ** -0.5)
    ffn_w_up = nrm((DEPTH, D_MODEL, 2 * FFN_DIM), D_MODEL ** -0.5)
    ffn_conv_w = nrm((DEPTH, FFN_CONV, FFN_DIM), FFN_CONV ** -0.5)
    ffn_conv_b = nrm((DEPTH, FFN_DIM), 0.02)
    ffn_w_down = nrm((DEPTH, FFN_DIM, D_MODEL), FFN_DIM ** -0.5)
    return {'x': x, 'norm_mix': norm_mix, 'norm_ffn': norm_ffn, 'norm_final': norm_final,
            'ab_w_in': ab_w_in, 'ab_conv_w': ab_conv_w, 'ab_conv_b': ab_conv_b, 'ab_gate_b': ab_gate_b,
            'ab_head_norm': ab_head_norm, 'ab_w_out': ab_w_out, 'cd_w_in': cd_w_in,
            'cd_cmp_pos': cd_cmp_pos, 'cd_cmp_w1': cd_cmp_w1, 'cd_cmp_w2': cd_cmp_w2,
            'hgrn_gamma': hgrn_gamma, 'cd_head_norm': cd_head_norm, 'cd_w_out': cd_w_out,
            'ffn_w_up': ffn_w_up, 'ffn_conv_w': ffn_conv_w, 'ffn_conv_b': ffn_conv_b,
            'ffn_w_down': ffn_w_down}


def reference(x, norm_mix, norm_ffn, norm_final, ab_w_in, ab_conv_w, ab_conv_b, ab_gate_b,
              ab_head_norm, ab_w_out, cd_w_in, cd_cmp_pos, cd_cmp_w1, cd_cmp_w2, hgrn_gamma,
              cd_head_norm, cd_w_out, ffn_w_up, ffn_conv_w, ffn_conv_b, ffn_w_down):
    sm = jax.nn.softmax(hgrn_gamma.astype(jnp.float32), axis=0)
    lb_all = jnp.cumsum(sm, axis=0) - sm[0]
    h = x
    for layer in range(DEPTH):
        j = layer // 2
        u = rms_norm(h, norm_mix[layer])
        if layer % 2 == 0:
            h = h + mixer_ab(u, ab_w_in[j], ab_conv_w[j], ab_conv_b[j], ab_gate_b[j],
                             ab_head_norm[j], ab_w_out[j]).astype(h.dtype)
        else:
            h = h + mixer_cd(u, cd_w_in[j], cd_cmp_pos[j], cd_cmp_w1[j], cd_cmp_w2[j],
                             lb_all[layer], cd_head_norm[j], cd_w_out[j]).astype(h.dtype)
        h = h + conv_ffn(rms_norm(h, norm_ffn[layer]), ffn_w_up[layer], ffn_conv_w[layer],
                         ffn_conv_b[layer], ffn_w_down[layer]).astype(h.dtype)
    return rms_norm(h, norm_final)
```

```python
import contextlib
import numpy as np
import ml_dtypes
import concourse.bass as bass
import concourse.mybir as mybir
from concourse.bass_utils import run_bass_kernel_spmd

F32 = mybir.dt.float32
BF16 = mybir.dt.bfloat16
AF = mybir.ActivationFunctionType
ALU = mybir.AluOpType
AX = mybir.AxisListType

D = 2048
S = 2048
NSEQ = 2
NT = S // 128
HD = 128
FFN = 5632
NF = FFN // 128
AB_IN = 7184
CD_IN = 6680
EPS = 1e-6
SEM_LIMIT = 30000


class Trk:
    __slots__ = ("w", "r", "x")

    def __init__(self):
        self.w = None
        self.r = {}
        self.x = False


class Sched:
    def __init__(self, nc, es):
        self.nc = nc
        self.es = es
        self.names = ["pe", "act", "dve", "pool", "sp"]
        self.prog = {e: [] for e in self.names}
        self.seen = {e: {} for e in self.names}
        self.sems = []
        self.csem = {}
        self.dq = {}
        self.dqi = {}
        self.ndma = 0

    def newsem(self):
        h = self.es.enter_context(self.nc.semaphore("s%d" % len(self.sems)))
        self.sems.append(h)
        return len(self.sems) - 1

    def _collect(self, eng, reads, writes):
        need = {}

        def add(tok, war):
            if tok is None:
                return
            s, v, e = tok
            if e == eng:
                if eng == "pe" or war:
                    return
            if self.seen[eng].get(s, 0) >= v:
                return
            if need.get(s, 0) < v:
                need[s] = v

        for b in reads:
            add(b.w, False)
            if b.x:
                for t in b.r.values():
                    add(t, True)
        for b in writes:
            add(b.w, False)
            for t in b.r.values():
                add(t, True)
        return need

    def _commit(self, eng, need):
        for s, v in need.items():
            self.seen[eng][s] = v
        return list(need.items())

    def op(self, eng, fn, reads=(), writes=()):
        need = self._collect(eng, reads, writes)
        cs = self.csem.get(eng)
        if cs is None or cs[1] >= SEM_LIMIT:
            cs = [self.newsem(), 0]
            self.csem[eng] = cs
        cs[1] += 1
        tok = (cs[0], cs[1], eng)
        self.prog[eng].append((self._commit(eng, need), fn, (cs[0], 1)))
        for b in reads:
            b.r[eng] = tok
        for b in writes:
            b.w = tok
            b.r = {}
        return tok

    def dma(self, q, out, in_, reads=(), writes=(), cast=False, **kw):
        if q == "pool" and not cast and getattr(self, "avoid_pool", False):
            q = "sp"
        need = self._collect(q, reads, writes)
        if q not in self.dq:
            self.dq[q] = [[self.newsem(), 0] for _ in range(6)]
            self.dqi[q] = 0
        lst = self.dq[q]
        i = self.dqi[q]
        self.dqi[q] = (i + 1) % len(lst)
        if lst[i][1] + 16 > SEM_LIMIT:
            lst[i] = [self.newsem(), 0]
        ds = lst[i]
        if ds[1] > 0 and self.seen[q].get(ds[0], 0) < ds[1]:
            if need.get(ds[0], 0) < ds[1]:
                need[ds[0]] = ds[1]
        ds[1] += 16
        tok = (ds[0], ds[1], "dma")
        self.ndma += 1
        key = "d%d" % self.ndma

        def fn(e, out=out, in_=in_, kw=kw):
            return e.dma_start(out=out, in_=in_, **kw)

        self.prog[q].append((self._commit(q, need), fn, (ds[0], 16)))
        for b in reads:
            if len(b.r) > 24:
                for k in [k for k in b.r if k.startswith("d")][:8]:
                    pass
            b.r[key] = tok
        for b in writes:
            b.w = tok
            b.r = {}
        return tok

    def finish(self, final_toks):
        need = {}
        for s, v, _ in final_toks:
            if need.get(s, 0) < v:
                need[s] = v
        self.final = list(need.items())

    def replay(self):
        nc = self.nc
        E = {"pe": nc.tensor, "act": nc.scalar, "dve": nc.vector, "pool": nc.gpsimd, "sp": nc.sync}
        sems = self.sems
        with nc.Block() as block:
            def run(name):
                def body(eng):
                    for waits, fn, inc in self.prog[name]:
                        for s, v in waits:
                            eng.wait_ge(sems[s], v)
                        ins = fn(eng)
                        ins.then_inc(sems[inc[0]], inc[1])
                    if name == "sp":
                        for s, v in self.final:
                            eng.wait_ge(sems[s], v)
                return body

            block.tensor(run("pe"))
            block.scalar(run("act"))
            block.vector(run("dve"))
            block.gpsimd(run("pool"))
            block.sync(run("sp"))


class T:
    def __init__(self, ap):
        self.ap = ap
        self.t = {}

    def k(self, *keys):
        out = []
        for key in keys:
            if key not in self.t:
                tr = Trk()
                base = self.t.get(None)
                if base is not None:
                    tr.w = base.w
                    tr.r = dict(base.r)
                    tr.x = base.x
                self.t[key] = tr
            out.append(self.t[key])
        return out

    def all(self):
        if None not in self.t:
            self.t[None] = Trk()
        return list(self.t.values())


class Prog:
    def __init__(self, nc, es, sc):
        self.nc = nc
        self.es = es
        self.sc = sc
        self._n = 0
        self.qi = 0

    def name(self, p):
        self._n += 1
        return "%s_%d" % (p, self._n)

    def sb(self, shape, dt, nm="sb"):
        return T(self.es.enter_context(self.nc.sbuf_tensor(self.name(nm), list(shape), dt)))

    def ps(self, shape, dt=F32, nm="ps"):
        return T(self.es.enter_context(self.nc.psum_tensor(self.name(nm), list(shape), dt)))

    def dram(self, nm, shape, dt, kind="Internal"):
        return T(self.nc.dram_tensor(nm, list(shape), dt, kind=kind).ap())

    def q(self):
        self.qi += 1
        return "sp" if self.qi % 2 else "pool"


def bc16(ap):
    return ap.bitcast(BF16)


class Net(Prog):
    def setup(self, ext):
        nc, sc = self.nc, self.sc
        self.ext = ext
        self.deferred = []
        self.A = self.sb([128, 16 * 2048], BF16, "A")
        self.slab = [self.sb([128, 8192], BF16, "slab") for _ in range(2)]
        self.slab_i = 0
        self.F = [self.sb([128, 2052], F32, "F") for _ in range(6)]
        self.H = [self.sb([128, 2048], BF16, "H") for _ in range(6)]
        self.banks = [self.ps([128, 512], F32, "bank") for _ in range(8)]
        for b in self.banks:
            b.all()[0].x = True
        self.bank_i = 0
        self.cst = self.sb([128, 1024], BF16, "cst")
        self.cstf = self.sb([128, 512], F32, "cstf")
        self.vec = self.sb([128, 1024], F32, "vec")
        sc.dma("sp", self.cst.ap[:, :], ext["cst"].ap[:, :], reads=ext["cst"].all(), writes=self.cst.all())
        sc.dma("sp", self.cstf.ap[:, :], ext["cstf"].ap[:, :], reads=ext["cstf"].all(), writes=self.cstf.all())
        sc.dma("sp", self.vec.ap[:, :], ext["vec"].ap[:, :], reads=ext["vec"].all(), writes=self.vec.all())
        self.identb = self.cst.ap[:, 0:128]
        self.onesb = self.cst.ap[:, 128:256]
        self.identf = self.cstf.ap[:, 0:128]
        self.maskUTb = self.cst.ap[:, 256:384]
        self.maskSLb = self.cst.ap[:, 384:512]
        self.triF = self.cstf.ap[:, 128:256]
        self.onesF = self.cstf.ap[:, 256:384]
        self.maskSLf = self.cstf.ap[:, 384:512]
        self.ones = self.sb([128, 2048], F32, "ones")
        sc.op("pool", lambda e: e.memset(self.ones.ap[:, :], 1.0), writes=self.ones.all())
        self.hn = self.sb([128, 1024], F32, "hn")
        self.sm = self.sb([128, 2048], F32, "sm")
        self.smi = 0

    @staticmethod
    def pipe(items, load, compute):
        if not items:
            return
        h = load(items[0])
        for i, it in enumerate(items):
            nh = load(items[i + 1]) if i + 1 < len(items) else None
            compute(it, h)
            h = nh

    def bank(self):
        b = self.banks[self.bank_i % 8]
        self.bank_i += 1
        return b

    def next_slab(self):
        s = self.slab[self.slab_i % 2]
        self.slab_i += 1
        return s

    def run_deferred(self, n):
        for _ in range(n):
            if self.deferred:
                self.deferred.pop(0)()

    def cast_up(self, src, dst):
        for k in range(16):
            for half in range(2):
                for fh in range(2):
                    f0 = fh * 22
                    c0 = half * FFN + f0 * 128
                    self.sc.dma("pool", dst.ap[f0:f0 + 22, :, k, half * 128:(half + 1) * 128],
                                src.ap[k * 128:(k + 1) * 128, c0:c0 + 22 * 128].rearrange("p (f c) -> f p c", c=128),
                                reads=src.k(k), writes=dst.k((k, half, fh)), cast=True)

    def cast_dn(self, src, dst):
        for k in range(NF):
            self.sc.dma("pool", dst.ap[:, :, k, :], src.ap[k * 128:(k + 1) * 128, :].rearrange("p (n c) -> n p c", c=128),
                        reads=src.k(k), writes=dst.k(k), cast=True)

    def cast_weight(self, src, dst, rows, cols, split=None):
        if split:
            dst.split = split
            for (c0, c1, hk) in ((0, split, 0), (split, cols, 1)):
                for r in range(0, rows, 128):
                    self.sc.dma("pool", dst.ap[r:r + 128, c0:c1], src.ap[r:r + 128, c0:c1], reads=src.k(r), writes=dst.k((r, hk)), cast=True)
            return
        for r in range(0, rows, 128):
            self.sc.dma("pool", dst.ap[r:r + 128, :], src.ap[r:r + 128, :], reads=src.k(r), writes=dst.k(r), cast=True)

    def x_to_hT(self, x_ap, xT, hT, base=0):
        sc = self.sc
        hv = hT.ap.rearrange("(c p) t -> p c t", p=128)
        for tt in range(NT):
            xt = self.F[base + tt % 2]
            ot = self.F[base + 2 + tt % 2]
            sc.dma("sp", xt.ap[:, 0:2048], x_ap[tt * 128:(tt + 1) * 128, :], reads=xT.all(), writes=xt.all())
            for g in range(4):
                b = self.bank()
                for j in range(4):
                    c = g * 4 + j
                    sc.op("pe", lambda e, b=b, j=j, c=c, xt=xt: e.transpose(
                        b.ap[:, j * 128:(j + 1) * 128], xt.ap[:, c * 128:(c + 1) * 128], self.identf),
                        reads=xt.all() + self.cstf.all(), writes=b.all())
                if g % 2 == 0:
                    sc.op("act", lambda e, b=b, g=g, ot=ot: e.activation(ot.ap[:, g * 512:(g + 1) * 512], b.ap[:, :], AF.Copy),
                          reads=b.all(), writes=ot.all())
                else:
                    sc.op("dve", lambda e, b=b, g=g, ot=ot: e.tensor_copy(ot.ap[:, g * 512:(g + 1) * 512], b.ap[:, :]),
                          reads=b.all(), writes=ot.all())
            sc.dma("pool", hv[:, :, tt * 128:(tt + 1) * 128], ot.ap[:, 0:2048].rearrange("p (c t) -> p c t", c=16),
                   reads=ot.all(), writes=hT.k(*range(16)))

    def norm_to_A(self, hT, gcol, tok0=False):
        sc = self.sc
        Av = self.A.ap.rearrange("p (c t) -> p c t", c=16)
        bs = [self.bank() for _ in range(4)]
        rstd = self.F[5]
        if getattr(self, "ssq_ready", False):
            acc = self.F[0]
            for j in range(4):
                self.o("pe", "matmul", bs[j].ap[:, :], self.onesF, acc.ap[:, j * 512:(j + 1) * 512], start=True, stop=True,
                       reads=acc.all() + self.cstf.all(), writes=bs[j].all())
            self.ssq_ready = False
        else:
          for c in range(16):
            ht = self.F[c % 2]
            sq = self.H[c % 2]
            sc.dma(self.q(), ht.ap[:, 0:2048], hT.ap[c * 128:(c + 1) * 128, :], reads=hT.k(c), writes=ht.all())
            sc.op("act", lambda e, ht=ht, sq=sq: e.activation(sq.ap[:, :], ht.ap[:, 0:2048], AF.Square),
                  reads=ht.all(), writes=sq.all())
            for j in range(4):
                sc.op("pe", lambda e, j=j, c=c, sq=sq: e.matmul(bs[j].ap[:, :], self.onesb, sq.ap[:, j * 512:(j + 1) * 512],
                                                             start=(c == 0), stop=(c == 15)),
                      reads=sq.all() + self.cst.all(), writes=bs[j].all())
        for j in range(4):
            sc.op("act", lambda e, j=j: e.activation(rstd.ap[:, j * 512:(j + 1) * 512], bs[j].ap[:, :], AF.Sqrt,
                                                      scale=1.0 / D, bias=EPS),
                  reads=bs[j].all(), writes=rstd.all())
        sc.op("dve", lambda e: e.reciprocal(rstd.ap[:, 0:2048], rstd.ap[:, 0:2048]), reads=rstd.all(), writes=rstd.all())
        if tok0:
            h0 = self.sm.ap[:, 1944:1960]
            u0 = self.sm.ap[:, 1960:1976]
            k0 = self.sm.k("u0")
            sc.dma("sp", h0.rearrange("p (c o) -> p c o", o=1), hT.ap.rearrange("(c p) t -> p c t", p=128)[:, :, 0:1],
                   reads=hT.k(*range(16)), writes=k0, allow_slow_non_contiguous=True)
            self.o("dve", "scalar_tensor_tensor", out=u0, in0=h0, scalar=rstd.ap[:, 0:1], in1=self.vec.ap[:, gcol:gcol + 16],
                   op0=ALU.mult, op1=ALU.mult, reads=k0 + rstd.all() + self.vec.all(), writes=k0)
        for c in range(16):
            ht = self.F[2 + c % 2]
            sc.dma(self.q(), ht.ap[:, 0:2048], hT.ap[c * 128:(c + 1) * 128, :], reads=hT.k(c), writes=ht.all())
            sc.op("dve", lambda e, c=c, ht=ht: e.scalar_tensor_tensor(
                out=Av[:, c, :], in0=ht.ap[:, 0:2048], scalar=self.vec.ap[:, gcol + c:gcol + c + 1],
                in1=rstd.ap[:, 0:2048], op0=ALU.mult, op1=ALU.mult),
                reads=ht.all() + rstd.all() + self.vec.all(), writes=self.A.k(c))

    def gemm_fm(self, Wb, col_groups, KC, rhs_fn, rhs_reads, evac, ntc=4):
        sc = self.sc
        Wv = Wb.ap.rearrange("(k p) n -> p k n", p=128)

        def load(grp):
            sl = self.next_slab()
            n = len(grp)
            sv = sl.ap[:, 0:KC * n * 128].rearrange("p (k c) -> p k c", k=KC)
            for i, c0 in enumerate(grp):
                sp_ = getattr(Wb, "split", None)
                if sp_ and c0 + 128 <= sp_:
                    rd = [t for k_, t in Wb.t.items() if isinstance(k_, tuple) and k_[1] == 0]
                else:
                    rd = Wb.all()
                sc.dma(self.q(), sv[:, :, i * 128:(i + 1) * 128], Wv[:, :, c0:c0 + 128], reads=rd, writes=sl.all())
            return sl, sv

        def compute(grp, h):
            sl, sv = h
            for i, c0 in enumerate(grp):
                for tc in range(ntc):
                    b = self.bank()
                    for kc in range(KC):
                        sc.op("pe", lambda e, b=b, kc=kc, i=i, tc=tc, sv=sv: e.matmul(
                            b.ap[:, :], sv[:, kc, i * 128:(i + 1) * 128], rhs_fn(kc, tc), start=(kc == 0), stop=(kc == KC - 1)),
                            reads=sl.all() + rhs_reads(kc), writes=b.all())
                    evac(c0, tc, b)

        self.pipe(col_groups, load, compute)

    def gemm_tm_multi(self, Wb, jobs, KC, lhs_fn, lhs_reads):
        sc = self.sc
        Wv = Wb.ap.rearrange("(k p) n -> p k n", p=128)

        def load(job):
            c0, ncols, evac = job
            sl = self.next_slab()
            sv = sl.ap[:, 0:KC * ncols].rearrange("p (k c) -> p k c", k=KC)
            sc.dma(self.q(), sv[:, :, :], Wv[:, :, c0:c0 + ncols], reads=Wb.all(), writes=sl.all())
            return sl, sv

        def compute(job, h):
            c0, ncols, evac = job
            sl, sv = h
            for tt in range(NT):
                b = self.bank()
                for kc in range(KC):
                    sc.op("pe", lambda e, b=b, kc=kc, tt=tt: e.matmul(
                        b.ap[:, 0:ncols], lhs_fn(kc, tt), sv[:, kc, :], start=(kc == 0), stop=(kc == KC - 1)),
                        reads=sl.all() + lhs_reads(kc), writes=b.all())
                evac(tt, b)

        self.pipe(jobs, load, compute)

    def gemm_tm(self, Wb, c0, ncols, KC, lhs_fn, lhs_reads, evac):
        self.gemm_tm_multi(Wb, [(c0, ncols, evac)], KC, lhs_fn, lhs_reads)

    def tok0_proj(self, Wf, chunks, dst):
        sc = self.sc
        u0 = self.sm.ap[:, 1960:1976]
        for j, c0 in enumerate(chunks):
            b = self.bank()
            for kq in range(4):
                ws = self.F[2 + kq % 2]
                for i in range(4):
                    kc = kq * 4 + i
                    sc.dma(self.q(), ws.ap[:, i * 512:(i + 1) * 512], Wf.ap[kc * 128:(kc + 1) * 128, c0:c0 + 512], reads=Wf.all(), writes=ws.all())
                for i in range(4):
                    kc = kq * 4 + i
                    self.o("pe", "matmul", b.ap[0:1, :], u0[:, kc:kc + 1], ws.ap[:, i * 512:(i + 1) * 512], start=(kc == 0), stop=(kc == 15),
                           reads=ws.all() + self.sm.k("u0"), writes=b.all())
            self.o("act", "activation", dst.ap[0:1, j * 512:(j + 1) * 512], b.ap[0:1, :], AF.Copy, reads=b.all(), writes=dst.all())

    def A_rhs(self):
        Av = self.A.ap.rearrange("p (c t) -> p c t", c=16)
        return (lambda kc, tc: Av[:, kc, tc * 512:(tc + 1) * 512]), (lambda kc: self.A.k(kc))

    def A_lhs(self):
        Av = self.A.ap.rearrange("p (c t) -> p c t", c=16)
        return (lambda kc, tt: Av[:, kc, tt * 128:(tt + 1) * 128]), (lambda kc: self.A.k(kc))

    def resid_evac(self, hT):
        sc = self.sc
        cnt = [0]
        acc = self.F[0]
        seen = set()
        self.ssq_ready = True

        def evac(c0, tc, b):
            i = cnt[0]
            cnt[0] += 1
            t = self.F[2 + i % 3]
            c = c0 // 128
            key = (c, tc)
            sc.dma("sp", t.ap[:, 0:512], hT.ap[c0:c0 + 128, tc * 512:(tc + 1) * 512], reads=hT.k(c), writes=t.all())
            sc.op("dve", lambda e, t=t, b=b: e.tensor_tensor(out=t.ap[:, 0:512], in0=b.ap[:, :], in1=t.ap[:, 0:512], op=ALU.add),
                  reads=b.all() + t.all(), writes=t.all())
            sc.dma("pool", hT.ap[c0:c0 + 128, tc * 512:(tc + 1) * 512], t.ap[:, 0:512], reads=t.all(), writes=hT.k(c))
            accs = acc.ap[:, tc * 512:(tc + 1) * 512]
            if tc not in seen:
                seen.add(tc)
                self.o("act", "activation", accs, t.ap[:, 0:512], AF.Square, reads=t.all(), writes=acc.k(tc))
            else:
                sq = self.H[5].ap.bitcast(F32)[:, (i % 2) * 512:(i % 2 + 1) * 512]
                sqk = self.H[5].k("sq%d" % (i % 2))
                self.o("act", "activation", sq, t.ap[:, 0:512], AF.Square, reads=t.all(), writes=sqk)
                self.o("dve", "tensor_tensor", out=accs, in0=accs, in1=sq, op=ALU.add, reads=sqk + acc.k(tc), writes=acc.k(tc))
        return evac

    def ffn(self, hT, layer, Wup, Wdn, actT):
        sc = self.sc
        self.norm_to_A(hT, 16 + 16 * layer)
        rhs_fn, rhs_reads = self.A_rhs()
        cwo = 80 + layer * 176
        def load_up(f):
            sl = self.next_slab()
            sv = sl.ap[:, 0:16 * 256].rearrange("p (k c) -> p k c", k=16)
            sc.dma("sp", sv[:, 0:8, :], Wup.ap[f, :, 0:8, :], reads=Wup.all(), writes=sl.all())
            sc.dma("pool", sv[:, 8:16, :], Wup.ap[f, :, 8:16, :], reads=Wup.all(), writes=sl.all())
            return sl, sv

        def comp_up(f, h):
            sl, sv = h
            gp = self.F[f % 2]
            up = self.H[2 + f % 2]
            cb = self.F[4]
            sl_ = self.H[4]
            ao = self.H[f % 2]
            if f < 2:
                sc.op("pool", lambda e, gp=gp: e.memset(gp.ap[:, 0:2], 0.0), writes=gp.all())
            for tc in range(4):
                bg = self.bank()
                bu = self.bank()
                for (b, off) in ((bg, 0), (bu, 128)):
                    for kc in range(16):
                        sc.op("pe", lambda e, b=b, kc=kc, tc=tc, off=off, sv=sv: e.matmul(
                            b.ap[:, :], sv[:, kc, off:off + 128], rhs_fn(kc, tc), start=(kc == 0), stop=(kc == 15)),
                            reads=sl.all() + rhs_reads(kc), writes=b.all())
                sc.op("act", lambda e, bg=bg, tc=tc, gp=gp: e.activation(gp.ap[:, 2 + tc * 512:2 + (tc + 1) * 512], bg.ap[:, :], AF.Copy),
                      reads=bg.all(), writes=gp.all())
                sc.op("dve", lambda e, bu=bu, tc=tc, up=up: e.tensor_copy(up.ap[:, tc * 512:(tc + 1) * 512], bu.ap[:, :]),
                      reads=bu.all(), writes=up.all())
            w = lambda j, f=f: self.vec.ap[:, cwo + f * 4 + j:cwo + f * 4 + j + 1]
            sc.op("dve", lambda e, gp=gp, w=w: e.tensor_scalar(out=cb.ap[:, 0:2048], in0=gp.ap[:, 2:2050], scalar1=w(2), scalar2=w(3),
                                                            op0=ALU.mult, op1=ALU.add),
                  reads=gp.all() + self.vec.all(), writes=cb.all())
            sc.op("dve", lambda e, gp=gp, w=w: e.scalar_tensor_tensor(out=cb.ap[:, 0:2048], in0=gp.ap[:, 1:2049], scalar=w(1),
                                                                   in1=cb.ap[:, 0:2048], op0=ALU.mult, op1=ALU.add),
                  reads=gp.all() + cb.all() + self.vec.all(), writes=cb.all())
            sc.op("dve", lambda e, gp=gp, w=w: e.scalar_tensor_tensor(out=cb.ap[:, 0:2048], in0=gp.ap[:, 0:2048], scalar=w(0),
                                                                   in1=cb.ap[:, 0:2048], op0=ALU.mult, op1=ALU.add),
                  reads=gp.all() + cb.all() + self.vec.all(), writes=cb.all())
            sc.op("act", lambda e: e.activation(sl_.ap[:, :], cb.ap[:, 0:2048], AF.Silu), reads=cb.all(), writes=sl_.all())
            sc.op("dve", lambda e, up=up, ao=ao: e.tensor_tensor(out=ao.ap[:, :], in0=sl_.ap[:, :], in1=up.ap[:, :], op=ALU.mult),
                  reads=sl_.all() + up.all(), writes=ao.all())
            sc.dma("sp", actT.ap[f * 128:(f + 1) * 128, :], ao.ap[:, :], reads=ao.all(), writes=actT.k(f))

        self.pipe(list(range(NF)), load_up, comp_up)
        self.run_deferred(4)
        Av = self.A.ap[:, 0:NF * 512].rearrange("p (k t) -> p k t", k=NF)
        av = actT.ap.rearrange("(k p) t -> p k t", p=128)
        evac = self.resid_evac(hT)
        def load_dn(it):
            tc, n = it
            sl = self.next_slab()
            sv = sl.ap[:, 0:NF * 128].rearrange("p (k c) -> p k c", k=NF)
            for kh in range(2):
                sc.dma(self.q(), sv[:, kh * 22:(kh + 1) * 22, :], Wdn.ap[n, :, kh * 22:(kh + 1) * 22, :], reads=Wdn.all(), writes=sl.all())
            return sl, sv

        def comp_dn(it, h):
            tc, n = it
            sl, sv = h
            if n == 0:
                for kh in range(4):
                    slots = sorted(set((k * 512) // 2048 for k in range(kh * 11, kh * 11 + 11)))
                    sc.dma(self.q(), Av[:, kh * 11:(kh + 1) * 11, :], av[:, kh * 11:(kh + 1) * 11, tc * 512:(tc + 1) * 512],
                           reads=actT.k(*range(kh * 11, kh * 11 + 11)), writes=self.A.k(*slots))
            b = self.bank()
            for kc in range(NF):
                sc.op("pe", lambda e, b=b, kc=kc, sv=sv: e.matmul(b.ap[:, :], sv[:, kc, :], Av[:, kc, :],
                                                               start=(kc == 0), stop=(kc == NF - 1)),
                      reads=sl.all() + self.A.k(kc // 4), writes=b.all())
            evac(n * 128, tc, b)

        self.pipe([(tc, n) for tc in range(4) for n in range(16)], load_dn, comp_dn)

    def o(self, eng, meth, *a, reads=(), writes=(), **kw):
        return self.sc.op(eng, lambda e: getattr(e, meth)(*a, **kw), reads=reads, writes=writes)

    def mixer_ab(self, hT, Win, Wout, qkT, sbT, tmv, tmg, mixT, Wf=None, crow=None):
        sc = self.sc
        self.norm_to_A(hT, 0, tok0=Wf is not None)
        rhs_fn, rhs_reads = self.A_rhs()
        lhs_fn, lhs_reads = self.A_lhs()
        cvo = 432
        cnt = [0]

        def evac_qk(c0, tc, b):
            c = c0 // 128
            gp = self.F[c % 2]
            if tc == 0 and c < 2:
                self.o("pool", "memset", gp.ap[:, 0:3], 0.0, writes=gp.all())
            self.o("act", "activation", gp.ap[:, 3 + tc * 512:3 + (tc + 1) * 512], b.ap[:, :], AF.Copy, reads=b.all(), writes=gp.all())
            if tc == 3:
                cb = self.F[2 + c % 2]
                w = lambda j: self.vec.ap[:, cvo + c * 5 + j:cvo + c * 5 + j + 1]
                self.o("dve", "tensor_scalar", out=cb.ap[:, 0:2048], in0=gp.ap[:, 3:2051], scalar1=w(3), scalar2=w(4),
                       op0=ALU.mult, op1=ALU.add, reads=gp.all() + self.vec.all(), writes=cb.all())
                for j in (2, 1, 0):
                    self.o("dve", "scalar_tensor_tensor", out=cb.ap[:, 0:2048], in0=gp.ap[:, j:j + 2048], scalar=w(j),
                           in1=cb.ap[:, 0:2048], op0=ALU.mult, op1=ALU.add, reads=gp.all() + cb.all() + self.vec.all(), writes=cb.all())
                ob = self.H[c % 2]
                self.o("act", "activation", ob.ap[:, :], cb.ap[:, 0:2048], AF.Silu, reads=cb.all(), writes=ob.all())
                sc.dma("sp", qkT.ap[c0:c0 + 128, :], ob.ap[:, :], reads=ob.all(), writes=qkT.k(c))

        self.gemm_fm(Win, [[g * 512 + j * 128 for j in range(4)] for g in range(4)], 16, rhs_fn, rhs_reads, evac_qk)

        def evac_sb(c0, tc, b):
            c = (c0 - 4112) // 128
            ob = self.H[2 + c % 2]
            self.o("act", "activation", ob.ap[:, tc * 512:(tc + 1) * 512], b.ap[:, :], AF.Copy,
                   scale=(HD ** -0.5 if c < 8 else 1.0), reads=b.all(), writes=ob.all())
            if tc == 3:
                sc.dma("sp", sbT.ap[c * 128:(c + 1) * 128, :], ob.ap[:, :], reads=ob.all(), writes=sbT.k(c))

        self.gemm_fm(Win, [[4112 + g * 512 + j * 128 for j in range(4)] for g in range(4)], 16, rhs_fn, rhs_reads, evac_sb)

        jobs = []
        for (c0, dst0) in ((2048, 0), (2560, 512), (3072, 1024), (3584, 1536), (6160, 2048), (6672, 2560)):
            def evac_tm(tt, b, dst0=dst0):
                i = cnt[0]
                cnt[0] += 1
                ob = self.H[4 + i % 2]
                if i % 2:
                    self.o("act", "activation", ob.ap[:, 0:512], b.ap[:, :], AF.Copy, reads=b.all(), writes=ob.all())
                else:
                    self.o("dve", "tensor_copy", ob.ap[:, 0:512], b.ap[:, :], reads=b.all(), writes=ob.all())
                sc.dma("pool", tmv.ap[tt * 128:(tt + 1) * 128, dst0:dst0 + 512], ob.ap[:, 0:512], reads=ob.all(), writes=tmv.k(tt))
            jobs.append((c0, 512, evac_tm))

        G = self.sm
        Gv = G.ap[:, 0:256].rearrange("p (t c) -> p t c", t=16)

        def evac_g(tt, b):
            self.o("dve", "tensor_tensor", out=Gv[:, tt, :], in0=b.ap[:, 0:16], in1=self.vec.ap[:, 512:528], op=ALU.add,
                   reads=b.all() + self.vec.all(), writes=G.all())
        jobs.append((4096, 16, evac_g))
        self.gemm_tm_multi(Win, jobs, 16, lhs_fn, lhs_reads)

        Lf = G.ap[:, 256:384]
        Ii = G.ap[:, 384:512]
        Et = G.ap[:, 512:640]
        Gs = G.ap[:, 640:768]
        EL = G.ap[:, 768:896]
        Lfv = Lf.rearrange("p (t h) -> p t h", t=16)
        Iiv = Ii.rearrange("p (t h) -> p t h", t=16)
        self.o("act", "activation", Lfv, Gv[:, :, 8:16], AF.Exp, scale=-1.0, reads=G.all(), writes=G.all())
        self.o("act", "activation", Lf, Lf, AF.Ln, bias=1.0, reads=G.all(), writes=G.all())
        self.o("dve", "tensor_copy", Iiv, Gv[:, :, 0:8], reads=G.all(), writes=G.all())
        bc = self.bank()
        bl = self.bank()
        self.o("pe", "matmul", bc.ap[:, 0:128], self.triF, Lf, start=True, stop=True, reads=G.all() + self.cstf.all(), writes=bc.all())
        self.o("pe", "matmul", bl.ap[:, 0:128], self.onesF, Lf, start=True, stop=True, reads=G.all() + self.cstf.all(), writes=bl.all())
        self.o("act", "activation", Et, bc.ap[:, 0:128], AF.Exp, scale=-1.0, reads=bc.all(), writes=G.all())
        self.o("act", "activation", EL, bl.ap[:, 0:128], AF.Exp, scale=-1.0, reads=bl.all(), writes=G.all())
        self.o("dve", "tensor_tensor", out=Gs, in0=bc.ap[:, 0:128], in1=Ii, op=ALU.add, reads=bc.all() + G.all(), writes=G.all())
        self.o("act", "activation", Gs, Gs, AF.Exp, bias=float(np.log(HD ** -0.5)), reads=G.all(), writes=G.all())

        self.run_deferred(2)
        s0 = self.sm.ap[0:1, 1976:1984]
        if Wf is not None:
            r0 = self.F[0]
            r1 = self.F[1]
            self.tok0_proj(Wf, [0, 512, 1024, 1536], r0)
            sc.dma("sp", r1.ap[0:1, 0:2048], crow.ap[0:1, 0:2048], reads=crow.all(), writes=r1.all())
            self.o("dve", "tensor_tensor", out=r0.ap[0:1, 0:2048], in0=r0.ap[0:1, 0:2048], in1=r1.ap[0:1, 0:2048], op=ALU.mult,
                   reads=r0.all() + r1.all(), writes=r0.all())
            sc.dma("sp", r1.ap[0:1, 0:2048], crow.ap[0:1, 2048:4096], reads=crow.all(), writes=r1.all())
            self.o("dve", "tensor_tensor", out=r0.ap[0:1, 0:2048], in0=r0.ap[0:1, 0:2048], in1=r1.ap[0:1, 0:2048], op=ALU.add,
                   reads=r0.all() + r1.all(), writes=r0.all())
            self.o("act", "activation", r0.ap[0:1, 0:2048], r0.ap[0:1, 0:2048], AF.Silu, reads=r0.all(), writes=r0.all())
            self.o("dve", "tensor_tensor", out=r0.ap[0:1, 0:1024], in0=r0.ap[0:1, 0:1024], in1=r0.ap[0:1, 1024:2048], op=ALU.mult,
                   reads=r0.all(), writes=r0.all())
            self.o("dve", "tensor_reduce", out=s0, in_=r0.ap[0:1, 0:1024].rearrange("p (h d) -> p h d", h=8), axis=AX.X, op=ALU.add,
                   reads=r0.all(), writes=self.sm.k("s0"))

        Av = self.A.ap.rearrange("p (c t) -> p c t", c=16)
        for c in range(16):
            sc.dma(self.q(), Av[:, c, :], qkT.ap[c * 128:(c + 1) * 128, :], reads=qkT.k(c), writes=self.A.k(c))

        Cst = self.F[5]
        Cb = self.H[5]
        Csv = Cst.ap[:, 0:8 * 129].rearrange("p (h c) -> p h c", h=8)
        Cbv = Cb.ap[:, 0:8 * 129].rearrange("p (h c) -> p h c", h=8)
        ST = self.sm.ap
        sigt = self.F[4]
        live = {}

        def front(it):
            tt, h = it
            gk = self.sm.k("G")
            i = tt * 8 + h
            vt = self.H[tt % 2]
            if h == 0:
                sc.dma("sp", vt.ap[:, :], tmv.ap[tt * 128:(tt + 1) * 128, 0:2048], reads=tmv.k(tt), writes=vt.all())
                self.o("act", "activation", sigt.ap[:, (tt % 2) * 1024:(tt % 2 + 1) * 1024], vt.ap[:, 1024:2048], AF.Sigmoid,
                       reads=vt.all(), writes=sigt.k(tt % 2))
            col = tt * 8 + h
            wb = self.H[4]
            woff = (i % 2) * 1024
            wkf = wb.k("f%d" % (i % 2))
            ktm = wb.ap[:, woff:woff + 128]
            vp = wb.ap[:, woff + 128:woff + 257]
            stm = wb.ap[:, woff + 384:woff + 512]
            qT = Av[:, h, tt * 128:(tt + 1) * 128]
            kT = Av[:, 8 + h, tt * 128:(tt + 1) * 128]
            self.o("dve", "tensor_scalar", out=vp[:, 0:128], in0=vt.ap[:, h * 128:(h + 1) * 128], scalar1=Gs[:, col:col + 1],
                   scalar2=None, op0=ALU.mult, reads=vt.all() + gk, writes=wkf)
            self.o("dve", "tensor_copy", vp[:, 128:129], Gs[:, col:col + 1], reads=gk, writes=wkf)
            b1 = self.bank()
            self.o("pe", "transpose", bc16(b1.ap)[:, 0:128], kT, self.identb, reads=self.A.k(8 + h) + self.cst.all(), writes=b1.all())
            self.o("act", "activation", ktm, bc16(b1.ap)[:, 0:128], AF.Copy, reads=b1.all(), writes=wkf)
            b2 = self.bank()
            self.o("pe", "matmul", b2.ap[:, 0:128], kT, qT, start=True, stop=True, reads=self.A.k(h, 8 + h), writes=b2.all())
            self.o("dve", "tensor_tensor", out=stm, in0=b2.ap[:, 0:128], in1=self.maskUTb, op=ALU.mult,
                   reads=b2.all() + self.cst.all(), writes=wkf)
            if tt == 0 and Wf is not None:
                self.o("dve", "tensor_copy", stm[0:1, 0:1], s0[:, h:h + 1], reads=self.sm.k("s0"), writes=wkf)
            b3 = self.bank()
            self.o("pe", "matmul", b3.ap[:, 0:129], stm, vp, start=True, stop=(tt == 0), reads=wkf, writes=b3.all())
            if tt > 0:
                self.o("pe", "matmul", b3.ap[:, 0:129], qT, Cbv[:, h, :], start=False, stop=True,
                       reads=self.A.k(h) + Cb.k(h), writes=b3.all())
            if tt < NT - 1:
                b4 = self.bank()
                self.o("pe", "matmul", b4.ap[:, 0:129], ktm, vp, start=True, stop=True, reads=wkf, writes=b4.all())
                if tt == 0:
                    self.o("dve", "tensor_scalar", out=Csv[:, h, :], in0=b4.ap[:, 0:129], scalar1=EL[:, col:col + 1], scalar2=None,
                           op0=ALU.mult, reads=b4.all() + gk, writes=Cst.k(h))
                else:
                    self.o("dve", "tensor_scalar", out=Csv[:, h, :], in0=Csv[:, h, :], scalar1=EL[:, col:col + 1], scalar2=None,
                           op0=ALU.mult, reads=gk + Cst.k(h), writes=Cst.k(h))
                    self.o("dve", "scalar_tensor_tensor", out=Csv[:, h, :], in0=b4.ap[:, 0:129], scalar=EL[:, col:col + 1],
                           in1=Csv[:, h, :], op0=ALU.mult, op1=ALU.add, reads=b4.all() + gk + Cst.k(h), writes=Cst.k(h))
                self.o("dve", "tensor_copy", Cbv[:, h, :], Csv[:, h, :], reads=Cst.k(h), writes=Cb.k(h))
            X = self.F[tt % 2]
            self.o("act", "activation", X.ap[:, h * 129:(h + 1) * 129], b3.ap[:, 0:129], AF.Copy, reads=b3.all(), writes=X.k("x%d" % h))

        def backT(tt):
            gk = self.sm.k("G")
            X = self.F[tt % 2]
            xk = X.k(*["x%d" % h for h in range(8)])
            Xv = X.ap[:, 0:8 * 129].rearrange("p (h c) -> p h c", h=8)
            HM = self.F[2]
            SQ = self.F[3]
            HMv = HM.ap[:, 0:1024].rearrange("p (h d) -> p h d", h=8)
            SQv = SQ.ap[:, 0:1024].rearrange("p (h d) -> p h d", h=8)
            og = self.H[2 + tt % 2]
            stk = self.sm.k("stT")
            d1 = ST[:, 1024:1032]
            r2 = ST[:, 1032:1040]
            ssq = ST[:, 1040:1048]
            rstd = ST[:, 1048:1056]
            Ett = Et[:, tt * 8:(tt + 1) * 8]
            bc8 = lambda a: a.rearrange("p (h o) -> p h o", o=1).broadcast_to([128, 8, 128])
            self.o("dve", "tensor_tensor", out=d1, in0=Xv[:, :, 128], in1=Ett, op=ALU.mult, reads=xk + gk, writes=stk)
            self.o("act", "activation", d1, d1, AF.Abs, reads=stk, writes=stk)
            self.o("dve", "tensor_scalar", out=d1, in0=d1, scalar1=1.0, scalar2=None, op0=ALU.max, reads=stk, writes=stk)
            self.o("dve", "reciprocal", d1, d1, reads=stk, writes=stk)
            self.o("dve", "tensor_tensor", out=r2, in0=d1, in1=Ett, op=ALU.mult, reads=stk + gk, writes=stk)
            self.o("dve", "tensor_tensor", out=HMv, in0=Xv[:, :, 0:128], in1=bc8(r2), op=ALU.mult, reads=xk + stk, writes=HM.all())
            self.o("dve", "tensor_tensor", out=SQv, in0=HMv, in1=HMv, op=ALU.mult, reads=HM.all(), writes=SQ.all())
            self.o("dve", "tensor_reduce", out=ssq, in_=SQv, axis=AX.X, op=ALU.add, reads=SQ.all(), writes=stk)
            self.o("act", "activation", rstd, ssq, AF.Sqrt, scale=1.0 / HD, bias=EPS, reads=stk, writes=stk)
            self.o("dve", "reciprocal", rstd, rstd, reads=stk, writes=stk)
            self.o("dve", "tensor_tensor", out=HMv, in0=HMv, in1=bc8(rstd), op=ALU.mult, reads=HM.all() + stk, writes=HM.all())
            self.o("dve", "tensor_tensor", out=HM.ap[:, 0:1024], in0=HM.ap[:, 0:1024], in1=self.hn.ap[:, 0:1024], op=ALU.mult,
                   reads=HM.all() + self.hn.all(), writes=HM.all())
            o2 = bc16(SQ.ap)[:, 0:1024]
            self.o("dve", "tensor_tensor", out=o2, in0=HM.ap[:, 0:1024], in1=sigt.ap[:, (tt % 2) * 1024:(tt % 2 + 1) * 1024], op=ALU.mult,
                   reads=HM.all() + sigt.k(tt % 2), writes=SQ.all())
            b5 = self.bank()
            for h in range(8):
                self.o("pe", "transpose", bc16(b5.ap)[:, h * 128:(h + 1) * 128], o2[:, h * 128:(h + 1) * 128], self.identb,
                       reads=SQ.all() + self.cst.all(), writes=b5.all())
            self.o("act", "activation", og.ap[:, 0:1024], bc16(b5.ap)[:, 0:1024], AF.Copy, reads=b5.all(), writes=og.all())
            sc.dma("pool", mixT.ap[0:1024, tt * 128:(tt + 1) * 128].rearrange("(h p) t -> p h t", p=128),
                   og.ap[:, 0:1024].rearrange("p (h t) -> p h t", h=8), reads=og.all(), writes=mixT.k(*range(8)))

        for tt in range(NT):
            for h in range(8):
                front((tt, h))
            if tt > 0:
                backT(tt - 1)
        backT(NT - 1)

        self.stick_breaking(sbT, tmv, mixT)

        for c in range(16):
            sc.dma(self.q(), Av[:, c, :], mixT.ap[c * 128:(c + 1) * 128, :], reads=mixT.k(c), writes=self.A.k(c))
        self.gemm_fm(Wout, [[g * 512 + j * 128 for j in range(4)] for g in range(4)], 16, rhs_fn, rhs_reads, self.resid_evac(hT))

    def stick_breaking(self, sbT, tmv, mixT):
        sc = self.sc
        qT = self.H[0]
        kT = self.H[1]
        og = self.F[4]
        ogb = bc16(og.ap)[:, 0:2048]
        aT = self.F[5]
        aTb = bc16(aT.ap)[:, 0:2048]
        negm = self.cst.ap[:, 768:896]

        def vvh(h):
            return self.H[2] if h % 2 == 0 else self.H[5]

        def A1(it):
            h, tt = it
            if tt == 0:
                vv = vvh(h)
                sc.dma("sp", qT.ap[:, :], sbT.ap[h * 128:(h + 1) * 128, :], reads=sbT.k(h), writes=qT.all())
                sc.dma("sp", kT.ap[:, :], sbT.ap[(8 + h) * 128:(9 + h) * 128, :], reads=sbT.k(8 + h), writes=kT.all())
                sc.dma("pool", vv.ap.rearrange("p (t d) -> p t d", t=16),
                       tmv.ap[:, 2048 + h * 128:2048 + (h + 1) * 128].rearrange("(t p) d -> p t d", p=128),
                       reads=tmv.k(*range(16)), writes=vv.all())
            W = (tt + 1) * 128
            nb = (W + 511) // 512
            sp = self.F[tt % 2]
            w2 = self.F[2 + tt % 2]
            for j in range(nb):
                wj = min(512, W - j * 512)
                zb = self.bank()
                last = (j == nb - 1)
                self.o("pe", "matmul", zb.ap[:, 0:wj], qT.ap[:, tt * 128:(tt + 1) * 128], kT.ap[:, j * 512:j * 512 + wj],
                       start=True, stop=not last, reads=qT.all() + kT.all(), writes=zb.all())
                if last:
                    self.o("pe", "matmul", zb.ap[:, wj - 128:wj], self.identb, negm, start=False, stop=True,
                           reads=self.cst.all(), writes=zb.all())
                self.o("act", "activation", sp.ap[:, j * 512:j * 512 + wj], zb.ap[:, 0:wj], AF.Exp, reads=zb.all(), writes=sp.k(j))
                self.o("act", "activation", sp.ap[:, j * 512:j * 512 + wj], sp.ap[:, j * 512:j * 512 + wj], AF.Ln, bias=1.0,
                       reads=sp.k(j), writes=sp.k(j))
                self.o("dve", "tensor_tensor", out=w2.ap[:, j * 512:j * 512 + wj], in0=zb.ap[:, 0:wj], in1=sp.ap[:, j * 512:j * 512 + wj],
                       op=ALU.subtract, reads=zb.all() + sp.k(j), writes=w2.k(j))

        def A2(it):
            h, tt = it
            W = (tt + 1) * 128
            sp = self.F[tt % 2]
            w2 = self.F[2 + tt % 2]
            ab = self.H[3 + tt % 2]
            nb = (W + 511) // 512
            spk = sp.k(*range(nb))
            w2k = w2.k(*range(nb))
            self.o("dve", "tensor_tensor_scan", out=sp.ap[:, 0:W], data0=self.ones.ap[:, 0:W], data1=sp.ap[:, 0:W], initial=0.0,
                   op0=ALU.mult, op1=ALU.add, reads=spk + self.ones.all(), writes=spk)
            self.o("dve", "tensor_tensor", out=w2.ap[:, 0:W], in0=w2.ap[:, 0:W], in1=sp.ap[:, 0:W], op=ALU.add,
                   reads=spk + w2k, writes=w2k)
            nt_ = self.sm.ap[:, 1992 + tt % 2:1993 + tt % 2]
            self.o("dve", "tensor_scalar", out=nt_, in0=sp.ap[:, W - 1:W], scalar1=-1.0, scalar2=None, op0=ALU.mult,
                   reads=spk, writes=self.sm.k("nt%d" % (tt % 2)))

        def A3(it):
            h, tt = it
            W = (tt + 1) * 128
            nb = (W + 511) // 512
            w2 = self.F[2 + tt % 2]
            ab = self.H[3 + tt % 2]
            nt_ = self.sm.ap[:, 1992 + tt % 2:1993 + tt % 2]
            self.o("act", "activation", ab.ap[:, 0:W], w2.ap[:, 0:W], AF.Exp, bias=nt_,
                   reads=w2.k(*range(nb)) + self.sm.k("nt%d" % (tt % 2)), writes=ab.all())

        def B(it):
            h, tt = it
            vv = vvh(h)
            vvv = vv.ap.rearrange("p (t d) -> p t d", t=16)
            ab = self.H[3 + tt % 2]
            for g in range((tt + 8) // 8):
                n = min(8, tt + 1 - g * 8)
                b = self.bank()
                for j in range(n):
                    kt = g * 8 + j
                    self.o("pe", "transpose", bc16(b.ap)[:, j * 128:(j + 1) * 128], ab.ap[:, kt * 128:(kt + 1) * 128], self.identb,
                           reads=ab.all() + self.cst.all(), writes=b.all())
                if g % 2:
                    self.o("act", "activation", aTb[:, g * 1024:g * 1024 + n * 128], bc16(b.ap)[:, 0:n * 128], AF.Copy,
                           reads=b.all(), writes=aT.all())
                else:
                    self.o("dve", "tensor_copy", aTb[:, g * 1024:g * 1024 + n * 128], bc16(b.ap)[:, 0:n * 128],
                           reads=b.all(), writes=aT.all())
            bo = self.bank()
            for kt in range(tt + 1):
                self.o("pe", "matmul", bo.ap[:, 0:128], vvv[:, kt, :], aTb[:, kt * 128:(kt + 1) * 128], start=(kt == 0), stop=(kt == tt),
                       reads=vv.all() + aT.all(), writes=bo.all())
            self.o("act", "activation", ogb[:, tt * 128:(tt + 1) * 128], bo.ap[:, 0:128], AF.Copy, reads=bo.all(), writes=og.all())
            if tt == NT - 1:
                sc.dma("sp", mixT.ap[(8 + h) * 128:(9 + h) * 128, :], ogb, reads=og.all(), writes=mixT.k(8 + h))

        its = [(h, tt) for h in range(8) for tt in range(NT)]
        n = len(its)
        for k in range(-2, n + 1):
            if 0 <= k < n:
                A3(its[k])
            if 0 <= k + 1 < n:
                A2(its[k + 1])
            if 0 <= k + 2 < n:
                A1(its[k + 2])
            if 0 <= k - 1 < n:
                B(its[k - 1])

    def setup_l1(self, ext):
        sc = self.sc
        self.c1 = self.sb([128, 1536], F32, "c1")
        self.Ef = self.sb([32, 2048], BF16, "Ef")
        sc.dma("sp", self.c1.ap[:, :], ext["c1"].ap[:, :], reads=ext["c1"].all(), writes=self.c1.all())
        sc.dma("sp", self.Ef.ap[:, :], ext["Ef"].ap[:, :], reads=ext["Ef"].all(), writes=self.Ef.all())
        self.maskBD = self.cst.ap[:, 512:640]
        self.maskFar = self.cst.ap[:, 640:768]

    def mixer_cd(self, hT, Win, Wout, W1b, W2b, fmT, tmv, mixT, ext, Wf=None, grow=None):
        sc = self.sc
        SC = HD ** -0.5
        self.norm_to_A(hT, 48, tok0=Wf is not None)
        rhs_fn, rhs_reads = self.A_rhs()
        lhs_fn, lhs_reads = self.A_lhs()
        cnt = [0]
        c1 = self.c1
        ropeQ = c1.ap[:, 0:512].rearrange("p (t c) -> p t c", t=16)
        G1 = self.sm.ap[:, 0:384].rearrange("p (t c) -> p t c", t=16)
        PS = self.sm.ap[:, 384:896].rearrange("p (t c) -> p t c", t=16)
        SEL = self.sm.ap[:, 896:1408].rearrange("p (t c) -> p t c", t=16)
        sm = self.sm

        def evac_fm(c0, tc, b):
            if c0 < 2584:
                c = (c0 - 1024) // 128
            elif c0 < 5656:
                c = 4 + (c0 - 2584) // 128
            else:
                c = 20 + (c0 - 5656) // 128
            ob = self.H[c % 2]
            if 12 <= c < 20:
                self.o("act", "activation", ob.ap[:, tc * 512:(tc + 1) * 512], b.ap[:, :], AF.Sigmoid, reads=b.all(), writes=ob.all())
            elif c >= 20:
                self.o("act", "activation", ob.ap[:, tc * 512:(tc + 1) * 512], b.ap[:, :], AF.Silu, reads=b.all(), writes=ob.all())
            else:
                self.o("dve", "tensor_copy", ob.ap[:, tc * 512:(tc + 1) * 512], b.ap[:, :], reads=b.all(), writes=ob.all())
            if tc == 3:
                sc.dma("sp", fmT.ap[c * 128:(c + 1) * 128, :], ob.ap[:, :], reads=ob.all(), writes=fmT.k(c))

        cols = [1024 + j * 128 for j in range(4)] + [2584 + j * 128 for j in range(16)] + [5656 + j * 128 for j in range(8)]
        self.gemm_fm(Win, [cols[i:i + 4] for i in range(0, 28, 4)], 16, rhs_fn, rhs_reads, evac_fm)

        def mk_evac(dst0, nrope):
            def evac(tt, b):
                i = cnt[0]
                cnt[0] += 1
                ob = self.H[4 + i % 2]
                if nrope == 0:
                    self.o("act", "activation", ob.ap[:, 0:512], b.ap[:, :], AF.Copy, reads=b.all(), writes=ob.all())
                else:
                    tk = self.F[i % 2].all()
                    bv = b.ap[:, 0:nrope * 128].rearrange("p (h d) -> p h d", h=nrope)
                    ov = ob.ap[:, 0:nrope * 128].rearrange("p (h d) -> p h d", h=nrope)
                    tmp = self.F[i % 2].ap[:, 0:256].rearrange("p (a h c) -> p a h c", a=4, h=4)
                    cos = ropeQ[:, tt, 0:16].rearrange("p (o c) -> p o c", o=1).broadcast_to([128, nrope, 16])
                    sin = ropeQ[:, tt, 16:32].rearrange("p (o c) -> p o c", o=1).broadcast_to([128, nrope, 16])
                    self.o("act", "activation", ob.ap[:, 0:512], b.ap[:, :], AF.Copy, reads=b.all(), writes=ob.all())
                    self.o("dve", "tensor_tensor", out=tmp[:, 0, 0:nrope, :], in0=bv[:, :, 0:16], in1=cos, op=ALU.mult, reads=b.all() + c1.all(), writes=tk)
                    self.o("dve", "tensor_tensor", out=tmp[:, 1, 0:nrope, :], in0=bv[:, :, 16:32], in1=sin, op=ALU.mult, reads=b.all() + c1.all(), writes=tk)
                    self.o("dve", "tensor_tensor", out=tmp[:, 2, 0:nrope, :], in0=bv[:, :, 16:32], in1=cos, op=ALU.mult, reads=b.all() + c1.all(), writes=tk)
                    self.o("dve", "tensor_tensor", out=tmp[:, 3, 0:nrope, :], in0=bv[:, :, 0:16], in1=sin, op=ALU.mult, reads=b.all() + c1.all(), writes=tk)
                    self.o("dve", "tensor_tensor", out=ov[:, :, 0:16], in0=tmp[:, 0, 0:nrope, :], in1=tmp[:, 1, 0:nrope, :], op=ALU.subtract,
                           reads=tk + ob.all(), writes=ob.all())
                    self.o("dve", "tensor_tensor", out=ov[:, :, 16:32], in0=tmp[:, 2, 0:nrope, :], in1=tmp[:, 3, 0:nrope, :], op=ALU.add,
                           reads=tk + ob.all(), writes=ob.all())
                sc.dma("pool", tmv.ap[tt * 128:(tt + 1) * 128, dst0:dst0 + 512], ob.ap[:, 0:512], reads=ob.all(), writes=tmv.k(tt))
            return evac

        jobs = [(c0, 512, mk_evac(dst0, nr)) for (c0, dst0, nr) in
                ((0, 0, 4), (512, 512, 4), (1536, 1024, 2), (2048, 1536, 2), (4632, 2048, 0), (5144, 2560, 0))]

        def evac_g(tt, b):
            self.o("act", "activation", G1[:, tt, :], b.ap[:, 0:24], AF.Sigmoid, reads=b.all(), writes=sm.all())
        jobs.append((2560, 24, evac_g))
        self.gemm_tm_multi(Win, jobs, 16, lhs_fn, lhs_reads)
        Av = self.A.ap.rearrange("p (c t) -> p c t", c=16)
        for tt in range(NT):
            xt = self.H[tt % 2]
            sc.dma("sp", xt.ap[:, 0:2048], tmv.ap[tt * 128:(tt + 1) * 128, 0:2048], reads=tmv.k(tt), writes=xt.all())
            srcs = [(h, h * 128) for h in range(8)] + [(8, 1024), (9, 1152), (10, 1536), (11, 1664)]
            for gi in range(0, 12, 4):
                b = self.bank()
                for j in range(4):
                    slot, off = srcs[gi + j]
                    self.o("pe", "transpose", bc16(b.ap)[:, j * 128:(j + 1) * 128], xt.ap[:, off:off + 128], self.identb,
                           reads=xt.all() + self.cst.all(), writes=b.all())
                s0_ = srcs[gi][0]
                dst = Av[:, s0_:s0_ + 4, tt * 128:(tt + 1) * 128]
                srcv = bc16(b.ap)[:, 0:512].rearrange("p (j t) -> p j t", j=4)
                if (gi // 4 + tt) % 2:
                    self.o("act", "activation", dst, srcv, AF.Copy, reads=b.all(), writes=self.A.k(*range(s0_, s0_ + 4)))
                else:
                    self.o("dve", "tensor_copy", dst, srcv, reads=b.all(), writes=self.A.k(*range(s0_, s0_ + 4)))

        self.run_deferred(2)
        visT = bc16(self.hn.ap)[:, 0:2048]
        sc.dma("sp", self.hn.ap[:, :], ext["vis"].ap[:, :], reads=ext["vis"].all(), writes=self.hn.all())
        for g in range(2):
            kcT = self.H[4].ap[:, 0:128]
            vcA = self.H[4].ap[:, 256:256 + 161]
            sc.dma("sp", vcA[0:127, 129:161], ext["cover"].ap[:, :], reads=ext["cover"].all(), writes=self.H[4].all())
            for kv in range(2):
                xs = self.H[0]
                sc.dma("sp", xs.ap[:, :], fmT.ap[(kv * 2 + g) * 128:(kv * 2 + g + 1) * 128, :], reads=fmT.k(kv * 2 + g), writes=xs.all())
                w1 = self.next_slab()
                w1v = w1.ap[:, 0:4096].rearrange("p (l e) -> p l e", l=32)
                for lq in range(4):
                    sc.dma(self.q(), w1v[:, lq * 8:(lq + 1) * 8, :],
                           W1b.ap[kv * 4096 + lq * 1024:kv * 4096 + (lq + 1) * 1024, :].rearrange("(l d) e -> d l e", d=128),
                           reads=W1b.all(), writes=w1.all())
                w2 = w1.ap[:, 4096:4224]
                sc.dma("sp", w2, W2b.ap[kv * 128:(kv + 1) * 128, :], reads=W2b.all(), writes=w1.all())
                posb = w1.ap[:, 4224:4256]
                self.o("dve", "tensor_copy", posb, self.vec.ap[:, 544 + kv * 32:544 + (kv + 1) * 32], reads=self.vec.all(), writes=w1.all())
                bh = self.bank()
                bcn = self.bank()
                xv = xs.ap[:, 0:2048].rearrange("p (n s) -> p n s", s=16)
                for l in range(32):
                    rhs = xv[:, l // 16:l // 16 + 127, l % 16]
                    self.o("pe", "matmul", bh.ap[:, 0:127], w1v[:, l, :], rhs, start=(l == 0), stop=(l == 31), reads=w1.all() + xs.all(), writes=bh.all())
                for l in range(32):
                    self.o("pe", "matmul", bcn.ap[:, 0:1], w1v[:, l, :], posb[:, l:l + 1], start=(l == 0), stop=(l == 31), reads=w1.all(), writes=bcn.all())
                wk = self.F[kv]
                cc = wk.ap[:, 1024:1025]
                xx = wk.ap[:, 0:127]
                x2 = wk.ap[:, 128:255]
                hid = self.H[1].ap[:, kv * 128:kv * 128 + 127]
                self.o("dve", "tensor_copy", cc, bcn.ap[:, 0:1], reads=bcn.all(), writes=wk.all())
                self.o("act", "activation", xx, bh.ap[:, 0:127], AF.Identity, bias=cc, reads=bh.all() + wk.all(), writes=wk.all())
                self.o("dve", "tensor_tensor", out=x2, in0=xx, in1=xx, op=ALU.mult, reads=wk.all(), writes=wk.all())
                self.o("dve", "tensor_scalar", out=x2, in0=x2, scalar1=0.044715, scalar2=1.0, op0=ALU.mult, op1=ALU.add, reads=wk.all(), writes=wk.all())
                self.o("dve", "tensor_tensor", out=x2, in0=x2, in1=xx, op=ALU.mult, reads=wk.all(), writes=wk.all())
                self.o("act", "activation", x2, x2, AF.Sigmoid, scale=1.5957691216, reads=wk.all(), writes=wk.all())
                self.o("dve", "tensor_tensor", out=hid, in0=x2, in1=xx, op=ALU.mult, reads=wk.all(), writes=self.H[1].all())
                bo = self.bank()
                self.o("pe", "matmul", bo.ap[0:127, 0:128], hid, w2, start=True, stop=True, reads=self.H[1].all() + w1.all(), writes=bo.all())
                if kv == 1:
                    self.o("act", "activation", vcA[0:127, 0:128], bo.ap[0:127, 0:128], AF.Copy, reads=bo.all(), writes=self.H[4].all())
                    self.o("pool", "memset", vcA[0:127, 128:129], 1.0, writes=self.H[4].all())
                else:
                    kk = wk.ap[:, 256:384]
                    tmp = wk.ap[:, 384:448]
                    cos = c1.ap[0:127, 512:528]
                    sin = c1.ap[0:127, 528:544]
                    self.o("act", "activation", kk[0:127, :], bo.ap[0:127, 0:128], AF.Copy, reads=bo.all(), writes=wk.all())
                    self.o("dve", "tensor_tensor", out=tmp[0:127, 0:16], in0=bo.ap[0:127, 0:16], in1=cos, op=ALU.mult, reads=bo.all() + c1.all(), writes=wk.all())
                    self.o("dve", "tensor_tensor", out=tmp[0:127, 16:32], in0=bo.ap[0:127, 16:32], in1=sin, op=ALU.mult, reads=bo.all() + c1.all(), writes=wk.all())
                    self.o("dve", "tensor_tensor", out=tmp[0:127, 32:48], in0=bo.ap[0:127, 16:32], in1=cos, op=ALU.mult, reads=bo.all() + c1.all(), writes=wk.all())
                    self.o("dve", "tensor_tensor", out=tmp[0:127, 48:64], in0=bo.ap[0:127, 0:16], in1=sin, op=ALU.mult, reads=bo.all() + c1.all(), writes=wk.all())
                    self.o("dve", "tensor_tensor", out=kk[0:127, 0:16], in0=tmp[0:127, 0:16], in1=tmp[0:127, 16:32], op=ALU.subtract, reads=wk.all(), writes=wk.all())
                    self.o("dve", "tensor_tensor", out=kk[0:127, 16:32], in0=tmp[0:127, 32:48], in1=tmp[0:127, 48:64], op=ALU.add, reads=wk.all(), writes=wk.all())
                    kkb = self.H[1].ap[:, 512:640]
                    self.o("pool", "memset", kkb, 0.0, writes=self.H[1].all())
                    self.o("dve", "tensor_copy", kkb[0:127, :], kk[0:127, :], reads=wk.all(), writes=self.H[1].all())
                    bt = self.bank()
                    self.o("pe", "transpose", bc16(bt.ap)[:, 0:128], kkb, self.identb, reads=self.H[1].all() + self.cst.all(), writes=bt.all())
                    self.o("act", "activation", kcT, bc16(bt.ap)[:, 0:128], AF.Copy, reads=bt.all(), writes=self.H[4].all())

            vsA = bc16(self.F[4].ap)[:, 0:2064].rearrange("p (t c) -> p t c", t=16)
            vwA = bc16(self.F[5].ap)[:, 0:2064].rearrange("p (t c) -> p t c", t=16)
            for (va, Ft, off) in ((vsA, self.F[4], 1280), (vwA, self.F[5], 1792)):
                self.o("pool", "memset", va[:, :, 128:129], 1.0, writes=Ft.all())
                sc.dma("pool", va[:, :, 0:128], tmv.ap[:, off + g * 128:off + (g + 1) * 128].rearrange("(t p) d -> p t d", p=128),
                       reads=tmv.k(*range(16)), writes=Ft.all())

            cmpo = self.slab[0]
            cmv = cmpo.ap.rearrange("p (h t d) -> p h t d", h=4, t=16)
            for r in range(4):
                h = g * 4 + r
                for tc in range(4):
                    bs = self.bank()
                    self.o("pe", "matmul", bs.ap[0:127, :], kcT[:, 0:127], Av[:, h, tc * 512:(tc + 1) * 512], start=True, stop=True,
                           reads=self.H[4].all() + self.A.k(h), writes=bs.all())
                    ee = self.F[tc % 2]
                    em = self.H[2 + tc % 2]
                    self.o("act", "activation", ee.ap[0:127, 0:512], bs.ap[0:127, :], AF.Exp, scale=SC, reads=bs.all(), writes=ee.all())
                    self.o("dve", "tensor_tensor", out=em.ap[0:127, 0:512], in0=ee.ap[0:127, 0:512], in1=visT[0:127, tc * 512:(tc + 1) * 512],
                           op=ALU.mult, reads=ee.all() + self.hn.all(), writes=em.all())
                    for j in range(4):
                        tt = tc * 4 + j
                        bo = self.bank()
                        self.o("pe", "matmul", bo.ap[:, 0:161], em.ap[0:127, j * 128:(j + 1) * 128], vcA[0:127, :], start=True, stop=True,
                               reads=em.all() + self.H[4].all(), writes=bo.all())
                        i = cnt[0]
                        cnt[0] += 1
                        so = 1408 + (i % 4) * 4
                        stt = sm.k("c%d" % (i % 4))
                        rd = sm.ap[:, so:so + 1]
                        rg = sm.ap[:, so + 1:so + 2]
                        self.o("dve", "tensor_scalar", out=rd, in0=bo.ap[:, 128:129], scalar1=1e-30, scalar2=None, op0=ALU.max, reads=bo.all(), writes=stt)
                        self.o("dve", "reciprocal", rd, rd, reads=stt, writes=stt)
                        self.o("dve", "tensor_tensor", out=rg, in0=rd, in1=G1[:, tt, h * 3:h * 3 + 1], op=ALU.mult, reads=stt + sm.all(), writes=stt)
                        self.o("act", "activation", cmv[:, r, tt, :], bo.ap[:, 0:128], AF.Copy, scale=rg, reads=bo.all() + stt, writes=cmpo.all())
                        if r == 0:
                            self.o("dve", "tensor_scalar", out=PS[:, tt, :], in0=bo.ap[:, 129:161], scalar1=rd, scalar2=None, op0=ALU.mult,
                                   reads=bo.all() + stt, writes=sm.k("ps"))
                        else:
                            self.o("dve", "scalar_tensor_tensor", out=PS[:, tt, :], in0=bo.ap[:, 129:161], scalar=rd, in1=PS[:, tt, :],
                                   op0=ALU.mult, op1=ALU.add, reads=bo.all() + stt + sm.k("ps"), writes=sm.k("ps"))
            sbias = c1.ap[:, 544:1056].rearrange("p (t c) -> p t c", t=16)
            selT = self.H[5]
            top8 = sm.ap[:, 1440:1448]
            for tt in range(NT):
                self.o("dve", "tensor_tensor", out=PS[:, tt, :], in0=PS[:, tt, :], in1=sbias[:, tt, :], op=ALU.add,
                       reads=sm.k("ps") + c1.all(), writes=sm.k("ps"))
                self.o("dve", "max", top8, PS[:, tt, :], reads=sm.k("ps"), writes=sm.k("t8"))
                self.o("dve", "tensor_scalar", out=SEL[:, tt, :], in0=PS[:, tt, :], scalar1=sm.ap[:, 1447:1448], scalar2=None, op0=ALU.is_ge,
                       reads=sm.k("ps", "t8"), writes=sm.k("sel"))
                selb = self.H[4].ap[:, 512:544]
                self.o("dve", "tensor_scalar", out=selb, in0=SEL[:, tt, :], scalar1=30000.0, scalar2=-30000.0, op0=ALU.mult, op1=ALU.add,
                       reads=sm.k("sel"), writes=self.H[4].k("selb"))
                bt = self.bank()
                self.o("pe", "transpose", bc16(bt.ap)[0:32, 0:128], selb, self.identb, reads=self.H[4].k("selb") + self.cst.all(), writes=bt.all())
                self.o("act", "activation", selT.ap[0:32, tt * 128:(tt + 1) * 128], bc16(bt.ap)[0:32, 0:128], AF.Copy, reads=bt.all(), writes=selT.all())

            negDiag = self.cst.ap[:, 384:512]
            negFar = self.cst.ap[:, 768:896]
            st8 = {}

            def S_(it):
                tt, r = it
                nk = tt + 1
                h = g * 4 + r
                i = tt * 4 + r
                qT = Av[:, h, tt * 128:(tt + 1) * 128]
                kts = [kt for kt in (tt - 2, tt - 1, tt) if kt >= 0]
                bw = self.bank()
                for j, kt in enumerate(kts):
                    diag = (kt == tt)
                    far = (kt == tt - 2)
                    self.o("pe", "matmul", bw.ap[:, j * 128:(j + 1) * 128], Av[:, 10 + g, kt * 128:(kt + 1) * 128], qT, start=True,
                           stop=not (diag or far), reads=self.A.k(10 + g, h), writes=bw.all())
                    if diag or far:
                        self.o("pe", "matmul", bw.ap[:, j * 128:(j + 1) * 128], self.identb, negDiag if diag else negFar, start=False, stop=True,
                               reads=self.cst.all(), writes=bw.all())
                ew = self.H[4]
                ewk = ew.k("ew%d" % (i % 2))
                ewo = 1024 + (i % 2) * 384
                nw = len(kts) * 128
                esT = self.H[i % 2]
                sb_ = []
                for gb in range((nk + 3) // 4):
                    n = min(4, nk - gb * 4)
                    bsx = self.bank()
                    sb_.append((gb, n, bsx))
                    for j in range(n):
                        kt = gb * 4 + j
                        self.o("pe", "matmul", bsx.ap[:, j * 128:(j + 1) * 128], Av[:, 8 + g, kt * 128:(kt + 1) * 128], qT, start=True, stop=False,
                               reads=self.A.k(8 + g, h), writes=bsx.all())
                        self.o("pe", "matmul", bsx.ap[:, j * 128:(j + 1) * 128], self.Ef.ap[0:32, kt * 128:(kt + 1) * 128],
                               selT.ap[0:32, tt * 128:(tt + 1) * 128], start=False, stop=(kt != tt), reads=self.Ef.all() + selT.all(), writes=bsx.all())
                        if kt == tt:
                            self.o("pe", "matmul", bsx.ap[:, j * 128:(j + 1) * 128], self.identb, negDiag, start=False, stop=True,
                                   reads=self.cst.all(), writes=bsx.all())
                self.o("act", "activation", ew.ap[:, ewo:ewo + nw], bw.ap[:, 0:nw], AF.Exp, scale=SC, reads=bw.all(), writes=ewk)
                for (gb, n, bsx) in sb_:
                    self.o("act", "activation", esT.ap[:, gb * 512:gb * 512 + n * 128], bsx.ap[:, 0:n * 128], AF.Exp, scale=SC,
                           reads=bsx.all(), writes=esT.k(gb))

            def P_(it):
                tt, r = it
                nk = tt + 1
                i = tt * 4 + r
                kts = [kt for kt in (tt - 2, tt - 1, tt) if kt >= 0]
                ew = self.H[4]
                ewk = ew.k("ew%d" % (i % 2))
                ewo = 1024 + (i % 2) * 384
                esT = self.H[i % 2]
                bout = self.bank()
                for j, kt in enumerate(kts):
                    self.o("pe", "matmul", bout.ap[:, 0:129], ew.ap[:, ewo + j * 128:ewo + (j + 1) * 128], vwA[:, kt, :],
                           start=(j == 0), stop=(j == len(kts) - 1), reads=ewk + self.F[5].all(), writes=bout.all())
                for kt in range(nk):
                    self.o("pe", "matmul", bout.ap[:, 160:289], esT.ap[:, kt * 128:(kt + 1) * 128], vsA[:, kt, :], start=(kt == 0), stop=(kt == nk - 1),
                           reads=esT.k(kt // 4) + self.F[4].all(), writes=bout.all())
                st8[it] = bout

            def C_(it):
                tt, r = it
                h = g * 4 + r
                i = tt * 4 + r
                bout = st8.pop(it)
                og = self.H[2 + tt % 2]
                ogv = og.ap[:, 0:512].rearrange("p (h t) -> p h t", h=4)
                so = 1456 + (i % 4) * 4
                stt = sm.k("d%d" % (i % 4))
                rs = sm.ap[:, so:so + 1]
                rw = sm.ap[:, so + 1:so + 2]
                self.o("dve", "reciprocal", rs, bout.ap[:, 288:289], reads=bout.all(), writes=stt)
                self.o("dve", "tensor_tensor", out=rs, in0=rs, in1=G1[:, tt, h * 3 + 1:h * 3 + 2], op=ALU.mult, reads=stt + sm.all(), writes=stt)
                self.o("dve", "reciprocal", rw, bout.ap[:, 128:129], reads=bout.all(), writes=stt)
                self.o("dve", "tensor_tensor", out=rw, in0=rw, in1=G1[:, tt, h * 3 + 2:h * 3 + 3], op=ALU.mult, reads=stt + sm.all(), writes=stt)
                acck = self.H[4].k("acc%d" % (i % 2))
                acc = self.H[4].ap[:, 1792 + (i % 2) * 128:1792 + (i % 2) * 128 + 128]
                accf = self.sm.ap[:, 1536 + (i % 2) * 128:1536 + (i % 2) * 128 + 128]
                self.o("dve", "scalar_tensor_tensor", out=accf, in0=bout.ap[:, 160:288], scalar=rs, in1=cmv[:, r, tt, :], op0=ALU.mult, op1=ALU.add,
                       reads=bout.all() + stt + cmpo.all(), writes=sm.k("acc%d" % (i % 2)))
                self.o("dve", "scalar_tensor_tensor", out=acc, in0=bout.ap[:, 0:128], scalar=rw, in1=accf, op0=ALU.mult, op1=ALU.add,
                       reads=bout.all() + stt + sm.k("acc%d" % (i % 2)), writes=acck)
                bt = self.bank()
                self.o("pe", "transpose", bc16(bt.ap)[:, 0:128], acc, self.identb, reads=acck + self.cst.all(), writes=bt.all())
                self.o("act", "activation", ogv[:, r, :], bc16(bt.ap)[:, 0:128], AF.Copy, reads=bt.all(), writes=og.all())
                if r == 3:
                    sc.dma("pool", mixT.ap[g * 512:(g + 1) * 512, tt * 128:(tt + 1) * 128].rearrange("(h p) t -> p h t", p=128), ogv,
                           reads=og.all(), writes=mixT.k(*range(g * 4, g * 4 + 4)))

            its = [(tt, r) for tt in range(NT) for r in range(4)]
            S_(its[0])
            for k, it in enumerate(its):
                P_(it)
                if k + 1 < len(its):
                    S_(its[k + 1])
                C_(it)

        s0 = None
        if Wf is not None:
            s0 = self.sm.ap[0:1, 1984:1992]
            r0 = self.F[0]
            r1 = self.F[1]
            self.tok0_proj(Wf, [2584, 3096, 3608, 4120], r0)
            sc.dma("sp", r1.ap[0:1, 0:2048], grow.ap[0:1, 0:2048], reads=grow.all(), writes=r1.all())
            self.o("dve", "tensor_tensor", out=r1.ap[0:1, 0:1024], in0=r1.ap[0:1, 1024:2048], in1=r1.ap[0:1, 0:1024], op=ALU.subtract,
                   reads=r1.all(), writes=r1.all())
            self.o("act", "activation", r1.ap[0:1, 0:1024], r1.ap[0:1, 0:1024], AF.Sigmoid, reads=r1.all(), writes=r1.all())
            self.o("dve", "tensor_scalar", out=r1.ap[0:1, 0:1024], in0=r1.ap[0:1, 0:1024], scalar1=-1.0, scalar2=1.0, op0=ALU.mult, op1=ALU.add,
                   reads=r1.all(), writes=r1.all())
            self.o("act", "activation", r0.ap[0:1, 1024:2048], r0.ap[0:1, 1024:2048], AF.Sigmoid, scale=-1.0, reads=r0.all(), writes=r0.all())
            self.o("dve", "tensor_tensor", out=r0.ap[0:1, 1024:2048], in0=r0.ap[0:1, 1024:2048], in1=r1.ap[0:1, 0:1024], op=ALU.mult,
                   reads=r0.all() + r1.all(), writes=r0.all())
            self.o("dve", "tensor_tensor", out=r0.ap[0:1, 0:1024], in0=r0.ap[0:1, 0:1024], in1=r0.ap[0:1, 1024:2048], op=ALU.mult,
                   reads=r0.all(), writes=r0.all())
            self.o("dve", "tensor_reduce", out=s0, in_=r0.ap[0:1, 0:1024].rearrange("p (h d) -> p h d", h=8), axis=AX.X, op=ALU.add,
                   reads=r0.all(), writes=self.sm.k("s0h"))
        self.hgrn2(fmT, tmv, mixT, s0)

        for c in range(16):
            sc.dma(self.q(), Av[:, c, :], mixT.ap[c * 128:(c + 1) * 128, :], reads=mixT.k(c), writes=self.A.k(c))
        self.gemm_fm(Wout, [[gg * 512 + j * 128 for j in range(4)] for gg in range(4)], 16, rhs_fn, rhs_reads, self.resid_evac(hT))

    def hgrn2(self, fmT, tmv, mixT, s0=None):
        sc = self.sc
        seg = self.ones
        self.o("pool", "memset", seg.ap[:, :].rearrange("p (c l) -> p c l", l=32)[:, :, 0:1], 0.0, writes=seg.all())
        lbv = self.sm.ap[:, 1800:1808]
        omv = self.sm.ap[:, 1808:1816]
        sk = self.sm.k("hg")
        self.o("dve", "tensor_tensor", out=lbv, in0=self.vec.ap[:, 616:624], in1=self.vec.ap[:, 608:616], op=ALU.subtract, reads=self.vec.all(), writes=sk)
        self.o("act", "activation", lbv, lbv, AF.Sigmoid, reads=sk, writes=sk)
        self.o("dve", "tensor_scalar", out=omv, in0=lbv, scalar1=-1.0, scalar2=1.0, op0=ALU.mult, op1=ALU.add, reads=sk, writes=sk)
        chm = self.c1.ap[:, 1056:1060]
        for h in range(8):
            qr = self.H[0]
            sg = self.H[1]
            gs = self.H[2]
            vv = self.H[3]
            sc.dma("sp", qr.ap[:, :], fmT.ap[(4 + h) * 128:(5 + h) * 128, :], reads=fmT.k(4 + h), writes=qr.all())
            sc.dma("pool", sg.ap[:, :], fmT.ap[(12 + h) * 128:(13 + h) * 128, :], reads=fmT.k(12 + h), writes=sg.all())
            sc.dma("sp", gs.ap[:, :], fmT.ap[(20 + h) * 128:(21 + h) * 128, :], reads=fmT.k(20 + h), writes=gs.all())
            vvv = vv.ap.rearrange("p (t d) -> p t d", t=16)
            sc.dma("pool", vvv, tmv.ap[:, 2048 + h * 128:2048 + (h + 1) * 128].rearrange("(t p) d -> p t d", p=128),
                   reads=tmv.k(*range(16)), writes=vv.all())
            f = self.F[0]
            bcm = self.F[1]
            EQ = self.F[2]
            EK = self.F[3]
            N = 2048
            self.o("dve", "tensor_scalar", out=f.ap[:, 0:N], in0=sg.ap[:, :], scalar1=omv[:, h:h + 1], scalar2=lbv[:, h:h + 1], op0=ALU.mult, op1=ALU.add,
                   reads=sg.all() + sk, writes=f.all())
            self.o("act", "activation", bcm.ap[:, 0:N], f.ap[:, 0:N], AF.Ln, reads=f.all(), writes=bcm.all())
            self.o("dve", "tensor_tensor_scan", out=bcm.ap[:, 0:N], data0=seg.ap[:, 0:N], data1=bcm.ap[:, 0:N], initial=0.0, op0=ALU.mult, op1=ALU.add,
                   reads=bcm.all() + seg.all(), writes=bcm.all())
            self.o("act", "activation", EQ.ap[:, 0:N], bcm.ap[:, 0:N], AF.Exp, reads=bcm.all(), writes=EQ.all())
            self.o("act", "activation", EK.ap[:, 0:N], bcm.ap[:, 0:N], AF.Exp, scale=-1.0, reads=bcm.all(), writes=EK.all())
            self.o("dve", "tensor_scalar", out=f.ap[:, 0:N], in0=f.ap[:, 0:N], scalar1=-1.0, scalar2=1.0, op0=ALU.mult, op1=ALU.add, reads=f.all(), writes=f.all())
            qt = self.H[4]
            kt_ = self.H[5]
            self.o("dve", "tensor_tensor", out=qt.ap[:, :], in0=qr.ap[:, :], in1=EQ.ap[:, 0:N], op=ALU.mult, reads=qr.all() + EQ.all(), writes=qt.all())
            self.o("dve", "tensor_tensor", out=kt_.ap[:, :], in0=f.ap[:, 0:N], in1=EK.ap[:, 0:N], op=ALU.mult, reads=f.all() + EK.all(), writes=kt_.all())
            St = self.sm.ap[:, 1824:1952]
            skS = self.sm.k("S")
            oT = self.F[4]

            def Sb(i):
                return self.H[1].ap[:, (i % 8) * 128:(i % 8 + 1) * 128], self.H[1].k("S%d" % (i % 8))

            km = self.F[5]
            kmb = bc16(km.ap)[:, 0:512].rearrange("p (c d) -> p c d", c=4)
            am = bc16(km.ap)[:, 512:640]

            def front(tt):
                tsl = slice(tt * 128, (tt + 1) * 128)
                b1 = self.bank()
                self.o("pe", "transpose", bc16(b1.ap)[:, 0:128], kt_.ap[:, tsl], self.identb, reads=kt_.all() + self.cst.all(), writes=b1.all())
                for c in range(4):
                    self.o("act", "activation", kmb[:, c, :], bc16(b1.ap)[:, 0:128], AF.Copy, scale=chm[:, c:c + 1],
                           reads=b1.all() + self.c1.all(), writes=km.k("k%d" % c))
                b2 = self.bank()
                self.o("pe", "matmul", b2.ap[:, 0:128], kt_.ap[:, tsl], qt.ap[:, tsl], start=True, stop=True, reads=kt_.all() + qt.all(), writes=b2.all())
                self.o("dve", "tensor_tensor", out=am, in0=b2.ap[:, 0:128], in1=self.maskBD, op=ALU.mult, reads=b2.all() + self.cst.all(), writes=km.k("am"))
                if tt == 0 and s0 is not None:
                    self.o("dve", "tensor_copy", am[0:1, 0:1], s0[:, h:h + 1], reads=self.sm.k("s0h"), writes=km.k("am"))
                bu = self.bank()
                nU = 4 if tt < NT - 1 else 3
                for c in range(nU):
                    self.o("pe", "matmul", bu.ap[:, c * 128:(c + 1) * 128], kmb[:, c, :], vvv[:, tt, :], start=True, stop=True,
                           reads=km.k("k%d" % c) + vv.all(), writes=bu.all())
                bo = self.bank()
                self.o("pe", "matmul", bo.ap[:, 0:128], vvv[:, tt, :], am, start=True, stop=False, reads=vv.all() + km.k("am"), writes=bo.all())
                return bo, bu, nU

            def chain(tt, bu, nU):
                for c in range(nU):
                    gi = tt * 4 + c
                    sbap, sbk = Sb(gi)
                    if gi == 0:
                        self.o("dve", "tensor_copy", St, bu.ap[:, 0:128], reads=bu.all(), writes=skS)
                    else:
                        eLp = EQ.ap[:, (gi - 1) * 32 + 31:(gi - 1) * 32 + 32]
                        self.o("dve", "scalar_tensor_tensor", out=St, in0=St, scalar=eLp, in1=bu.ap[:, c * 128:(c + 1) * 128], op0=ALU.mult, op1=ALU.add,
                               reads=bu.all() + EQ.all() + skS, writes=skS)
                    eL = EQ.ap[:, gi * 32 + 31:gi * 32 + 32]
                    self.o("act", "activation", sbap, St, AF.Copy, scale=eL, reads=skS + EQ.all(), writes=sbk)

            def readout(tt, bo):
                for c in range(4):
                    gi = tt * 4 + c
                    if gi == 0:
                        continue
                    sbap, sbk = Sb(gi - 1)
                    self.o("pe", "matmul", bo.ap[:, c * 32:(c + 1) * 32], sbap, qt.ap[:, gi * 32:(gi + 1) * 32], start=False, stop=(c == 3),
                           reads=sbk + qt.all(), writes=bo.all())
                self.o("act", "activation", oT.ap[:, tt * 128:(tt + 1) * 128], bo.ap[:, 0:128], AF.Copy, reads=bo.all(), writes=oT.all())

            cur = front(0)
            for tt in range(NT):
                bo, bu, nU = cur
                chain(tt, bu, nU)
                if tt + 1 < NT:
                    cur = front(tt + 1)
                readout(tt, bo)
            sq = kt_
            self.o("act", "activation", sq.ap[:, :], oT.ap[:, 0:N], AF.Square, reads=oT.all(), writes=sq.all())
            rstd = self.F[0]
            for j in range(4):
                bs = self.bank()
                self.o("pe", "matmul", bs.ap[:, :], self.onesb, sq.ap[:, j * 512:(j + 1) * 512], start=True, stop=True, reads=sq.all() + self.cst.all(), writes=bs.all())
                self.o("act", "activation", rstd.ap[:, j * 512:(j + 1) * 512], bs.ap[:, :], AF.Sqrt, scale=1.0 / HD, bias=EPS, reads=bs.all(), writes=rstd.all())
            self.o("dve", "reciprocal", rstd.ap[:, 0:N], rstd.ap[:, 0:N], reads=rstd.all(), writes=rstd.all())
            self.o("dve", "scalar_tensor_tensor", out=oT.ap[:, 0:N], in0=oT.ap[:, 0:N], scalar=self.vec.ap[:, 624 + h:625 + h], in1=rstd.ap[:, 0:N],
                   op0=ALU.mult, op1=ALU.mult, reads=oT.all() + rstd.all() + self.vec.all(), writes=oT.all())
            self.o("dve", "tensor_tensor", out=qt.ap[:, :], in0=oT.ap[:, 0:N], in1=gs.ap[:, :], op=ALU.mult, reads=oT.all() + gs.all(), writes=qt.all())
            sc.dma("sp", mixT.ap[(8 + h) * 128:(9 + h) * 128, :], qt.ap[:, :], reads=qt.all(), writes=mixT.k(8 + h))
        self.o("pool", "memset", self.ones.ap[:, :], 1.0, writes=self.ones.all())

    def final_out(self, hT, out_ap, outT):
        sc = self.sc
        bs = [self.bank() for _ in range(4)]
        rstd = self.F[5]
        if getattr(self, "ssq_ready", False):
            acc = self.F[0]
            for j in range(4):
                self.o("pe", "matmul", bs[j].ap[:, :], self.onesF, acc.ap[:, j * 512:(j + 1) * 512], start=True, stop=True,
                       reads=acc.all() + self.cstf.all(), writes=bs[j].all())
            self.ssq_ready = False
        else:
          for c in range(16):
            ht = self.F[c % 2]
            sq = self.H[c % 2]
            sc.dma(self.q(), ht.ap[:, 0:2048], hT.ap[c * 128:(c + 1) * 128, :], reads=hT.k(c), writes=ht.all())
            self.o("act", "activation", sq.ap[:, :], ht.ap[:, 0:2048], AF.Square, reads=ht.all(), writes=sq.all())
            for j in range(4):
                self.o("pe", "matmul", bs[j].ap[:, :], self.onesb, sq.ap[:, j * 512:(j + 1) * 512], start=(c == 0), stop=(c == 15),
                       reads=sq.all() + self.cst.all(), writes=bs[j].all())
        for j in range(4):
            self.o("act", "activation", rstd.ap[:, j * 512:(j + 1) * 512], bs[j].ap[:, :], AF.Sqrt, scale=1.0 / D, bias=EPS,
                   reads=bs[j].all(), writes=rstd.all())
        self.o("dve", "reciprocal", rstd.ap[:, 0:2048], rstd.ap[:, 0:2048], reads=rstd.all(), writes=rstd.all())
        hv = hT.ap.rearrange("(c p) t -> p c t", p=128)
        for tt in range(NT):
            ht = self.F[tt % 2]
            yn = self.F[2 + tt % 2]
            ot = self.F[4]
            htv = ht.ap[:, 0:2048].rearrange("p (c t) -> p c t", c=16)
            ynv = yn.ap[:, 0:2048].rearrange("p (c t) -> p c t", c=16)
            sc.dma(self.q(), htv, hv[:, :, tt * 128:(tt + 1) * 128], reads=hT.k(*range(16)), writes=ht.all())
            for c in range(16):
                self.o("dve", "scalar_tensor_tensor", out=ynv[:, c, :], in0=htv[:, c, :], scalar=self.vec.ap[:, 64 + c:65 + c],
                       in1=rstd.ap[:, tt * 128:(tt + 1) * 128], op0=ALU.mult, op1=ALU.mult,
                       reads=ht.all() + rstd.all() + self.vec.all(), writes=yn.all())
            for g in range(4):
                b = self.bank()
                for j in range(4):
                    c = g * 4 + j
                    self.o("pe", "transpose", b.ap[:, j * 128:(j + 1) * 128], ynv[:, c, :], self.identf, reads=yn.all() + self.cstf.all(), writes=b.all())
                if g % 2:
                    self.o("act", "activation", ot.ap[:, g * 512:(g + 1) * 512], b.ap[:, :], AF.Copy, reads=b.all(), writes=ot.all())
                else:
                    self.o("dve", "tensor_copy", ot.ap[:, g * 512:(g + 1) * 512], b.ap[:, :], reads=b.all(), writes=ot.all())
            sc.dma("sp", out_ap[tt * 128:(tt + 1) * 128, :], ot.ap[:, 0:2048], reads=ot.all(), writes=outT.all())


def _host_consts():
    bf = ml_dtypes.bfloat16
    i = np.arange(128)
    cst = np.zeros((128, 1024), dtype=np.float32)
    cst[:, 0:128] = np.eye(128)
    cst[:, 128:256] = 1.0
    cst[:, 256:384] = (i[:, None] <= i[None, :])
    cst[:, 384:512] = np.where(i[:, None] > i[None, :], -30000.0, 0.0)
    cst[:, 512:640] = (i[:, None] <= i[None, :]) & ((i[:, None] // 32) == (i[None, :] // 32))
    cst[:, 640:768] = (i[:, None] > i[None, :])
    cst[:, 768:896] = np.where(i[None, :] >= i[:, None], -30000.0, 0.0)
    cstf = np.zeros((128, 512), np.float32)
    cstf[:, 0:128] = np.eye(128)
    cstf[:, 128:256] = (i[:, None] <= i[None, :])
    cstf[:, 256:384] = 1.0
    cstf[:, 384:512] = (i[None, :] < i[:, None])
    half = 16
    freqs = (500000.0 ** (-np.arange(half, dtype=np.float32) / half)).astype(np.float32)

    def cs(pos):
        ang = pos.astype(np.float32)[:, None] * freqs
        return np.concatenate([np.cos(ang), np.sin(ang)], -1).astype(np.float32)

    c1 = np.zeros((128, 1536), np.float32)
    c1[:, 0:512] = cs(np.arange(S)).reshape(16, 128, 32).transpose(1, 0, 2).reshape(128, 512)
    cend = np.arange(127) * 16 + 31
    c1[0:127, 512:544] = cs(cend)
    t = np.arange(S)
    jb = np.arange(32)
    blk = t // 64
    started = jb[None, :] <= blk[:, None]
    forced = (jb[None, :] == 0) | (started & (blk[:, None] - jb[None, :] < 2))
    sb = np.where(started, np.where(forced, 1000.0, 0.0), -1e30).astype(np.float32)
    c1[:, 544:1056] = sb.reshape(16, 128, 32).transpose(1, 0, 2).reshape(128, 512)
    c1[:, 1056:1060] = (i[:, None] // 32 == np.arange(4)[None, :])
    Ef = (np.arange(S)[None, :] // 64 == np.arange(32)[:, None]).astype(np.float32)
    vis = np.zeros((128, 2048), np.float32)
    vis[0:127] = (cend[:, None] <= t[None, :])
    cs_ = np.arange(127) * 16
    ss = np.arange(32) * 64
    cover = ((cs_[:, None] < ss[None, :] + 64) & (cs_[:, None] + 32 > ss[None, :])).astype(np.float32)
    return dict(cst=cst.astype(bf), cstf=cstf, c1=c1, Ef=Ef.astype(bf), vis=np.ascontiguousarray(vis.astype(bf)).view(np.float32),
                cover=cover.astype(bf))


def _build(with_cd=True):
    nc = bass.Bass("TRN2", target_bir_lowering=False)
    es = contextlib.ExitStack()
    with es:
        sc = Sched(nc, es)
        P = Net(nc, es, sc)
        ext = {}
        ext["cst"] = P.dram("cst", [128, 1024], BF16, kind="ExternalInput")
        ext["cstf"] = P.dram("cstf", [128, 512], F32, kind="ExternalInput")
        ext["vec"] = P.dram("vec", [128, 1024], F32, kind="ExternalInput")
        ext["c1"] = P.dram("c1", [128, 1536], F32, kind="ExternalInput")
        ext["Ef"] = P.dram("Ef", [32, 2048], BF16, kind="ExternalInput")
        ext["vis"] = P.dram("vis", [128, 1024], F32, kind="ExternalInput")
        ext["cover"] = P.dram("cover", [127, 32], BF16, kind="ExternalInput")
        hn0 = P.dram("hn0", [128, 1024], F32, kind="ExternalInput")
        x = P.dram("x", [NSEQ * S, D], F32, kind="ExternalInput")
        out = P.dram("out", [NSEQ * S, D], F32, kind="ExternalOutput")
        wspec = [("ab_win", D, AB_IN), ("ab_wout", D, D), ("cd_win", D, CD_IN), ("cd_wout", D, D),
                 ("up0", D, 2 * FFN), ("up1", D, 2 * FFN), ("dn0", FFN, D), ("dn1", FFN, D), ("w1", 8192, 128), ("w2", 256, 128)]
        W = {}
        WF = {}
        crow = P.dram("crow", [1, 4096], F32, kind="ExternalInput")
        grow = P.dram("grow", [1, 2048], F32, kind="ExternalInput")
        casts = {}
        for nm, r, c in wspec:
            src = P.dram(nm, [r, c], F32, kind="ExternalInput")
            WF[nm] = src
            if nm.startswith("up"):
                W[nm] = P.dram(nm + "_b", [NF, 128, 16, 256], BF16)
                casts[nm] = (lambda src=src, dst=W[nm]: P.cast_up(src, dst))
            elif nm.startswith("dn"):
                W[nm] = P.dram(nm + "_b", [16, 128, NF, 128], BF16)
                casts[nm] = (lambda src=src, dst=W[nm]: P.cast_dn(src, dst))
            else:
                W[nm] = P.dram(nm + "_b", [r, c], BF16)
                casts[nm] = (lambda src=src, dst=W[nm], r=r, c=c, nm=nm: P.cast_weight(src, dst, r, c, split=(2048 if nm == "ab_win" else None)))
        hTs = [P.dram("hT%d" % i, [D, S], F32) for i in range(NSEQ)]
        qkT = P.dram("qkT", [D, S], BF16)
        sbT = P.dram("sbT", [D, S], BF16)
        tmv = P.dram("tmv", [S, 3072], BF16)
        mixT = P.dram("mixT", [D, S], BF16)
        actT = P.dram("actT", [FFN, S], BF16)
        fmT = P.dram("fmT", [28 * 128, S], BF16)
        P.setup(ext)
        P.setup_l1(ext)
        casts["ab_win"]()
        casts["ab_wout"]()
        P.deferred = [casts[k] for k in ("up0", "dn0", "cd_win", "cd_wout", "w1", "w2", "up1", "dn1")]
        P.x_to_hT(x.ap[0:S, :], x, hTs[0])
        for s_ in range(NSEQ):
            hT = hTs[s_]
            sc.avoid_pool = (s_ == 0)
            sc.dma("sp", P.hn.ap[:, :], hn0.ap[:, :], reads=hn0.all(), writes=P.hn.all())
            P.mixer_ab(hT, W["ab_win"], W["ab_wout"], qkT, sbT, tmv, None, mixT, Wf=WF["ab_win"], crow=crow)
            P.ffn(hT, 0, W["up0"], W["dn0"], actT)
            if with_cd:
                P.mixer_cd(hT, W["cd_win"], W["cd_wout"], W["w1"], W["w2"], fmT, tmv, mixT, ext, Wf=WF["cd_win"], grow=grow)
            P.ffn(hT, 1, W["up1"], W["dn1"], actT)
            if s_ + 1 < NSEQ:
                ready = P.ssq_ready
                P.x_to_hT(x.ap[(s_ + 1) * S:(s_ + 2) * S, :], x, hTs[s_ + 1], base=2)
                P.ssq_ready = ready
            P.final_out(hT, out.ap[s_ * S:(s_ + 1) * S, :], out)
        sc.finish([t.w for t in out.all() if t.w is not None])
        sc.replay()
    return nc


def _vec(p):
    v = np.zeros((128, 1024), np.float32)
    T16 = lambda g: np.asarray(g, np.float32).reshape(16, 128).T
    v[:, 0:16] = T16(p["norm_mix"][0])
    v[:, 16:32] = T16(p["norm_ffn"][0])
    v[:, 32:48] = T16(p["norm_ffn"][1])
    v[:, 48:64] = T16(p["norm_mix"][1])
    v[:, 64:80] = T16(p["norm_final"])
    for l in range(2):
        cw = np.concatenate([p["ffn_conv_w"][l], p["ffn_conv_b"][l][None]], 0)
        v[:, 80 + l * 176:80 + (l + 1) * 176] = cw.reshape(4, 44, 128).transpose(2, 1, 0).reshape(128, 176)
    cwv = np.concatenate([p["ab_conv_w"][0], p["ab_conv_b"][0][None]], 0)
    v[:, 432:512] = cwv.reshape(5, 16, 128).transpose(2, 1, 0).reshape(128, 80)
    v[:, 512:528] = p["ab_gate_b"][0][None, :]
    v[:, 544:608] = p["cd_cmp_pos"][0].transpose(2, 0, 1).reshape(128, 64)
    v[:, 608:616] = p["hgrn_gamma"][0].reshape(8, 128).T
    v[:, 616:624] = p["hgrn_gamma"][1].reshape(8, 128).T
    v[:, 624:632] = p["cd_head_norm"][0].reshape(8, 128).T
    return v


def kernel(**p):
    p = {k: np.asarray(v) for k, v in p.items()}
    hc = _host_consts()
    vec = _vec(p)
    hn0 = np.ascontiguousarray(np.broadcast_to(p["ab_head_norm"][0][None, :], (128, 1024))).astype(np.float32)
    crow = np.concatenate([p["ab_conv_w"][0][3], p["ab_conv_b"][0]])[None, :].astype(np.float32)
    grow = np.concatenate([p["hgrn_gamma"][0], p["hgrn_gamma"][1]])[None, :].astype(np.float32)
    shared = {"crow": np.ascontiguousarray(crow), "grow": np.ascontiguousarray(grow), "cst": hc["cst"], "cstf": hc["cstf"], "vec": vec, "c1": hc["c1"], "Ef": hc["Ef"], "vis": hc["vis"], "cover": hc["cover"],
              "hn0": hn0, "ab_win": p["ab_w_in"][0], "ab_wout": p["ab_w_out"][0], "cd_win": p["cd_w_in"][0], "cd_wout": p["cd_w_out"][0],
              "up0": p["ffn_w_up"][0], "up1": p["ffn_w_up"][1], "dn0": p["ffn_w_down"][0], "dn1": p["ffn_w_down"][1],
              "w1": np.ascontiguousarray(p["cd_cmp_w1"][0].reshape(8192, 128)), "w2": np.ascontiguousarray(p["cd_cmp_w2"][0].reshape(256, 128))}
    nc = _build()
    in_maps = []
    for c in range(8):
        m = dict(shared)
        m["x"] = np.ascontiguousarray(p["x"][2 * c:2 * c + 2].reshape(NSEQ * S, D))
        in_maps.append(m)
    res = run_bass_kernel_spmd(nc, in_maps, core_ids=list(range(8)))
    outs = [np.asarray(r["out"]).reshape(NSEQ, S, D) for r in res.results]
    return np.concatenate(outs, 0).astype(np.float32)
```

```python
import contextlib
import numpy as np
import ml_dtypes
import concourse.bass as bass
import concourse.mybir as mybir
from concourse.bass_utils import run_bass_kernel_spmd

F32 = mybir.dt.float32
BF16 = mybir.dt.bfloat16
AF = mybir.ActivationFunctionType
ALU = mybir.AluOpType
AX = mybir.AxisListType

D = 2048
S = 2048
NSEQ = 2
NT = S // 128
HD = 128
FFN = 5632
NF = FFN // 128
AB_IN = 7184
CD_IN = 6680
EPS = 1e-6
SEM_LIMIT = 30000


class Trk:
    __slots__ = ("w", "r", "x")

    def __init__(self):
        self.w = None
        self.r = {}
        self.x = False


class Sched:
    def __init__(self, nc, es):
        self.nc = nc
        self.es = es
        self.names = ["pe", "act", "dve", "pool", "sp"]
        self.prog = {e: [] for e in self.names}
        self.seen = {e: {} for e in self.names}
        self.sems = []
        self.csem = {}
        self.dq = {}
        self.dqi = {}
        self.ndma = 0

    def newsem(self):
        h = self.es.enter_context(self.nc.semaphore("s%d" % len(self.sems)))
        self.sems.append(h)
        return len(self.sems) - 1

    def _collect(self, eng, reads, writes):
        need = {}

        def add(tok, war):
            if tok is None:
                return
            s, v, e = tok
            if e == eng:
                if eng == "pe" or war:
                    return
            if self.seen[eng].get(s, 0) >= v:
                return
            if need.get(s, 0) < v:
                need[s] = v

        for b in reads:
            add(b.w, False)
            if b.x:
                for t in b.r.values():
                    add(t, True)
        for b in writes:
            add(b.w, False)
            for t in b.r.values():
                add(t, True)
        return need

    def _commit(self, eng, need):
        for s, v in need.items():
            self.seen[eng][s] = v
        return list(need.items())

    def op(self, eng, fn, reads=(), writes=()):
        need = self._collect(eng, reads, writes)
        cs = self.csem.get(eng)
        if cs is None or cs[1] >= SEM_LIMIT:
            cs = [self.newsem(), 0]
            self.csem[eng] = cs
        cs[1] += 1
        tok = (cs[0], cs[1], eng)
        self.prog[eng].append((self._commit(eng, need), fn, (cs[0], 1)))
        for b in reads:
            b.r[eng] = tok
        for b in writes:
            b.w = tok
            b.r = {}
        return tok

    def dma(self, q, out, in_, reads=(), writes=(), cast=False, **kw):
        if q == "pool" and not cast and getattr(self, "avoid_pool", False):
            q = "sp"
        need = self._collect(q, reads, writes)
        if q not in self.dq:
            self.dq[q] = [[self.newsem(), 0] for _ in range(6)]
            self.dqi[q] = 0
        lst = self.dq[q]
        i = self.dqi[q]
        self.dqi[q] = (i + 1) % len(lst)
        if lst[i][1] + 16 > SEM_LIMIT:
            lst[i] = [self.newsem(), 0]
        ds = lst[i]
        if ds[1] > 0 and self.seen[q].get(ds[0], 0) < ds[1]:
            if need.get(ds[0], 0) < ds[1]:
                need[ds[0]] = ds[1]
        ds[1] += 16
        tok = (ds[0], ds[1], "dma")
        self.ndma += 1
        key = "d%d" % self.ndma

        def fn(e, out=out, in_=in_, kw=kw):
            return e.dma_start(out=out, in_=in_, **kw)

        self.prog[q].append((self._commit(q, need), fn, (ds[0], 16)))
        for b in reads:
            if len(b.r) > 24:
                for k in [k for k in b.r if k.startswith("d")][:8]:
                    pass
            b.r[key] = tok
        for b in writes:
            b.w = tok
            b.r = {}
        return tok

    def finish(self, final_toks):
        need = {}
        for s, v, _ in final_toks:
            if need.get(s, 0) < v:
                need[s] = v
        self.final = list(need.items())

    def replay(self):
        nc = self.nc
        E = {"pe": nc.tensor, "act": nc.scalar, "dve": nc.vector, "pool": nc.gpsimd, "sp": nc.sync}
        sems = self.sems
        with nc.Block() as block:
            def run(name):
                def body(eng):
                    for waits, fn, inc in self.prog[name]:
                        for s, v in waits:
                            eng.wait_ge(sems[s], v)
                        ins = fn(eng)
                        ins.then_inc(sems[inc[0]], inc[1])
                    if name == "sp":
                        for s, v in self.final:
                            eng.wait_ge(sems[s], v)
                return body

            block.tensor(run("pe"))
            block.scalar(run("act"))
            block.vector(run("dve"))
            block.gpsimd(run("pool"))
            block.sync(run("sp"))


class T:
    def __init__(self, ap):
        self.ap = ap
        self.t = {}

    def k(self, *keys):
        out = []
        for key in keys:
            if key not in self.t:
                tr = Trk()
                base = self.t.get(None)
                if base is not None:
                    tr.w = base.w
                    tr.r = dict(base.r)
                    tr.x = base.x
                self.t[key] = tr
            out.append(self.t[key])
        return out

    def all(self):
        if None not in self.t:
            self.t[None] = Trk()
        return list(self.t.values())


class Prog:
    def __init__(self, nc, es, sc):
        self.nc = nc
        self.es = es
        self.sc = sc
        self._n = 0
        self.qi = 0

    def name(self, p):
        self._n += 1
        return "%s_%d" % (p, self._n)

    def sb(self, shape, dt, nm="sb"):
        return T(self.es.enter_context(self.nc.sbuf_tensor(self.name(nm), list(shape), dt)))

    def ps(self, shape, dt=F32, nm="ps"):
        return T(self.es.enter_context(self.nc.psum_tensor(self.name(nm), list(shape), dt)))

    def dram(self, nm, shape, dt, kind="Internal"):
        return T(self.nc.dram_tensor(nm, list(shape), dt, kind=kind).ap())

    def q(self):
        self.qi += 1
        return "sp" if self.qi % 2 else "pool"


def bc16(ap):
    return ap.bitcast(BF16)


class Net(Prog):
    def setup(self, ext):
        nc, sc = self.nc, self.sc
        self.ext = ext
        self.deferred = []
        self.A = self.sb([128, 16 * 2048], BF16, "A")
        self.slab = [self.sb([128, 8192], BF16, "slab") for _ in range(2)]
        self.slab_i = 0
        self.F = [self.sb([128, 2052], F32, "F") for _ in range(6)]
        self.H = [self.sb([128, 2048], BF16, "H") for _ in range(6)]
        self.banks = [self.ps([128, 512], F32, "bank") for _ in range(8)]
        for b in self.banks:
            b.all()[0].x = True
        self.bank_i = 0
        self.cst = self.sb([128, 1024], BF16, "cst")
        self.cstf = self.sb([128, 512], F32, "cstf")
        self.vec = self.sb([128, 1024], F32, "vec")
        sc.dma("sp", self.cst.ap[:, :], ext["cst"].ap[:, :], reads=ext["cst"].all(), writes=self.cst.all())
        sc.dma("sp", self.cstf.ap[:, :], ext["cstf"].ap[:, :], reads=ext["cstf"].all(), writes=self.cstf.all())
        sc.dma("sp", self.vec.ap[:, :], ext["vec"].ap[:, :], reads=ext["vec"].all(), writes=self.vec.all())
        self.identb = self.cst.ap[:, 0:128]
        self.onesb = self.cst.ap[:, 128:256]
        self.identf = self.cstf.ap[:, 0:128]
        self.maskUTb = self.cst.ap[:, 256:384]
        self.maskSLb = self.cst.ap[:, 384:512]
        self.triF = self.cstf.ap[:, 128:256]
        self.onesF = self.cstf.ap[:, 256:384]
        self.maskSLf = self.cstf.ap[:, 384:512]
        self.ones = self.sb([128, 2048], F32, "ones")
        sc.op("pool", lambda e: e.memset(self.ones.ap[:, :], 1.0), writes=self.ones.all())
        self.hn = self.sb([128, 1024], F32, "hn")
        self.sm = self.sb([128, 2048], F32, "sm")
        self.smi = 0

    @staticmethod
    def pipe(items, load, compute):
        if not items:
            return
        h = load(items[0])
        for i, it in enumerate(items):
            nh = load(items[i + 1]) if i + 1 < len(items) else None
            compute(it, h)
            h = nh

    def bank(self):
        b = self.banks[self.bank_i % 8]
        self.bank_i += 1
        return b

    def next_slab(self):
        s = self.slab[self.slab_i % 2]
        self.slab_i += 1
        return s

    def run_deferred(self, n):
        for _ in range(n):
            if self.deferred:
                self.deferred.pop(0)()

    def cast_up(self, src, dst):
        for k in range(16):
            for half in range(2):
                for fh in range(2):
                    f0 = fh * 22
                    c0 = half * FFN + f0 * 128
                    self.sc.dma("pool", dst.ap[f0:f0 + 22, :, k, half * 128:(half + 1) * 128],
                                src.ap[k * 128:(k + 1) * 128, c0:c0 + 22 * 128].rearrange("p (f c) -> f p c", c=128),
                                reads=src.k(k), writes=dst.k((k, half, fh)), cast=True)

    def cast_dn(self, src, dst):
        for k in range(NF):
            self.sc.dma("pool", dst.ap[:, :, k, :], src.ap[k * 128:(k + 1) * 128, :].rearrange("p (n c) -> n p c", c=128),
                        reads=src.k(k), writes=dst.k(k), cast=True)

    def cast_weight(self, src, dst, rows, cols, split=None):
        if split:
            dst.split = split
            for (c0, c1, hk) in ((0, split, 0), (split, cols, 1)):
                for r in range(0, rows, 128):
                    self.sc.dma("pool", dst.ap[r:r + 128, c0:c1], src.ap[r:r + 128, c0:c1], reads=src.k(r), writes=dst.k((r, hk)), cast=True)
            return
        for r in range(0, rows, 128):
            self.sc.dma("pool", dst.ap[r:r + 128, :], src.ap[r:r + 128, :], reads=src.k(r), writes=dst.k(r), cast=True)

    def x_to_hT(self, x_ap, xT, hT, base=0):
        sc = self.sc
        hv = hT.ap.rearrange("(c p) t -> p c t", p=128)
        for tt in range(NT):
            xt = self.F[base + tt % 2]
            ot = self.F[base + 2 + tt % 2]
            sc.dma("sp", xt.ap[:, 0:2048], x_ap[tt * 128:(tt + 1) * 128, :], reads=xT.all(), writes=xt.all())
            for g in range(4):
                b = self.bank()
                for j in range(4):
                    c = g * 4 + j
                    sc.op("pe", lambda e, b=b, j=j, c=c, xt=xt: e.transpose(
                        b.ap[:, j * 128:(j + 1) * 128], xt.ap[:, c * 128:(c + 1) * 128], self.identf),
                        reads=xt.all() + self.cstf.all(), writes=b.all())
                if g % 2 == 0:
                    sc.op("act", lambda e, b=b, g=g, ot=ot: e.activation(ot.ap[:, g * 512:(g + 1) * 512], b.ap[:, :], AF.Copy),
                          reads=b.all(), writes=ot.all())
                else:
                    sc.op("dve", lambda e, b=b, g=g, ot=ot: e.tensor_copy(ot.ap[:, g * 512:(g + 1) * 512], b.ap[:, :]),
                          reads=b.all(), writes=ot.all())
            sc.dma("pool", hv[:, :, tt * 128:(tt + 1) * 128], ot.ap[:, 0:2048].rearrange("p (c t) -> p c t", c=16),
                   reads=ot.all(), writes=hT.k(*range(16)))

    def norm_to_A(self, hT, gcol, tok0=False):
        sc = self.sc
        Av = self.A.ap.rearrange("p (c t) -> p c t", c=16)
        bs = [self.bank() for _ in range(4)]
        rstd = self.F[5]
        if getattr(self, "ssq_ready", False):
            acc = self.F[0]
            for j in range(4):
                self.o("pe", "matmul", bs[j].ap[:, :], self.onesF, acc.ap[:, j * 512:(j + 1) * 512], start=True, stop=True,
                       reads=acc.all() + self.cstf.all(), writes=bs[j].all())
            self.ssq_ready = False
        else:
          for c in range(16):
            ht = self.F[c % 2]
            sq = self.H[c % 2]
            sc.dma(self.q(), ht.ap[:, 0:2048], hT.ap[c * 128:(c + 1) * 128, :], reads=hT.k(c), writes=ht.all())
            sc.op("act", lambda e, ht=ht, sq=sq: e.activation(sq.ap[:, :], ht.ap[:, 0:2048], AF.Square),
                  reads=ht.all(), writes=sq.all())
            for j in range(4):
                sc.op("pe", lambda e, j=j, c=c, sq=sq: e.matmul(bs[j].ap[:, :], self.onesb, sq.ap[:, j * 512:(j + 1) * 512],
                                                             start=(c == 0), stop=(c == 15)),
                      reads=sq.all() + self.cst.all(), writes=bs[j].all())
        for j in range(4):
            sc.op("act", lambda e, j=j: e.activation(rstd.ap[:, j * 512:(j + 1) * 512], bs[j].ap[:, :], AF.Sqrt,
                                                      scale=1.0 / D, bias=EPS),
                  reads=bs[j].all(), writes=rstd.all())
        sc.op("dve", lambda e: e.reciprocal(rstd.ap[:, 0:2048], rstd.ap[:, 0:2048]), reads=rstd.all(), writes=rstd.all())
        if tok0:
            h0 = self.sm.ap[:, 1944:1960]
            u0 = self.sm.ap[:, 1960:1976]
            k0 = self.sm.k("u0")
            sc.dma("sp", h0.rearrange("p (c o) -> p c o", o=1), hT.ap.rearrange("(c p) t -> p c t", p=128)[:, :, 0:1],
                   reads=hT.k(*range(16)), writes=k0, allow_slow_non_contiguous=True)
            self.o("dve", "scalar_tensor_tensor", out=u0, in0=h0, scalar=rstd.ap[:, 0:1], in1=self.vec.ap[:, gcol:gcol + 16],
                   op0=ALU.mult, op1=ALU.mult, reads=k0 + rstd.all() + self.vec.all(), writes=k0)
        for c in range(16):
            ht = self.F[2 + c % 2]
            sc.dma(self.q(), ht.ap[:, 0:2048], hT.ap[c * 128:(c + 1) * 128, :], reads=hT.k(c), writes=ht.all())
            sc.op("dve", lambda e, c=c, ht=ht: e.scalar_tensor_tensor(
                out=Av[:, c, :], in0=ht.ap[:, 0:2048], scalar=self.vec.ap[:, gcol + c:gcol + c + 1],
                in1=rstd.ap[:, 0:2048], op0=ALU.mult, op1=ALU.mult),
                reads=ht.all() + rstd.all() + self.vec.all(), writes=self.A.k(c))

    def gemm_fm(self, Wb, col_groups, KC, rhs_fn, rhs_reads, evac, ntc=4):
        sc = self.sc
        Wv = Wb.ap.rearrange("(k p) n -> p k n", p=128)

        def load(grp):
            sl = self.next_slab()
            n = len(grp)
            sv = sl.ap[:, 0:KC * n * 128].rearrange("p (k c) -> p k c", k=KC)
            for i, c0 in enumerate(grp):
                sp_ = getattr(Wb, "split", None)
                if sp_ and c0 + 128 <= sp_:
                    rd = [t for k_, t in Wb.t.items() if isinstance(k_, tuple) and k_[1] == 0]
                else:
                    rd = Wb.all()
                sc.dma(self.q(), sv[:, :, i * 128:(i + 1) * 128], Wv[:, :, c0:c0 + 128], reads=rd, writes=sl.all())
            return sl, sv

        def compute(grp, h):
            sl, sv = h
            for i, c0 in enumerate(grp):
                for tc in range(ntc):
                    b = self.bank()
                    for kc in range(KC):
                        sc.op("pe", lambda e, b=b, kc=kc, i=i, tc=tc, sv=sv: e.matmul(
                            b.ap[:, :], sv[:, kc, i * 128:(i + 1) * 128], rhs_fn(kc, tc), start=(kc == 0), stop=(kc == KC - 1)),
                            reads=sl.all() + rhs_reads(kc), writes=b.all())
                    evac(c0, tc, b)

        self.pipe(col_groups, load, compute)

    def gemm_tm_multi(self, Wb, jobs, KC, lhs_fn, lhs_reads):
        sc = self.sc
        Wv = Wb.ap.rearrange("(k p) n -> p k n", p=128)

        def load(job):
            c0, ncols, evac = job
            sl = self.next_slab()
            sv = sl.ap[:, 0:KC * ncols].rearrange("p (k c) -> p k c", k=KC)
            sc.dma(self.q(), sv[:, :, :], Wv[:, :, c0:c0 + ncols], reads=Wb.all(), writes=sl.all())
            return sl, sv

        def compute(job, h):
            c0, ncols, evac = job
            sl, sv = h
            for tt in range(NT):
                b = self.bank()
                for kc in range(KC):
                    sc.op("pe", lambda e, b=b, kc=kc, tt=tt: e.matmul(
                        b.ap[:, 0:ncols], lhs_fn(kc, tt), sv[:, kc, :], start=(kc == 0), stop=(kc == KC - 1)),
                        reads=sl.all() + lhs_reads(kc), writes=b.all())
                evac(tt, b)

        self.pipe(jobs, load, compute)

    def gemm_tm(self, Wb, c0, ncols, KC, lhs_fn, lhs_reads, evac):
        self.gemm_tm_multi(Wb, [(c0, ncols, evac)], KC, lhs_fn, lhs_reads)

    def tok0_proj(self, Wf, chunks, dst):
        sc = self.sc
        u0 = self.sm.ap[:, 1960:1976]
        for j, c0 in enumerate(chunks):
            b = self.bank()
            for kq in range(4):
                ws = self.F[2 + kq % 2]
                for i in range(4):
                    kc = kq * 4 + i
                    sc.dma(self.q(), ws.ap[:, i * 512:(i + 1) * 512], Wf.ap[kc * 128:(kc + 1) * 128, c0:c0 + 512], reads=Wf.all(), writes=ws.all())
                for i in range(4):
                    kc = kq * 4 + i
                    self.o("pe", "matmul", b.ap[0:1, :], u0[:, kc:kc + 1], ws.ap[:, i * 512:(i + 1) * 512], start=(kc == 0), stop=(kc == 15),
                           reads=ws.all() + self.sm.k("u0"), writes=b.all())
            self.o("act", "activation", dst.ap[0:1, j * 512:(j + 1) * 512], b.ap[0:1, :], AF.Copy, reads=b.all(), writes=dst.all())

    def A_rhs(self):
        Av = self.A.ap.rearrange("p (c t) -> p c t", c=16)
        return (lambda kc, tc: Av[:, kc, tc * 512:(tc + 1) * 512]), (lambda kc: self.A.k(kc))

    def A_lhs(self):
        Av = self.A.ap.rearrange("p (c t) -> p c t", c=16)
        return (lambda kc, tt: Av[:, kc, tt * 128:(tt + 1) * 128]), (lambda kc: self.A.k(kc))

    def resid_evac(self, hT):
        sc = self.sc
        cnt = [0]
        acc = self.F[0]
        seen = set()
        self.ssq_ready = True

        def evac(c0, tc, b):
            i = cnt[0]
            cnt[0] += 1
            t = self.F[2 + i % 3]
            c = c0 // 128
            key = (c, tc)
            sc.dma("sp", t.ap[:, 0:512], hT.ap[c0:c0 + 128, tc * 512:(tc + 1) * 512], reads=hT.k(c), writes=t.all())
            sc.op("dve", lambda e, t=t, b=b: e.tensor_tensor(out=t.ap[:, 0:512], in0=b.ap[:, :], in1=t.ap[:, 0:512], op=ALU.add),
                  reads=b.all() + t.all(), writes=t.all())
            sc.dma("pool", hT.ap[c0:c0 + 128, tc * 512:(tc + 1) * 512], t.ap[:, 0:512], reads=t.all(), writes=hT.k(c))
            accs = acc.ap[:, tc * 512:(tc + 1) * 512]
            if tc not in seen:
                seen.add(tc)
                self.o("act", "activation", accs, t.ap[:, 0:512], AF.Square, reads=t.all(), writes=acc.k(tc))
            else:
                sq = self.H[5].ap.bitcast(F32)[:, (i % 2) * 512:(i % 2 + 1) * 512]
                sqk = self.H[5].k("sq%d" % (i % 2))
                self.o("act", "activation", sq, t.ap[:, 0:512], AF.Square, reads=t.all(), writes=sqk)
                self.o("dve", "tensor_tensor", out=accs, in0=accs, in1=sq, op=ALU.add, reads=sqk + acc.k(tc), writes=acc.k(tc))
        return evac

    def ffn(self, hT, layer, Wup, Wdn, actT):
        sc = self.sc
        self.norm_to_A(hT, 16 + 16 * layer)
        rhs_fn, rhs_reads = self.A_rhs()
        cwo = 80 + layer * 176
        def load_up(f):
            sl = self.next_slab()
            sv = sl.ap[:, 0:16 * 256].rearrange("p (k c) -> p k c", k=16)
            sc.dma("sp", sv[:, 0:8, :], Wup.ap[f, :, 0:8, :], reads=Wup.all(), writes=sl.all())
            sc.dma("pool", sv[:, 8:16, :], Wup.ap[f, :, 8:16, :], reads=Wup.all(), writes=sl.all())
            return sl, sv

        def comp_up(f, h):
            sl, sv = h
            gp = self.F[f % 2]
            up = self.H[2 + f % 2]
            cb = self.F[4]
            sl_ = self.H[4]
            ao = self.H[f % 2]
            if f < 2:
                sc.op("pool", lambda e, gp=gp: e.memset(gp.ap[:, 0:2], 0.0), writes=gp.all())
            for tc in range(4):
                bg = self.bank()
                bu = self.bank()
                for (b, off) in ((bg, 0), (bu, 128)):
                    for kc in range(16):
                        sc.op("pe", lambda e, b=b, kc=kc, tc=tc, off=off, sv=sv: e.matmul(
                            b.ap[:, :], sv[:, kc, off:off + 128], rhs_fn(kc, tc), start=(kc == 0), stop=(kc == 15)),
                            reads=sl.all() + rhs_reads(kc), writes=b.all())
                sc.op("act", lambda e, bg=bg, tc=tc, gp=gp: e.activation(gp.ap[:, 2 + tc * 512:2 + (tc + 1) * 512], bg.ap[:, :], AF.Copy),
                      reads=bg.all(), writes=gp.all())
                sc.op("dve", lambda e, bu=bu, tc=tc, up=up: e.tensor_copy(up.ap[:, tc * 512:(tc + 1) * 512], bu.ap[:, :]),
                      reads=bu.all(), writes=up.all())
            w = lambda j, f=f: self.vec.ap[:, cwo + f * 4 + j:cwo + f * 4 + j + 1]
            sc.op("dve", lambda e, gp=gp, w=w: e.tensor_scalar(out=cb.ap[:, 0:2048], in0=gp.ap[:, 2:2050], scalar1=w(2), scalar2=w(3),
                                                            op0=ALU.mult, op1=ALU.add),
                  reads=gp.all() + self.vec.all(), writes=cb.all())
            sc.op("dve", lambda e, gp=gp, w=w: e.scalar_tensor_tensor(out=cb.ap[:, 0:2048], in0=gp.ap[:, 1:2049], scalar=w(1),
                                                                   in1=cb.ap[:, 0:2048], op0=ALU.mult, op1=ALU.add),
                  reads=gp.all() + cb.all() + self.vec.all(), writes=cb.all())
            sc.op("dve", lambda e, gp=gp, w=w: e.scalar_tensor_tensor(out=cb.ap[:, 0:2048], in0=gp.ap[:, 0:2048], scalar=w(0),
                                                                   in1=cb.ap[:, 0:2048], op0=ALU.mult, op1=ALU.add),
                  reads=gp.all() + cb.all() + self.vec.all(), writes=cb.all())
            sc.op("act", lambda e: e.activation(sl_.ap[:, :], cb.ap[:, 0:2048], AF.Silu), reads=cb.all(), writes=sl_.all())
            sc.op("dve", lambda e, up=up, ao=ao: e.tensor_tensor(out=ao.ap[:, :], in0=sl_.ap[:, :], in1=up.ap[:, :], op=ALU.mult),
                  reads=sl_.all() + up.all(), writes=ao.all())
            sc.dma("sp", actT.ap[f * 128:(f + 1) * 128, :], ao.ap[:, :], reads=ao.all(), writes=actT.k(f))

        self.pipe(list(range(NF)), load_up, comp_up)
        self.run_deferred(4)
        Av = self.A.ap[:, 0:NF * 512].rearrange("p (k t) -> p k t", k=NF)
        av = actT.ap.rearrange("(k p) t -> p k t", p=128)
        evac = self.resid_evac(hT)
        def load_dn(it):
            tc, n = it
            sl = self.next_slab()
            sv = sl.ap[:, 0:NF * 128].rearrange("p (k c) -> p k c", k=NF)
            for kh in range(2):
                sc.dma(self.q(), sv[:, kh * 22:(kh + 1) * 22, :], Wdn.ap[n, :, kh * 22:(kh + 1) * 22, :], reads=Wdn.all(), writes=sl.all())
            return sl, sv

        def comp_dn(it, h):
            tc, n = it
            sl, sv = h
            if n == 0:
                for kh in range(4):
                    slots = sorted(set((k * 512) // 2048 for k in range(kh * 11, kh * 11 + 11)))
                    sc.dma(self.q(), Av[:, kh * 11:(kh + 1) * 11, :], av[:, kh * 11:(kh + 1) * 11, tc * 512:(tc + 1) * 512],
                           reads=actT.k(*range(kh * 11, kh * 11 + 11)), writes=self.A.k(*slots))
            b = self.bank()
            for kc in range(NF):
                sc.op("pe", lambda e, b=b, kc=kc, sv=sv: e.matmul(b.ap[:, :], sv[:, kc, :], Av[:, kc, :],
                                                               start=(kc == 0), stop=(kc == NF - 1)),
                      reads=sl.all() + self.A.k(kc // 4), writes=b.all())
            evac(n * 128, tc, b)

        self.pipe([(tc, n) for tc in range(4) for n in range(16)], load_dn, comp_dn)

    def o(self, eng, meth, *a, reads=(), writes=(), **kw):
        return self.sc.op(eng, lambda e: getattr(e, meth)(*a, **kw), reads=reads, writes=writes)

    def mixer_ab(self, hT, Win, Wout, qkT, sbT, tmv, tmg, mixT, Wf=None, crow=None):
        sc = self.sc
        self.norm_to_A(hT, 0, tok0=Wf is not None)
        rhs_fn, rhs_reads = self.A_rhs()
        lhs_fn, lhs_reads = self.A_lhs()
        cvo = 432
        cnt = [0]

        def evac_qk(c0, tc, b):
            c = c0 // 128
            gp = self.F[c % 2]
            if tc == 0 and c < 2:
                self.o("pool", "memset", gp.ap[:, 0:3], 0.0, writes=gp.all())
            self.o("act", "activation", gp.ap[:, 3 + tc * 512:3 + (tc + 1) * 512], b.ap[:, :], AF.Copy, reads=b.all(), writes=gp.all())
            if tc == 3:
                cb = self.F[2 + c % 2]
                w = lambda j: self.vec.ap[:, cvo + c * 5 + j:cvo + c * 5 + j + 1]
                self.o("dve", "tensor_scalar", out=cb.ap[:, 0:2048], in0=gp.ap[:, 3:2051], scalar1=w(3), scalar2=w(4),
                       op0=ALU.mult, op1=ALU.add, reads=gp.all() + self.vec.all(), writes=cb.all())
                for j in (2, 1, 0):
                    self.o("dve", "scalar_tensor_tensor", out=cb.ap[:, 0:2048], in0=gp.ap[:, j:j + 2048], scalar=w(j),
                           in1=cb.ap[:, 0:2048], op0=ALU.mult, op1=ALU.add, reads=gp.all() + cb.all() + self.vec.all(), writes=cb.all())
                ob = self.H[c % 2]
                self.o("act", "activation", ob.ap[:, :], cb.ap[:, 0:2048], AF.Silu, reads=cb.all(), writes=ob.all())
                sc.dma("sp", qkT.ap[c0:c0 + 128, :], ob.ap[:, :], reads=ob.all(), writes=qkT.k(c))

        self.gemm_fm(Win, [[g * 512 + j * 128 for j in range(4)] for g in range(4)], 16, rhs_fn, rhs_reads, evac_qk)

        def evac_sb(c0, tc, b):
            c = (c0 - 4112) // 128
            ob = self.H[2 + c % 2]
            self.o("act", "activation", ob.ap[:, tc * 512:(tc + 1) * 512], b.ap[:, :], AF.Copy,
                   scale=(HD ** -0.5 if c < 8 else 1.0), reads=b.all(), writes=ob.all())
            if tc == 3:
                sc.dma("sp", sbT.ap[c * 128:(c + 1) * 128, :], ob.ap[:, :], reads=ob.all(), writes=sbT.k(c))

        self.gemm_fm(Win, [[4112 + g * 512 + j * 128 for j in range(4)] for g in range(4)], 16, rhs_fn, rhs_reads, evac_sb)

        jobs = []
        for (c0, dst0) in ((2048, 0), (2560, 512), (3072, 1024), (3584, 1536), (6160, 2048), (6672, 2560)):
            def evac_tm(tt, b, dst0=dst0):
                i = cnt[0]
                cnt[0] += 1
                ob = self.H[4 + i % 2]
                if i % 2:
                    self.o("act", "activation", ob.ap[:, 0:512], b.ap[:, :], AF.Copy, reads=b.all(), writes=ob.all())
                else:
                    self.o("dve", "tensor_copy", ob.ap[:, 0:512], b.ap[:, :], reads=b.all(), writes=ob.all())
                sc.dma("pool", tmv.ap[tt * 128:(tt + 1) * 128, dst0:dst0 + 512], ob.ap[:, 0:512], reads=ob.all(), writes=tmv.k(tt))
            jobs.append((c0, 512, evac_tm))

        G = self.sm
        Gv = G.ap[:, 0:256].rearrange("p (t c) -> p t c", t=16)

        def evac_g(tt, b):
            self.o("dve", "tensor_tensor", out=Gv[:, tt, :], in0=b.ap[:, 0:16], in1=self.vec.ap[:, 512:528], op=ALU.add,
                   reads=b.all() + self.vec.all(), writes=G.all())
        jobs.append((4096, 16, evac_g))
        self.gemm_tm_multi(Win, jobs, 16, lhs_fn, lhs_reads)

        Lf = G.ap[:, 256:384]
        Ii = G.ap[:, 384:512]
        Et = G.ap[:, 512:640]
        Gs = G.ap[:, 640:768]
        EL = G.ap[:, 768:896]
        Lfv = Lf.rearrange("p (t h) -> p t h", t=16)
        Iiv = Ii.rearrange("p (t h) -> p t h", t=16)
        self.o("act", "activation", Lfv, Gv[:, :, 8:16], AF.Exp, scale=-1.0, reads=G.all(), writes=G.all())
        self.o("act", "activation", Lf, Lf, AF.Ln, bias=1.0, reads=G.all(), writes=G.all())
        self.o("dve", "tensor_copy", Iiv, Gv[:, :, 0:8], reads=G.all(), writes=G.all())
        bc = self.bank()
        bl = self.bank()
        self.o("pe", "matmul", bc.ap[:, 0:128], self.triF, Lf, start=True, stop=True, reads=G.all() + self.cstf.all(), writes=bc.all())
        self.o("pe", "matmul", bl.ap[:, 0:128], self.onesF, Lf, start=True, stop=True, reads=G.all() + self.cstf.all(), writes=bl.all())
        self.o("act", "activation", Et, bc.ap[:, 0:128], AF.Exp, scale=-1.0, reads=bc.all(), writes=G.all())
        self.o("act", "activation", EL, bl.ap[:, 0:128], AF.Exp, scale=-1.0, reads=bl.all(), writes=G.all())
        self.o("dve", "tensor_tensor", out=Gs, in0=bc.ap[:, 0:128], in1=Ii, op=ALU.add, reads=bc.all() + G.all(), writes=G.all())
        self.o("act", "activation", Gs, Gs, AF.Exp, bias=float(np.log(HD ** -0.5)), reads=G.all(), writes=G.all())

        self.run_deferred(2)
        s0 = self.sm.ap[0:1, 1976:1984]
        if Wf is not None:
            r0 = self.F[0]
            r1 = self.F[1]
            self.tok0_proj(Wf, [0, 512, 1024, 1536], r0)
            sc.dma("sp", r1.ap[0:1, 0:2048], crow.ap[0:1, 0:2048], reads=crow.all(), writes=r1.all())
            self.o("dve", "tensor_tensor", out=r0.ap[0:1, 0:2048], in0=r0.ap[0:1, 0:2048], in1=r1.ap[0:1, 0:2048], op=ALU.mult,
                   reads=r0.all() + r1.all(), writes=r0.all())
            sc.dma("sp", r1.ap[0:1, 0:2048], crow.ap[0:1, 2048:4096], reads=crow.all(), writes=r1.all())
            self.o("dve", "tensor_tensor", out=r0.ap[0:1, 0:2048], in0=r0.ap[0:1, 0:2048], in1=r1.ap[0:1, 0:2048], op=ALU.add,
                   reads=r0.all() + r1.all(), writes=r0.all())
            self.o("act", "activation", r0.ap[0:1, 0:2048], r0.ap[0:1, 0:2048], AF.Silu, reads=r0.all(), writes=r0.all())
            self.o("dve", "tensor_tensor", out=r0.ap[0:1, 0:1024], in0=r0.ap[0:1, 0:1024], in1=r0.ap[0:1, 1024:2048], op=ALU.mult,
                   reads=r0.all(), writes=r0.all())
            self.o("dve", "tensor_reduce", out=s0, in_=r0.ap[0:1, 0:1024].rearrange("p (h d) -> p h d", h=8), axis=AX.X, op=ALU.add,
                   reads=r0.all(), writes=self.sm.k("s0"))

        Av = self.A.ap.rearrange("p (c t) -> p c t", c=16)
        for c in range(16):
            sc.dma(self.q(), Av[:, c, :], qkT.ap[c * 128:(c + 1) * 128, :], reads=qkT.k(c), writes=self.A.k(c))

        Cst = self.F[5]
        Cb = self.H[5]
        Csv = Cst.ap[:, 0:8 * 129].rearrange("p (h c) -> p h c", h=8)
        Cbv = Cb.ap[:, 0:8 * 129].rearrange("p (h c) -> p h c", h=8)
        ST = self.sm.ap
        sigt = self.F[4]
        live = {}

        def front(it):
            tt, h = it
            gk = self.sm.k("G")
            i = tt * 8 + h
            vt = self.H[tt % 2]
            if h == 0:
                sc.dma("sp", vt.ap[:, :], tmv.ap[tt * 128:(tt + 1) * 128, 0:2048], reads=tmv.k(tt), writes=vt.all())
                self.o("act", "activation", sigt.ap[:, (tt % 2) * 1024:(tt % 2 + 1) * 1024], vt.ap[:, 1024:2048], AF.Sigmoid,
                       reads=vt.all(), writes=sigt.k(tt % 2))
            col = tt * 8 + h
            wb = self.H[4]
            woff = (i % 2) * 1024
            wkf = wb.k("f%d" % (i % 2))
            ktm = wb.ap[:, woff:woff + 128]
            vp = wb.ap[:, woff + 128:woff + 257]
            stm = wb.ap[:, woff + 384:woff + 512]
            qT = Av[:, h, tt * 128:(tt + 1) * 128]
            kT = Av[:, 8 + h, tt * 128:(tt + 1) * 128]
            self.o("dve", "tensor_scalar", out=vp[:, 0:128], in0=vt.ap[:, h * 128:(h + 1) * 128], scalar1=Gs[:, col:col + 1],
                   scalar2=None, op0=ALU.mult, reads=vt.all() + gk, writes=wkf)
            self.o("dve", "tensor_copy", vp[:, 128:129], Gs[:, col:col + 1], reads=gk, writes=wkf)
            b1 = self.bank()
            self.o("pe", "transpose", bc16(b1.ap)[:, 0:128], kT, self.identb, reads=self.A.k(8 + h) + self.cst.all(), writes=b1.all())
            self.o("act", "activation", ktm, bc16(b1.ap)[:, 0:128], AF.Copy, reads=b1.all(), writes=wkf)
            b2 = self.bank()
            self.o("pe", "matmul", b2.ap[:, 0:128], kT, qT, start=True, stop=True, reads=self.A.k(h, 8 + h), writes=b2.all())
            self.o("dve", "tensor_tensor", out=stm, in0=b2.ap[:, 0:128], in1=self.maskUTb, op=ALU.mult,
                   reads=b2.all() + self.cst.all(), writes=wkf)
            if tt == 0 and Wf is not None:
                self.o("dve", "tensor_copy", stm[0:1, 0:1], s0[:, h:h + 1], reads=self.sm.k("s0"), writes=wkf)
            b3 = self.bank()
            self.o("pe", "matmul", b3.ap[:, 0:129], stm, vp, start=True, stop=(tt == 0), reads=wkf, writes=b3.all())
            if tt > 0:
                self.o("pe", "matmul", b3.ap[:, 0:129], qT, Cbv[:, h, :], start=False, stop=True,
                       reads=self.A.k(h) + Cb.k(h), writes=b3.all())
            if tt < NT - 1:
                b4 = self.bank()
                self.o("pe", "matmul", b4.ap[:, 0:129], ktm, vp, start=True, stop=True, reads=wkf, writes=b4.all())
                if tt == 0:
                    self.o("dve", "tensor_scalar", out=Csv[:, h, :], in0=b4.ap[:, 0:129], scalar1=EL[:, col:col + 1], scalar2=None,
                           op0=ALU.mult, reads=b4.all() + gk, writes=Cst.k(h))
                else:
                    self.o("dve", "tensor_scalar", out=Csv[:, h, :], in0=Csv[:, h, :], scalar1=EL[:, col:col + 1], scalar2=None,
                           op0=ALU.mult, reads=gk + Cst.k(h), writes=Cst.k(h))
                    self.o("dve", "scalar_tensor_tensor", out=Csv[:, h, :], in0=b4.ap[:, 0:129], scalar=EL[:, col:col + 1],
                           in1=Csv[:, h, :], op0=ALU.mult, op1=ALU.add, reads=b4.all() + gk + Cst.k(h), writes=Cst.k(h))
                self.o("dve", "tensor_copy", Cbv[:, h, :], Csv[:, h, :], reads=Cst.k(h), writes=Cb.k(h))
            X = self.F[tt % 2]
            self.o("act", "activation", X.ap[:, h * 129:(h + 1) * 129], b3.ap[:, 0:129], AF.Copy, reads=b3.all(), writes=X.k("x%d" % h))

        def backT(tt):
            gk = self.sm.k("G")
            X = self.F[tt % 2]
            xk = X.k(*["x%d" % h for h in range(8)])
            Xv = X.ap[:, 0:8 * 129].rearrange("p (h c) -> p h c", h=8)
            HM = self.F[2]
            SQ = self.F[3]
            HMv = HM.ap[:, 0:1024].rearrange("p (h d) -> p h d", h=8)
            SQv = SQ.ap[:, 0:1024].rearrange("p (h d) -> p h d", h=8)
            og = self.H[2 + tt % 2]
            stk = self.sm.k("stT")
            d1 = ST[:, 1024:1032]
            r2 = ST[:, 1032:1040]
            ssq = ST[:, 1040:1048]
            rstd = ST[:, 1048:1056]
            Ett = Et[:, tt * 8:(tt + 1) * 8]
            bc8 = lambda a: a.rearrange("p (h o) -> p h o", o=1).broadcast_to([128, 8, 128])
            self.o("dve", "tensor_tensor", out=d1, in0=Xv[:, :, 128], in1=Ett, op=ALU.mult, reads=xk + gk, writes=stk)
            self.o("act", "activation", d1, d1, AF.Abs, reads=stk, writes=stk)
            self.o("dve", "tensor_scalar", out=d1, in0=d1, scalar1=1.0, scalar2=None, op0=ALU.max, reads=stk, writes=stk)
            self.o("dve", "reciprocal", d1, d1, reads=stk, writes=stk)
            self.o("dve", "tensor_tensor", out=r2, in0=d1, in1=Ett, op=ALU.mult, reads=stk + gk, writes=stk)
            self.o("dve", "tensor_tensor", out=HMv, in0=Xv[:, :, 0:128], in1=bc8(r2), op=ALU.mult, reads=xk + stk, writes=HM.all())
            self.o("dve", "tensor_tensor", out=SQv, in0=HMv, in1=HMv, op=ALU.mult, reads=HM.all(), writes=SQ.all())
            self.o("dve", "tensor_reduce", out=ssq, in_=SQv, axis=AX.X, op=ALU.add, reads=SQ.all(), writes=stk)
            self.o("act", "activation", rstd, ssq, AF.Sqrt, scale=1.0 / HD, bias=EPS, reads=stk, writes=stk)
            self.o("dve", "reciprocal", rstd, rstd, reads=stk, writes=stk)
            self.o("dve", "tensor_tensor", out=HMv, in0=HMv, in1=bc8(rstd), op=ALU.mult, reads=HM.all() + stk, writes=HM.all())
            self.o("dve", "tensor_tensor", out=HM.ap[:, 0:1024], in0=HM.ap[:, 0:1024], in1=self.hn.ap[:, 0:1024], op=ALU.mult,
                   reads=HM.all() + self.hn.all(), writes=HM.all())
            o2 = bc16(SQ.ap)[:, 0:1024]
            self.o("dve", "tensor_tensor", out=o2, in0=HM.ap[:, 0:1024], in1=sigt.ap[:, (tt % 2) * 1024:(tt % 2 + 1) * 1024], op=ALU.mult,
                   reads=HM.all() + sigt.k(tt % 2), writes=SQ.all())
            b5 = self.bank()
            for h in range(8):
                self.o("pe", "transpose", bc16(b5.ap)[:, h * 128:(h + 1) * 128], o2[:, h * 128:(h + 1) * 128], self.identb,
                       reads=SQ.all() + self.cst.all(), writes=b5.all())
            self.o("act", "activation", og.ap[:, 0:1024], bc16(b5.ap)[:, 0:1024], AF.Copy, reads=b5.all(), writes=og.all())
            sc.dma("pool", mixT.ap[0:1024, tt * 128:(tt + 1) * 128].rearrange("(h p) t -> p h t", p=128),
                   og.ap[:, 0:1024].rearrange("p (h t) -> p h t", h=8), reads=og.all(), writes=mixT.k(*range(8)))

        for tt in range(NT):
            for h in range(8):
                front((tt, h))
            if tt > 0:
                backT(tt - 1)
        backT(NT - 1)

        self.stick_breaking(sbT, tmv, mixT)

        for c in range(16):
            sc.dma(self.q(), Av[:, c, :], mixT.ap[c * 128:(c + 1) * 128, :], reads=mixT.k(c), writes=self.A.k(c))
        self.gemm_fm(Wout, [[g * 512 + j * 128 for j in range(4)] for g in range(4)], 16, rhs_fn, rhs_reads, self.resid_evac(hT))

    def stick_breaking(self, sbT, tmv, mixT):
        sc = self.sc
        qT = self.H[0]
        kT = self.H[1]
        og = self.F[4]
        ogb = bc16(og.ap)[:, 0:2048]
        aT = self.F[5]
        aTb = bc16(aT.ap)[:, 0:2048]
        negm = self.cst.ap[:, 768:896]

        def vvh(h):
            return self.H[2] if h % 2 == 0 else self.H[5]

        def A1(it):
            h, tt = it
            if tt == 0:
                vv = vvh(h)
                sc.dma("sp", qT.ap[:, :], sbT.ap[h * 128:(h + 1) * 128, :], reads=sbT.k(h), writes=qT.all())
                sc.dma("sp", kT.ap[:, :], sbT.ap[(8 + h) * 128:(9 + h) * 128, :], reads=sbT.k(8 + h), writes=kT.all())
                sc.dma("pool", vv.ap.rearrange("p (t d) -> p t d", t=16),
                       tmv.ap[:, 2048 + h * 128:2048 + (h + 1) * 128].rearrange("(t p) d -> p t d", p=128),
                       reads=tmv.k(*range(16)), writes=vv.all())
            W = (tt + 1) * 128
            nb = (W + 511) // 512
            sp = self.F[tt % 2]
            w2 = self.F[2 + tt % 2]
            zbs = []
            for j in range(nb):
                wj = min(512, W - j * 512)
                zb = self.bank()
                zbs.append((j, wj, zb))
                last = (j == nb - 1)
                self.o("pe", "matmul", zb.ap[:, 0:wj], qT.ap[:, tt * 128:(tt + 1) * 128], kT.ap[:, j * 512:j * 512 + wj],
                       start=True, stop=not last, reads=qT.all() + kT.all(), writes=zb.all())
                if last:
                    self.o("pe", "matmul", zb.ap[:, wj - 128:wj], self.identb, negm, start=False, stop=True,
                           reads=self.cst.all(), writes=zb.all())
            for (j, wj, zb) in zbs:
                self.o("act", "activation", sp.ap[:, j * 512:j * 512 + wj], zb.ap[:, 0:wj], AF.Exp, reads=zb.all(), writes=sp.k(j))
            for (j, wj, zb) in zbs:
                self.o("act", "activation", sp.ap[:, j * 512:j * 512 + wj], sp.ap[:, j * 512:j * 512 + wj], AF.Ln, bias=1.0,
                       reads=sp.k(j), writes=sp.k(j))
            for (j, wj, zb) in zbs:
                self.o("dve", "tensor_tensor", out=w2.ap[:, j * 512:j * 512 + wj], in0=zb.ap[:, 0:wj], in1=sp.ap[:, j * 512:j * 512 + wj],
                       op=ALU.subtract, reads=zb.all() + sp.k(j), writes=w2.k(j))

        def A2(it):
            h, tt = it
            W = (tt + 1) * 128
            sp = self.F[tt % 2]
            w2 = self.F[2 + tt % 2]
            ab = self.H[3 + tt % 2]
            nb = (W + 511) // 512
            spk = sp.k(*range(nb))
            w2k = w2.k(*range(nb))
            self.o("dve", "tensor_tensor_scan", out=sp.ap[:, 0:W], data0=self.ones.ap[:, 0:W], data1=sp.ap[:, 0:W], initial=0.0,
                   op0=ALU.mult, op1=ALU.add, reads=spk + self.ones.all(), writes=spk)
            self.o("dve", "tensor_tensor", out=w2.ap[:, 0:W], in0=w2.ap[:, 0:W], in1=sp.ap[:, 0:W], op=ALU.add,
                   reads=spk + w2k, writes=w2k)
            nt_ = self.sm.ap[:, 1992 + tt % 2:1993 + tt % 2]
            self.o("dve", "tensor_scalar", out=nt_, in0=sp.ap[:, W - 1:W], scalar1=-1.0, scalar2=None, op0=ALU.mult,
                   reads=spk, writes=self.sm.k("nt%d" % (tt % 2)))

        def A3(it):
            h, tt = it
            W = (tt + 1) * 128
            nb = (W + 511) // 512
            w2 = self.F[2 + tt % 2]
            ab = self.H[3 + tt % 2]
            nt_ = self.sm.ap[:, 1992 + tt % 2:1993 + tt % 2]
            self.o("act", "activation", ab.ap[:, 0:W], w2.ap[:, 0:W], AF.Exp, bias=nt_,
                   reads=w2.k(*range(nb)) + self.sm.k("nt%d" % (tt % 2)), writes=ab.all())

        def B(it):
            h, tt = it
            vv = vvh(h)
            vvv = vv.ap.rearrange("p (t d) -> p t d", t=16)
            ab = self.H[3 + tt % 2]
            for g in range((tt + 8) // 8):
                n = min(8, tt + 1 - g * 8)
                b = self.bank()
                for j in range(n):
                    kt = g * 8 + j
                    self.o("pe", "transpose", bc16(b.ap)[:, j * 128:(j + 1) * 128], ab.ap[:, kt * 128:(kt + 1) * 128], self.identb,
                           reads=ab.all() + self.cst.all(), writes=b.all())
                if g % 2:
                    self.o("act", "activation", aTb[:, g * 1024:g * 1024 + n * 128], bc16(b.ap)[:, 0:n * 128], AF.Copy,
                           reads=b.all(), writes=aT.all())
                else:
                    self.o("dve", "tensor_copy", aTb[:, g * 1024:g * 1024 + n * 128], bc16(b.ap)[:, 0:n * 128],
                           reads=b.all(), writes=aT.all())
            bo = self.bank()
            for kt in range(tt + 1):
                self.o("pe", "matmul", bo.ap[:, 0:128], vvv[:, kt, :], aTb[:, kt * 128:(kt + 1) * 128], start=(kt == 0), stop=(kt == tt),
                       reads=vv.all() + aT.all(), writes=bo.all())
            self.o("act", "activation", ogb[:, tt * 128:(tt + 1) * 128], bo.ap[:, 0:128], AF.Copy, reads=bo.all(), writes=og.all())
            if tt == NT - 1:
                sc.dma("sp", mixT.ap[(8 + h) * 128:(9 + h) * 128, :], ogb, reads=og.all(), writes=mixT.k(8 + h))

        its = [(h, tt) for h in range(8) for tt in range(NT)]
        n = len(its)
        for k in range(-2, n + 1):
            if 0 <= k < n:
                A3(its[k])
            if 0 <= k + 1 < n:
                A2(its[k + 1])
            if 0 <= k + 2 < n:
                A1(its[k + 2])
            if 0 <= k - 1 < n:
                B(its[k - 1])

    def setup_l1(self, ext):
        sc = self.sc
        self.c1 = self.sb([128, 1536], F32, "c1")
        self.Ef = self.sb([32, 2048], BF16, "Ef")
        sc.dma("sp", self.c1.ap[:, :], ext["c1"].ap[:, :], reads=ext["c1"].all(), writes=self.c1.all())
        sc.dma("sp", self.Ef.ap[:, :], ext["Ef"].ap[:, :], reads=ext["Ef"].all(), writes=self.Ef.all())
        self.maskBD = self.cst.ap[:, 512:640]
        self.maskFar = self.cst.ap[:, 640:768]

    def mixer_cd(self, hT, Win, Wout, W1b, W2b, fmT, tmv, mixT, ext, Wf=None, grow=None):
        sc = self.sc
        SC = HD ** -0.5
        self.norm_to_A(hT, 48, tok0=Wf is not None)
        rhs_fn, rhs_reads = self.A_rhs()
        lhs_fn, lhs_reads = self.A_lhs()
        cnt = [0]
        c1 = self.c1
        ropeQ = c1.ap[:, 0:512].rearrange("p (t c) -> p t c", t=16)
        G1 = self.sm.ap[:, 0:384].rearrange("p (t c) -> p t c", t=16)
        PS = self.sm.ap[:, 384:896].rearrange("p (t c) -> p t c", t=16)
        SEL = self.sm.ap[:, 896:1408].rearrange("p (t c) -> p t c", t=16)
        sm = self.sm

        def evac_fm(c0, tc, b):
            if c0 < 2584:
                c = (c0 - 1024) // 128
            elif c0 < 5656:
                c = 4 + (c0 - 2584) // 128
            else:
                c = 20 + (c0 - 5656) // 128
            ob = self.H[c % 2]
            if 12 <= c < 20:
                self.o("act", "activation", ob.ap[:, tc * 512:(tc + 1) * 512], b.ap[:, :], AF.Sigmoid, reads=b.all(), writes=ob.all())
            elif c >= 20:
                self.o("act", "activation", ob.ap[:, tc * 512:(tc + 1) * 512], b.ap[:, :], AF.Silu, reads=b.all(), writes=ob.all())
            else:
                self.o("dve", "tensor_copy", ob.ap[:, tc * 512:(tc + 1) * 512], b.ap[:, :], reads=b.all(), writes=ob.all())
            if tc == 3:
                sc.dma("sp", fmT.ap[c * 128:(c + 1) * 128, :], ob.ap[:, :], reads=ob.all(), writes=fmT.k(c))

        cols = [1024 + j * 128 for j in range(4)] + [2584 + j * 128 for j in range(16)] + [5656 + j * 128 for j in range(8)]
        self.gemm_fm(Win, [cols[i:i + 4] for i in range(0, 28, 4)], 16, rhs_fn, rhs_reads, evac_fm)

        def mk_evac(dst0, nrope):
            def evac(tt, b):
                i = cnt[0]
                cnt[0] += 1
                ob = self.H[4 + i % 2]
                if nrope == 0:
                    self.o("act", "activation", ob.ap[:, 0:512], b.ap[:, :], AF.Copy, reads=b.all(), writes=ob.all())
                else:
                    tk = self.F[i % 2].all()
                    bv = b.ap[:, 0:nrope * 128].rearrange("p (h d) -> p h d", h=nrope)
                    ov = ob.ap[:, 0:nrope * 128].rearrange("p (h d) -> p h d", h=nrope)
                    tmp = self.F[i % 2].ap[:, 0:256].rearrange("p (a h c) -> p a h c", a=4, h=4)
                    cos = ropeQ[:, tt, 0:16].rearrange("p (o c) -> p o c", o=1).broadcast_to([128, nrope, 16])
                    sin = ropeQ[:, tt, 16:32].rearrange("p (o c) -> p o c", o=1).broadcast_to([128, nrope, 16])
                    self.o("act", "activation", ob.ap[:, 0:512], b.ap[:, :], AF.Copy, reads=b.all(), writes=ob.all())
                    self.o("dve", "tensor_tensor", out=tmp[:, 0, 0:nrope, :], in0=bv[:, :, 0:16], in1=cos, op=ALU.mult, reads=b.all() + c1.all(), writes=tk)
                    self.o("dve", "tensor_tensor", out=tmp[:, 1, 0:nrope, :], in0=bv[:, :, 16:32], in1=sin, op=ALU.mult, reads=b.all() + c1.all(), writes=tk)
                    self.o("dve", "tensor_tensor", out=tmp[:, 2, 0:nrope, :], in0=bv[:, :, 16:32], in1=cos, op=ALU.mult, reads=b.all() + c1.all(), writes=tk)
                    self.o("dve", "tensor_tensor", out=tmp[:, 3, 0:nrope, :], in0=bv[:, :, 0:16], in1=sin, op=ALU.mult, reads=b.all() + c1.all(), writes=tk)
                    self.o("dve", "tensor_tensor", out=ov[:, :, 0:16], in0=tmp[:, 0, 0:nrope, :], in1=tmp[:, 1, 0:nrope, :], op=ALU.subtract,
                           reads=tk + ob.all(), writes=ob.all())
                    self.o("dve", "tensor_tensor", out=ov[:, :, 16:32], in0=tmp[:, 2, 0:nrope, :], in1=tmp[:, 3, 0:nrope, :], op=ALU.add,
                           reads=tk + ob.all(), writes=ob.all())
                sc.dma("pool", tmv.ap[tt * 128:(tt + 1) * 128, dst0:dst0 + 512], ob.ap[:, 0:512], reads=ob.all(), writes=tmv.k(tt))
            return evac

        jobs = [(c0, 512, mk_evac(dst0, nr)) for (c0, dst0, nr) in
                ((0, 0, 4), (512, 512, 4), (1536, 1024, 2), (2048, 1536, 2), (4632, 2048, 0), (5144, 2560, 0))]

        def evac_g(tt, b):
            self.o("act", "activation", G1[:, tt, :], b.ap[:, 0:24], AF.Sigmoid, reads=b.all(), writes=sm.all())
        jobs.append((2560, 24, evac_g))
        self.gemm_tm_multi(Win, jobs, 16, lhs_fn, lhs_reads)
        Av = self.A.ap.rearrange("p (c t) -> p c t", c=16)
        for tt in range(NT):
            xt = self.H[tt % 2]
            sc.dma("sp", xt.ap[:, 0:2048], tmv.ap[tt * 128:(tt + 1) * 128, 0:2048], reads=tmv.k(tt), writes=xt.all())
            srcs = [(h, h * 128) for h in range(8)] + [(8, 1024), (9, 1152), (10, 1536), (11, 1664)]
            for gi in range(0, 12, 4):
                b = self.bank()
                for j in range(4):
                    slot, off = srcs[gi + j]
                    self.o("pe", "transpose", bc16(b.ap)[:, j * 128:(j + 1) * 128], xt.ap[:, off:off + 128], self.identb,
                           reads=xt.all() + self.cst.all(), writes=b.all())
                s0_ = srcs[gi][0]
                dst = Av[:, s0_:s0_ + 4, tt * 128:(tt + 1) * 128]
                srcv = bc16(b.ap)[:, 0:512].rearrange("p (j t) -> p j t", j=4)
                if (gi // 4 + tt) % 2:
                    self.o("act", "activation", dst, srcv, AF.Copy, reads=b.all(), writes=self.A.k(*range(s0_, s0_ + 4)))
                else:
                    self.o("dve", "tensor_copy", dst, srcv, reads=b.all(), writes=self.A.k(*range(s0_, s0_ + 4)))

        self.run_deferred(2)
        visT = bc16(self.hn.ap)[:, 0:2048]
        sc.dma("sp", self.hn.ap[:, :], ext["vis"].ap[:, :], reads=ext["vis"].all(), writes=self.hn.all())
        for g in range(2):
            kcT = self.H[4].ap[:, 0:128]
            vcA = self.H[4].ap[:, 256:256 + 161]
            sc.dma("sp", vcA[0:127, 129:161], ext["cover"].ap[:, :], reads=ext["cover"].all(), writes=self.H[4].all())
            for kv in range(2):
                xs = self.H[0]
                sc.dma("sp", xs.ap[:, :], fmT.ap[(kv * 2 + g) * 128:(kv * 2 + g + 1) * 128, :], reads=fmT.k(kv * 2 + g), writes=xs.all())
                w1 = self.next_slab()
                w1v = w1.ap[:, 0:4096].rearrange("p (l e) -> p l e", l=32)
                for lq in range(4):
                    sc.dma(self.q(), w1v[:, lq * 8:(lq + 1) * 8, :],
                           W1b.ap[kv * 4096 + lq * 1024:kv * 4096 + (lq + 1) * 1024, :].rearrange("(l d) e -> d l e", d=128),
                           reads=W1b.all(), writes=w1.all())
                w2 = w1.ap[:, 4096:4224]
                sc.dma("sp", w2, W2b.ap[kv * 128:(kv + 1) * 128, :], reads=W2b.all(), writes=w1.all())
                posb = w1.ap[:, 4224:4256]
                self.o("dve", "tensor_copy", posb, self.vec.ap[:, 544 + kv * 32:544 + (kv + 1) * 32], reads=self.vec.all(), writes=w1.all())
                bh = self.bank()
                bcn = self.bank()
                xv = xs.ap[:, 0:2048].rearrange("p (n s) -> p n s", s=16)
                for l in range(32):
                    rhs = xv[:, l // 16:l // 16 + 127, l % 16]
                    self.o("pe", "matmul", bh.ap[:, 0:127], w1v[:, l, :], rhs, start=(l == 0), stop=(l == 31), reads=w1.all() + xs.all(), writes=bh.all())
                for l in range(32):
                    self.o("pe", "matmul", bcn.ap[:, 0:1], w1v[:, l, :], posb[:, l:l + 1], start=(l == 0), stop=(l == 31), reads=w1.all(), writes=bcn.all())
                wk = self.F[kv]
                cc = wk.ap[:, 1024:1025]
                xx = wk.ap[:, 0:127]
                x2 = wk.ap[:, 128:255]
                hid = self.H[1].ap[:, kv * 128:kv * 128 + 127]
                self.o("dve", "tensor_copy", cc, bcn.ap[:, 0:1], reads=bcn.all(), writes=wk.all())
                self.o("act", "activation", xx, bh.ap[:, 0:127], AF.Identity, bias=cc, reads=bh.all() + wk.all(), writes=wk.all())
                self.o("dve", "tensor_tensor", out=x2, in0=xx, in1=xx, op=ALU.mult, reads=wk.all(), writes=wk.all())
                self.o("dve", "tensor_scalar", out=x2, in0=x2, scalar1=0.044715, scalar2=1.0, op0=ALU.mult, op1=ALU.add, reads=wk.all(), writes=wk.all())
                self.o("dve", "tensor_tensor", out=x2, in0=x2, in1=xx, op=ALU.mult, reads=wk.all(), writes=wk.all())
                self.o("act", "activation", x2, x2, AF.Sigmoid, scale=1.5957691216, reads=wk.all(), writes=wk.all())
                self.o("dve", "tensor_tensor", out=hid, in0=x2, in1=xx, op=ALU.mult, reads=wk.all(), writes=self.H[1].all())
                bo = self.bank()
                self.o("pe", "matmul", bo.ap[0:127, 0:128], hid, w2, start=True, stop=True, reads=self.H[1].all() + w1.all(), writes=bo.all())
                if kv == 1:
                    self.o("act", "activation", vcA[0:127, 0:128], bo.ap[0:127, 0:128], AF.Copy, reads=bo.all(), writes=self.H[4].all())
                    self.o("pool", "memset", vcA[0:127, 128:129], 1.0, writes=self.H[4].all())
                else:
                    kk = wk.ap[:, 256:384]
                    tmp = wk.ap[:, 384:448]
                    cos = c1.ap[0:127, 512:528]
                    sin = c1.ap[0:127, 528:544]
                    self.o("act", "activation", kk[0:127, :], bo.ap[0:127, 0:128], AF.Copy, reads=bo.all(), writes=wk.all())
                    self.o("dve", "tensor_tensor", out=tmp[0:127, 0:16], in0=bo.ap[0:127, 0:16], in1=cos, op=ALU.mult, reads=bo.all() + c1.all(), writes=wk.all())
                    self.o("dve", "tensor_tensor", out=tmp[0:127, 16:32], in0=bo.ap[0:127, 16:32], in1=sin, op=ALU.mult, reads=bo.all() + c1.all(), writes=wk.all())
                    self.o("dve", "tensor_tensor", out=tmp[0:127, 32:48], in0=bo.ap[0:127, 16:32], in1=cos, op=ALU.mult, reads=bo.all() + c1.all(), writes=wk.all())
                    self.o("dve", "tensor_tensor", out=tmp[0:127, 48:64], in0=bo.ap[0:127, 0:16], in1=sin, op=ALU.mult, reads=bo.all() + c1.all(), writes=wk.all())
                    self.o("dve", "tensor_tensor", out=kk[0:127, 0:16], in0=tmp[0:127, 0:16], in1=tmp[0:127, 16:32], op=ALU.subtract, reads=wk.all(), writes=wk.all())
                    self.o("dve", "tensor_tensor", out=kk[0:127, 16:32], in0=tmp[0:127, 32:48], in1=tmp[0:127, 48:64], op=ALU.add, reads=wk.all(), writes=wk.all())
                    kkb = self.H[1].ap[:, 512:640]
                    self.o("pool", "memset", kkb, 0.0, writes=self.H[1].all())
                    self.o("dve", "tensor_copy", kkb[0:127, :], kk[0:127, :], reads=wk.all(), writes=self.H[1].all())
                    bt = self.bank()
                    self.o("pe", "transpose", bc16(bt.ap)[:, 0:128], kkb, self.identb, reads=self.H[1].all() + self.cst.all(), writes=bt.all())
                    self.o("act", "activation", kcT, bc16(bt.ap)[:, 0:128], AF.Copy, reads=bt.all(), writes=self.H[4].all())

            vsA = bc16(self.F[4].ap)[:, 0:2064].rearrange("p (t c) -> p t c", t=16)
            vwA = bc16(self.F[5].ap)[:, 0:2064].rearrange("p (t c) -> p t c", t=16)
            for (va, Ft, off) in ((vsA, self.F[4], 1280), (vwA, self.F[5], 1792)):
                self.o("pool", "memset", va[:, :, 128:129], 1.0, writes=Ft.all())
                sc.dma("pool", va[:, :, 0:128], tmv.ap[:, off + g * 128:off + (g + 1) * 128].rearrange("(t p) d -> p t d", p=128),
                       reads=tmv.k(*range(16)), writes=Ft.all())

            cmpo = self.slab[0]
            cmv = cmpo.ap.rearrange("p (h t d) -> p h t d", h=4, t=16)
            for r in range(4):
                h = g * 4 + r
                for tc in range(4):
                    bs = self.bank()
                    self.o("pe", "matmul", bs.ap[0:127, :], kcT[:, 0:127], Av[:, h, tc * 512:(tc + 1) * 512], start=True, stop=True,
                           reads=self.H[4].all() + self.A.k(h), writes=bs.all())
                    ee = self.F[tc % 2]
                    em = self.H[2 + tc % 2]
                    self.o("act", "activation", ee.ap[0:127, 0:512], bs.ap[0:127, :], AF.Exp, scale=SC, reads=bs.all(), writes=ee.all())
                    self.o("dve", "tensor_tensor", out=em.ap[0:127, 0:512], in0=ee.ap[0:127, 0:512], in1=visT[0:127, tc * 512:(tc + 1) * 512],
                           op=ALU.mult, reads=ee.all() + self.hn.all(), writes=em.all())
                    for j in range(4):
                        tt = tc * 4 + j
                        bo = self.bank()
                        self.o("pe", "matmul", bo.ap[:, 0:161], em.ap[0:127, j * 128:(j + 1) * 128], vcA[0:127, :], start=True, stop=True,
                               reads=em.all() + self.H[4].all(), writes=bo.all())
                        i = cnt[0]
                        cnt[0] += 1
                        so = 1408 + (i % 4) * 4
                        stt = sm.k("c%d" % (i % 4))
                        rd = sm.ap[:, so:so + 1]
                        rg = sm.ap[:, so + 1:so + 2]
                        self.o("dve", "tensor_scalar", out=rd, in0=bo.ap[:, 128:129], scalar1=1e-30, scalar2=None, op0=ALU.max, reads=bo.all(), writes=stt)
                        self.o("dve", "reciprocal", rd, rd, reads=stt, writes=stt)
                        self.o("dve", "tensor_tensor", out=rg, in0=rd, in1=G1[:, tt, h * 3:h * 3 + 1], op=ALU.mult, reads=stt + sm.all(), writes=stt)
                        self.o("act", "activation", cmv[:, r, tt, :], bo.ap[:, 0:128], AF.Copy, scale=rg, reads=bo.all() + stt, writes=cmpo.all())
                        if r == 0:
                            self.o("dve", "tensor_scalar", out=PS[:, tt, :], in0=bo.ap[:, 129:161], scalar1=rd, scalar2=None, op0=ALU.mult,
                                   reads=bo.all() + stt, writes=sm.k("ps"))
                        else:
                            self.o("dve", "scalar_tensor_tensor", out=PS[:, tt, :], in0=bo.ap[:, 129:161], scalar=rd, in1=PS[:, tt, :],
                                   op0=ALU.mult, op1=ALU.add, reads=bo.all() + stt + sm.k("ps"), writes=sm.k("ps"))
            sbias = c1.ap[:, 544:1056].rearrange("p (t c) -> p t c", t=16)
            selT = self.H[5]
            top8 = sm.ap[:, 1440:1448]
            for tt in range(NT):
                self.o("dve", "tensor_tensor", out=PS[:, tt, :], in0=PS[:, tt, :], in1=sbias[:, tt, :], op=ALU.add,
                       reads=sm.k("ps") + c1.all(), writes=sm.k("ps"))
                self.o("dve", "max", top8, PS[:, tt, :], reads=sm.k("ps"), writes=sm.k("t8"))
                self.o("dve", "tensor_scalar", out=SEL[:, tt, :], in0=PS[:, tt, :], scalar1=sm.ap[:, 1447:1448], scalar2=None, op0=ALU.is_ge,
                       reads=sm.k("ps", "t8"), writes=sm.k("sel"))
                selb = self.H[4].ap[:, 512:544]
                self.o("dve", "tensor_scalar", out=selb, in0=SEL[:, tt, :], scalar1=30000.0, scalar2=-30000.0, op0=ALU.mult, op1=ALU.add,
                       reads=sm.k("sel"), writes=self.H[4].k("selb"))
                bt = self.bank()
                self.o("pe", "transpose", bc16(bt.ap)[0:32, 0:128], selb, self.identb, reads=self.H[4].k("selb") + self.cst.all(), writes=bt.all())
                self.o("act", "activation", selT.ap[0:32, tt * 128:(tt + 1) * 128], bc16(bt.ap)[0:32, 0:128], AF.Copy, reads=bt.all(), writes=selT.all())

            negDiag = self.cst.ap[:, 384:512]
            negFar = self.cst.ap[:, 768:896]
            st8 = {}

            def S_(it):
                tt, r = it
                nk = tt + 1
                h = g * 4 + r
                i = tt * 4 + r
                qT = Av[:, h, tt * 128:(tt + 1) * 128]
                kts = [kt for kt in (tt - 2, tt - 1, tt) if kt >= 0]
                bw = self.bank()
                for j, kt in enumerate(kts):
                    diag = (kt == tt)
                    far = (kt == tt - 2)
                    self.o("pe", "matmul", bw.ap[:, j * 128:(j + 1) * 128], Av[:, 10 + g, kt * 128:(kt + 1) * 128], qT, start=True,
                           stop=not (diag or far), reads=self.A.k(10 + g, h), writes=bw.all())
                    if diag or far:
                        self.o("pe", "matmul", bw.ap[:, j * 128:(j + 1) * 128], self.identb, negDiag if diag else negFar, start=False, stop=True,
                               reads=self.cst.all(), writes=bw.all())
                ew = self.H[4]
                ewk = ew.k("ew%d" % (i % 2))
                ewo = 1024 + (i % 2) * 384
                nw = len(kts) * 128
                esT = self.H[i % 2]
                sb_ = []
                for gb in range((nk + 3) // 4):
                    n = min(4, nk - gb * 4)
                    bsx = self.bank()
                    sb_.append((gb, n, bsx))
                    for j in range(n):
                        kt = gb * 4 + j
                        self.o("pe", "matmul", bsx.ap[:, j * 128:(j + 1) * 128], Av[:, 8 + g, kt * 128:(kt + 1) * 128], qT, start=True, stop=False,
                               reads=self.A.k(8 + g, h), writes=bsx.all())
                        self.o("pe", "matmul", bsx.ap[:, j * 128:(j + 1) * 128], self.Ef.ap[0:32, kt * 128:(kt + 1) * 128],
                               selT.ap[0:32, tt * 128:(tt + 1) * 128], start=False, stop=(kt != tt), reads=self.Ef.all() + selT.all(), writes=bsx.all())
                        if kt == tt:
                            self.o("pe", "matmul", bsx.ap[:, j * 128:(j + 1) * 128], self.identb, negDiag, start=False, stop=True,
                                   reads=self.cst.all(), writes=bsx.all())
                self.o("act", "activation", ew.ap[:, ewo:ewo + nw], bw.ap[:, 0:nw], AF.Exp, scale=SC, reads=bw.all(), writes=ewk)
                for (gb, n, bsx) in sb_:
                    self.o("act", "activation", esT.ap[:, gb * 512:gb * 512 + n * 128], bsx.ap[:, 0:n * 128], AF.Exp, scale=SC,
                           reads=bsx.all(), writes=esT.k(gb))

            def P_(it):
                tt, r = it
                nk = tt + 1
                i = tt * 4 + r
                kts = [kt for kt in (tt - 2, tt - 1, tt) if kt >= 0]
                ew = self.H[4]
                ewk = ew.k("ew%d" % (i % 2))
                ewo = 1024 + (i % 2) * 384
                esT = self.H[i % 2]
                bout = self.bank()
                for j, kt in enumerate(kts):
                    self.o("pe", "matmul", bout.ap[:, 0:129], ew.ap[:, ewo + j * 128:ewo + (j + 1) * 128], vwA[:, kt, :],
                           start=(j == 0), stop=(j == len(kts) - 1), reads=ewk + self.F[5].all(), writes=bout.all())
                for kt in range(nk):
                    self.o("pe", "matmul", bout.ap[:, 160:289], esT.ap[:, kt * 128:(kt + 1) * 128], vsA[:, kt, :], start=(kt == 0), stop=(kt == nk - 1),
                           reads=esT.k(kt // 4) + self.F[4].all(), writes=bout.all())
                st8[it] = bout

            def C_(it):
                tt, r = it
                h = g * 4 + r
                i = tt * 4 + r
                bout = st8.pop(it)
                og = self.H[2 + tt % 2]
                ogv = og.ap[:, 0:512].rearrange("p (h t) -> p h t", h=4)
                so = 1456 + (i % 4) * 4
                stt = sm.k("d%d" % (i % 4))
                rs = sm.ap[:, so:so + 1]
                rw = sm.ap[:, so + 1:so + 2]
                self.o("dve", "reciprocal", rs, bout.ap[:, 288:289], reads=bout.all(), writes=stt)
                self.o("dve", "tensor_tensor", out=rs, in0=rs, in1=G1[:, tt, h * 3 + 1:h * 3 + 2], op=ALU.mult, reads=stt + sm.all(), writes=stt)
                self.o("dve", "reciprocal", rw, bout.ap[:, 128:129], reads=bout.all(), writes=stt)
                self.o("dve", "tensor_tensor", out=rw, in0=rw, in1=G1[:, tt, h * 3 + 2:h * 3 + 3], op=ALU.mult, reads=stt + sm.all(), writes=stt)
                acck = self.H[4].k("acc%d" % (i % 2))
                acc = self.H[4].ap[:, 1792 + (i % 2) * 128:1792 + (i % 2) * 128 + 128]
                accf = self.sm.ap[:, 1536 + (i % 2) * 128:1536 + (i % 2) * 128 + 128]
                self.o("dve", "scalar_tensor_tensor", out=accf, in0=bout.ap[:, 160:288], scalar=rs, in1=cmv[:, r, tt, :], op0=ALU.mult, op1=ALU.add,
                       reads=bout.all() + stt + cmpo.all(), writes=sm.k("acc%d" % (i % 2)))
                self.o("dve", "scalar_tensor_tensor", out=acc, in0=bout.ap[:, 0:128], scalar=rw, in1=accf, op0=ALU.mult, op1=ALU.add,
                       reads=bout.all() + stt + sm.k("acc%d" % (i % 2)), writes=acck)
                bt = self.bank()
                self.o("pe", "transpose", bc16(bt.ap)[:, 0:128], acc, self.identb, reads=acck + self.cst.all(), writes=bt.all())
                self.o("act", "activation", ogv[:, r, :], bc16(bt.ap)[:, 0:128], AF.Copy, reads=bt.all(), writes=og.all())
                if r == 3:
                    sc.dma("pool", mixT.ap[g * 512:(g + 1) * 512, tt * 128:(tt + 1) * 128].rearrange("(h p) t -> p h t", p=128), ogv,
                           reads=og.all(), writes=mixT.k(*range(g * 4, g * 4 + 4)))

            its = [(tt, r) for tt in range(NT) for r in range(4)]
            S_(its[0])
            for k, it in enumerate(its):
                P_(it)
                if k + 1 < len(its):
                    S_(its[k + 1])
                C_(it)

        s0 = None
        if Wf is not None:
            s0 = self.sm.ap[0:1, 1984:1992]
            r0 = self.F[0]
            r1 = self.F[1]
            self.tok0_proj(Wf, [2584, 3096, 3608, 4120], r0)
            sc.dma("sp", r1.ap[0:1, 0:2048], grow.ap[0:1, 0:2048], reads=grow.all(), writes=r1.all())
            self.o("dve", "tensor_tensor", out=r1.ap[0:1, 0:1024], in0=r1.ap[0:1, 1024:2048], in1=r1.ap[0:1, 0:1024], op=ALU.subtract,
                   reads=r1.all(), writes=r1.all())
            self.o("act", "activation", r1.ap[0:1, 0:1024], r1.ap[0:1, 0:1024], AF.Sigmoid, reads=r1.all(), writes=r1.all())
            self.o("dve", "tensor_scalar", out=r1.ap[0:1, 0:1024], in0=r1.ap[0:1, 0:1024], scalar1=-1.0, scalar2=1.0, op0=ALU.mult, op1=ALU.add,
                   reads=r1.all(), writes=r1.all())
            self.o("act", "activation", r0.ap[0:1, 1024:2048], r0.ap[0:1, 1024:2048], AF.Sigmoid, scale=-1.0, reads=r0.all(), writes=r0.all())
            self.o("dve", "tensor_tensor", out=r0.ap[0:1, 1024:2048], in0=r0.ap[0:1, 1024:2048], in1=r1.ap[0:1, 0:1024], op=ALU.mult,
                   reads=r0.all() + r1.all(), writes=r0.all())
            self.o("dve", "tensor_tensor", out=r0.ap[0:1, 0:1024], in0=r0.ap[0:1, 0:1024], in1=r0.ap[0:1, 1024:2048], op=ALU.mult,
                   reads=r0.all(), writes=r0.all())
            self.o("dve", "tensor_reduce", out=s0, in_=r0.ap[0:1, 0:1024].rearrange("p (h d) -> p h d", h=8), axis=AX.X, op=ALU.add,
                   reads=r0.all(), writes=self.sm.k("s0h"))
        self.hgrn2(fmT, tmv, mixT, s0)

        for c in range(16):
            sc.dma(self.q(), Av[:, c, :], mixT.ap[c * 128:(c + 1) * 128, :], reads=mixT.k(c), writes=self.A.k(c))
        self.gemm_fm(Wout, [[gg * 512 + j * 128 for j in range(4)] for gg in range(4)], 16, rhs_fn, rhs_reads, self.resid_evac(hT))

    def hgrn2(self, fmT, tmv, mixT, s0=None):
        sc = self.sc
        seg = self.ones
        self.o("pool", "memset", seg.ap[:, :].rearrange("p (c l) -> p c l", l=32)[:, :, 0:1], 0.0, writes=seg.all())
        lbv = self.sm.ap[:, 1800:1808]
        omv = self.sm.ap[:, 1808:1816]
        sk = self.sm.k("hg")
        self.o("dve", "tensor_tensor", out=lbv, in0=self.vec.ap[:, 616:624], in1=self.vec.ap[:, 608:616], op=ALU.subtract, reads=self.vec.all(), writes=sk)
        self.o("act", "activation", lbv, lbv, AF.Sigmoid, reads=sk, writes=sk)
        self.o("dve", "tensor_scalar", out=omv, in0=lbv, scalar1=-1.0, scalar2=1.0, op0=ALU.mult, op1=ALU.add, reads=sk, writes=sk)
        chm = self.c1.ap[:, 1056:1060]
        for h in range(8):
            qr = self.H[0]
            sg = self.H[1]
            gs = self.H[2]
            vv = self.H[3]
            sc.dma("sp", qr.ap[:, :], fmT.ap[(4 + h) * 128:(5 + h) * 128, :], reads=fmT.k(4 + h), writes=qr.all())
            sc.dma("pool", sg.ap[:, :], fmT.ap[(12 + h) * 128:(13 + h) * 128, :], reads=fmT.k(12 + h), writes=sg.all())
            sc.dma("sp", gs.ap[:, :], fmT.ap[(20 + h) * 128:(21 + h) * 128, :], reads=fmT.k(20 + h), writes=gs.all())
            vvv = vv.ap.rearrange("p (t d) -> p t d", t=16)
            sc.dma("pool", vvv, tmv.ap[:, 2048 + h * 128:2048 + (h + 1) * 128].rearrange("(t p) d -> p t d", p=128),
                   reads=tmv.k(*range(16)), writes=vv.all())
            f = self.F[0]
            bcm = self.F[1]
            EQ = self.F[2]
            EK = self.F[3]
            N = 2048
            self.o("dve", "tensor_scalar", out=f.ap[:, 0:N], in0=sg.ap[:, :], scalar1=omv[:, h:h + 1], scalar2=lbv[:, h:h + 1], op0=ALU.mult, op1=ALU.add,
                   reads=sg.all() + sk, writes=f.all())
            self.o("act", "activation", bcm.ap[:, 0:N], f.ap[:, 0:N], AF.Ln, reads=f.all(), writes=bcm.all())
            self.o("dve", "tensor_tensor_scan", out=bcm.ap[:, 0:N], data0=seg.ap[:, 0:N], data1=bcm.ap[:, 0:N], initial=0.0, op0=ALU.mult, op1=ALU.add,
                   reads=bcm.all() + seg.all(), writes=bcm.all())
            self.o("act", "activation", EQ.ap[:, 0:N], bcm.ap[:, 0:N], AF.Exp, reads=bcm.all(), writes=EQ.all())
            self.o("act", "activation", EK.ap[:, 0:N], bcm.ap[:, 0:N], AF.Exp, scale=-1.0, reads=bcm.all(), writes=EK.all())
            self.o("dve", "tensor_scalar", out=f.ap[:, 0:N], in0=f.ap[:, 0:N], scalar1=-1.0, scalar2=1.0, op0=ALU.mult, op1=ALU.add, reads=f.all(), writes=f.all())
            qt = self.H[4]
            kt_ = self.H[5]
            self.o("dve", "tensor_tensor", out=qt.ap[:, :], in0=qr.ap[:, :], in1=EQ.ap[:, 0:N], op=ALU.mult, reads=qr.all() + EQ.all(), writes=qt.all())
            self.o("dve", "tensor_tensor", out=kt_.ap[:, :], in0=f.ap[:, 0:N], in1=EK.ap[:, 0:N], op=ALU.mult, reads=f.all() + EK.all(), writes=kt_.all())
            St = self.sm.ap[:, 1824:1952]
            skS = self.sm.k("S")
            oT = self.F[4]

            def Sb(i):
                return self.H[1].ap[:, (i % 8) * 128:(i % 8 + 1) * 128], self.H[1].k("S%d" % (i % 8))

            km = self.F[5]
            kmb = bc16(km.ap)[:, 0:512].rearrange("p (c d) -> p c d", c=4)
            am = bc16(km.ap)[:, 512:640]

            def front(tt):
                tsl = slice(tt * 128, (tt + 1) * 128)
                b1 = self.bank()
                self.o("pe", "transpose", bc16(b1.ap)[:, 0:128], kt_.ap[:, tsl], self.identb, reads=kt_.all() + self.cst.all(), writes=b1.all())
                for c in range(4):
                    self.o("act", "activation", kmb[:, c, :], bc16(b1.ap)[:, 0:128], AF.Copy, scale=chm[:, c:c + 1],
                           reads=b1.all() + self.c1.all(), writes=km.k("k%d" % c))
                b2 = self.bank()
                self.o("pe", "matmul", b2.ap[:, 0:128], kt_.ap[:, tsl], qt.ap[:, tsl], start=True, stop=True, reads=kt_.all() + qt.all(), writes=b2.all())
                self.o("dve", "tensor_tensor", out=am, in0=b2.ap[:, 0:128], in1=self.maskBD, op=ALU.mult, reads=b2.all() + self.cst.all(), writes=km.k("am"))
                if tt == 0 and s0 is not None:
                    self.o("dve", "tensor_copy", am[0:1, 0:1], s0[:, h:h + 1], reads=self.sm.k("s0h"), writes=km.k("am"))
                bu = self.bank()
                nU = 4 if tt < NT - 1 else 3
                for c in range(nU):
                    self.o("pe", "matmul", bu.ap[:, c * 128:(c + 1) * 128], kmb[:, c, :], vvv[:, tt, :], start=True, stop=True,
                           reads=km.k("k%d" % c) + vv.all(), writes=bu.all())
                bo = self.bank()
                self.o("pe", "matmul", bo.ap[:, 0:128], vvv[:, tt, :], am, start=True, stop=False, reads=vv.all() + km.k("am"), writes=bo.all())
                return bo, bu, nU

            def chain(tt, bu, nU):
                for c in range(nU):
                    gi = tt * 4 + c
                    sbap, sbk = Sb(gi)
                    if gi == 0:
                        self.o("dve", "tensor_copy", St, bu.ap[:, 0:128], reads=bu.all(), writes=skS)
                    else:
                        eLp = EQ.ap[:, (gi - 1) * 32 + 31:(gi - 1) * 32 + 32]
                        self.o("dve", "scalar_tensor_tensor", out=St, in0=St, scalar=eLp, in1=bu.ap[:, c * 128:(c + 1) * 128], op0=ALU.mult, op1=ALU.add,
                               reads=bu.all() + EQ.all() + skS, writes=skS)
                    eL = EQ.ap[:, gi * 32 + 31:gi * 32 + 32]
                    self.o("dve", "tensor_scalar", out=sbap, in0=St, scalar1=eL, scalar2=None, op0=ALU.mult, reads=skS + EQ.all(), writes=sbk)

            def readout(tt, bo):
                for c in range(4):
                    gi = tt * 4 + c
                    if gi == 0:
                        continue
                    sbap, sbk = Sb(gi - 1)
                    self.o("pe", "matmul", bo.ap[:, c * 32:(c + 1) * 32], sbap, qt.ap[:, gi * 32:(gi + 1) * 32], start=False, stop=(c == 3),
                           reads=sbk + qt.all(), writes=bo.all())
                self.o("act", "activation", oT.ap[:, tt * 128:(tt + 1) * 128], bo.ap[:, 0:128], AF.Copy, reads=bo.all(), writes=oT.all())

            cur = front(0)
            for tt in range(NT):
                bo, bu, nU = cur
                chain(tt, bu, nU)
                if tt + 1 < NT:
                    cur = front(tt + 1)
                readout(tt, bo)
            sq = kt_
            self.o("act", "activation", sq.ap[:, :], oT.ap[:, 0:N], AF.Square, reads=oT.all(), writes=sq.all())
            rstd = self.F[0]
            for j in range(4):
                bs = self.bank()
                self.o("pe", "matmul", bs.ap[:, :], self.onesb, sq.ap[:, j * 512:(j + 1) * 512], start=True, stop=True, reads=sq.all() + self.cst.all(), writes=bs.all())
                self.o("act", "activation", rstd.ap[:, j * 512:(j + 1) * 512], bs.ap[:, :], AF.Sqrt, scale=1.0 / HD, bias=EPS, reads=bs.all(), writes=rstd.all())
            self.o("dve", "reciprocal", rstd.ap[:, 0:N], rstd.ap[:, 0:N], reads=rstd.all(), writes=rstd.all())
            self.o("dve", "scalar_tensor_tensor", out=oT.ap[:, 0:N], in0=oT.ap[:, 0:N], scalar=self.vec.ap[:, 624 + h:625 + h], in1=rstd.ap[:, 0:N],
                   op0=ALU.mult, op1=ALU.mult, reads=oT.all() + rstd.all() + self.vec.all(), writes=oT.all())
            self.o("dve", "tensor_tensor", out=qt.ap[:, :], in0=oT.ap[:, 0:N], in1=gs.ap[:, :], op=ALU.mult, reads=oT.all() + gs.all(), writes=qt.all())
            sc.dma("sp", mixT.ap[(8 + h) * 128:(9 + h) * 128, :], qt.ap[:, :], reads=qt.all(), writes=mixT.k(8 + h))
        self.o("pool", "memset", self.ones.ap[:, :], 1.0, writes=self.ones.all())

    def final_out(self, hT, out_ap, outT):
        sc = self.sc
        bs = [self.bank() for _ in range(4)]
        rstd = self.F[5]
        if getattr(self, "ssq_ready", False):
            acc = self.F[0]
            for j in range(4):
                self.o("pe", "matmul", bs[j].ap[:, :], self.onesF, acc.ap[:, j * 512:(j + 1) * 512], start=True, stop=True,
                       reads=acc.all() + self.cstf.all(), writes=bs[j].all())
            self.ssq_ready = False
        else:
          for c in range(16):
            ht = self.F[c % 2]
            sq = self.H[c % 2]
            sc.dma(self.q(), ht.ap[:, 0:2048], hT.ap[c * 128:(c + 1) * 128, :], reads=hT.k(c), writes=ht.all())
            self.o("act", "activation", sq.ap[:, :], ht.ap[:, 0:2048], AF.Square, reads=ht.all(), writes=sq.all())
            for j in range(4):
                self.o("pe", "matmul", bs[j].ap[:, :], self.onesb, sq.ap[:, j * 512:(j + 1) * 512], start=(c == 0), stop=(c == 15),
                       reads=sq.all() + self.cst.all(), writes=bs[j].all())
        for j in range(4):
            self.o("act", "activation", rstd.ap[:, j * 512:(j + 1) * 512], bs[j].ap[:, :], AF.Sqrt, scale=1.0 / D, bias=EPS,
                   reads=bs[j].all(), writes=rstd.all())
        self.o("dve", "reciprocal", rstd.ap[:, 0:2048], rstd.ap[:, 0:2048], reads=rstd.all(), writes=rstd.all())
        hv = hT.ap.rearrange("(c p) t -> p c t", p=128)
        for tt in range(NT):
            ht = self.F[tt % 2]
            yn = self.F[2 + tt % 2]
            ot = self.F[4]
            htv = ht.ap[:, 0:2048].rearrange("p (c t) -> p c t", c=16)
            ynv = yn.ap[:, 0:2048].rearrange("p (c t) -> p c t", c=16)
            sc.dma(self.q(), htv, hv[:, :, tt * 128:(tt + 1) * 128], reads=hT.k(*range(16)), writes=ht.all())
            for c in range(16):
                self.o("dve", "scalar_tensor_tensor", out=ynv[:, c, :], in0=htv[:, c, :], scalar=self.vec.ap[:, 64 + c:65 + c],
                       in1=rstd.ap[:, tt * 128:(tt + 1) * 128], op0=ALU.mult, op1=ALU.mult,
                       reads=ht.all() + rstd.all() + self.vec.all(), writes=yn.all())
            for g in range(4):
                b = self.bank()
                for j in range(4):
                    c = g * 4 + j
                    self.o("pe", "transpose", b.ap[:, j * 128:(j + 1) * 128], ynv[:, c, :], self.identf, reads=yn.all() + self.cstf.all(), writes=b.all())
                if g % 2:
                    self.o("act", "activation", ot.ap[:, g * 512:(g + 1) * 512], b.ap[:, :], AF.Copy, reads=b.all(), writes=ot.all())
                else:
                    self.o("dve", "tensor_copy", ot.ap[:, g * 512:(g + 1) * 512], b.ap[:, :], reads=b.all(), writes=ot.all())
            sc.dma("sp", out_ap[tt * 128:(tt + 1) * 128, :], ot.ap[:, 0:2048], reads=ot.all(), writes=outT.all())


def _host_consts():
    bf = ml_dtypes.bfloat16
    i = np.arange(128)
    cst = np.zeros((128, 1024), dtype=np.float32)
    cst[:, 0:128] = np.eye(128)
    cst[:, 128:256] = 1.0
    cst[:, 256:384] = (i[:, None] <= i[None, :])
    cst[:, 384:512] = np.where(i[:, None] > i[None, :], -30000.0, 0.0)
    cst[:, 512:640] = (i[:, None] <= i[None, :]) & ((i[:, None] // 32) == (i[None, :] // 32))
    cst[:, 640:768] = (i[:, None] > i[None, :])
    cst[:, 768:896] = np.where(i[None, :] >= i[:, None], -30000.0, 0.0)
    cstf = np.zeros((128, 512), np.float32)
    cstf[:, 0:128] = np.eye(128)
    cstf[:, 128:256] = (i[:, None] <= i[None, :])
    cstf[:, 256:384] = 1.0
    cstf[:, 384:512] = (i[None, :] < i[:, None])
    half = 16
    freqs = (500000.0 ** (-np.arange(half, dtype=np.float32) / half)).astype(np.float32)

    def cs(pos):
        ang = pos.astype(np.float32)[:, None] * freqs
        return np.concatenate([np.cos(ang), np.sin(ang)], -1).astype(np.float32)

    c1 = np.zeros((128, 1536), np.float32)
    c1[:, 0:512] = cs(np.arange(S)).reshape(16, 128, 32).transpose(1, 0, 2).reshape(128, 512)
    cend = np.arange(127) * 16 + 31
    c1[0:127, 512:544] = cs(cend)
    t = np.arange(S)
    jb = np.arange(32)
    blk = t // 64
    started = jb[None, :] <= blk[:, None]
    forced = (jb[None, :] == 0) | (started & (blk[:, None] - jb[None, :] < 2))
    sb = np.where(started, np.where(forced, 1000.0, 0.0), -1e30).astype(np.float32)
    c1[:, 544:1056] = sb.reshape(16, 128, 32).transpose(1, 0, 2).reshape(128, 512)
    c1[:, 1056:1060] = (i[:, None] // 32 == np.arange(4)[None, :])
    Ef = (np.arange(S)[None, :] // 64 == np.arange(32)[:, None]).astype(np.float32)
    vis = np.zeros((128, 2048), np.float32)
    vis[0:127] = (cend[:, None] <= t[None, :])
    cs_ = np.arange(127) * 16
    ss = np.arange(32) * 64
    cover = ((cs_[:, None] < ss[None, :] + 64) & (cs_[:, None] + 32 > ss[None, :])).astype(np.float32)
    return dict(cst=cst.astype(bf), cstf=cstf, c1=c1, Ef=Ef.astype(bf), vis=np.ascontiguousarray(vis.astype(bf)).view(np.float32),
                cover=cover.astype(bf))


def _build(with_cd=True):
    nc = bass.Bass("TRN2", target_bir_lowering=False)
    es = contextlib.ExitStack()
    with es:
        sc = Sched(nc, es)
        P = Net(nc, es, sc)
        ext = {}
        ext["cst"] = P.dram("cst", [128, 1024], BF16, kind="ExternalInput")
        ext["cstf"] = P.dram("cstf", [128, 512], F32, kind="ExternalInput")
        ext["vec"] = P.dram("vec", [128, 1024], F32, kind="ExternalInput")
        ext["c1"] = P.dram("c1", [128, 1536], F32, kind="ExternalInput")
        ext["Ef"] = P.dram("Ef", [32, 2048], BF16, kind="ExternalInput")
        ext["vis"] = P.dram("vis", [128, 1024], F32, kind="ExternalInput")
        ext["cover"] = P.dram("cover", [127, 32], BF16, kind="ExternalInput")
        hn0 = P.dram("hn0", [128, 1024], F32, kind="ExternalInput")
        x = P.dram("x", [NSEQ * S, D], F32, kind="ExternalInput")
        out = P.dram("out", [NSEQ * S, D], F32, kind="ExternalOutput")
        wspec = [("ab_win", D, AB_IN), ("ab_wout", D, D), ("cd_win", D, CD_IN), ("cd_wout", D, D),
                 ("up0", D, 2 * FFN), ("up1", D, 2 * FFN), ("dn0", FFN, D), ("dn1", FFN, D), ("w1", 8192, 128), ("w2", 256, 128)]
        W = {}
        WF = {}
        crow = P.dram("crow", [1, 4096], F32, kind="ExternalInput")
        grow = P.dram("grow", [1, 2048], F32, kind="ExternalInput")
        casts = {}
        for nm, r, c in wspec:
            src = P.dram(nm, [r, c], F32, kind="ExternalInput")
            WF[nm] = src
            if nm.startswith("up"):
                W[nm] = P.dram(nm + "_b", [NF, 128, 16, 256], BF16)
                casts[nm] = (lambda src=src, dst=W[nm]: P.cast_up(src, dst))
            elif nm.startswith("dn"):
                W[nm] = P.dram(nm + "_b", [16, 128, NF, 128], BF16)
                casts[nm] = (lambda src=src, dst=W[nm]: P.cast_dn(src, dst))
            else:
                W[nm] = P.dram(nm + "_b", [r, c], BF16)
                casts[nm] = (lambda src=src, dst=W[nm], r=r, c=c, nm=nm: P.cast_weight(src, dst, r, c, split=(2048 if nm == "ab_win" else None)))
        hTs = [P.dram("hT%d" % i, [D, S], F32) for i in range(NSEQ)]
        qkT = P.dram("qkT", [D, S], BF16)
        sbT = P.dram("sbT", [D, S], BF16)
        tmv = P.dram("tmv", [S, 3072], BF16)
        mixT = P.dram("mixT", [D, S], BF16)
        actT = P.dram("actT", [FFN, S], BF16)
        fmT = P.dram("fmT", [28 * 128, S], BF16)
        P.setup(ext)
        P.setup_l1(ext)
        casts["ab_win"]()
        casts["ab_wout"]()
        P.deferred = [casts[k] for k in ("up0", "dn0", "cd_win", "cd_wout", "w1", "w2", "up1", "dn1")]
        P.x_to_hT(x.ap[0:S, :], x, hTs[0])
        for s_ in range(NSEQ):
            hT = hTs[s_]
            sc.avoid_pool = (s_ == 0)
            sc.dma("sp", P.hn.ap[:, :], hn0.ap[:, :], reads=hn0.all(), writes=P.hn.all())
            P.mixer_ab(hT, W["ab_win"], W["ab_wout"], qkT, sbT, tmv, None, mixT, Wf=WF["ab_win"], crow=crow)
            P.ffn(hT, 0, W["up0"], W["dn0"], actT)
            if with_cd:
                P.mixer_cd(hT, W["cd_win"], W["cd_wout"], W["w1"], W["w2"], fmT, tmv, mixT, ext, Wf=WF["cd_win"], grow=grow)
            P.ffn(hT, 1, W["up1"], W["dn1"], actT)
            if s_ + 1 < NSEQ:
                ready = P.ssq_ready
                P.x_to_hT(x.ap[(s_ + 1) * S:(s_ + 2) * S, :], x, hTs[s_ + 1], base=2)
                P.ssq_ready = ready
            P.final_out(hT, out.ap[s_ * S:(s_ + 1) * S, :], out)
        sc.finish([t.w for t in out.all() if t.w is not None])
        sc.replay()
    return nc


def _vec(p):
    v = np.zeros((128, 1024), np.float32)
    T16 = lambda g: np.asarray(g, np.float32).reshape(16, 128).T
    v[:, 0:16] = T16(p["norm_mix"][0])
    v[:, 16:32] = T16(p["norm_ffn"][0])
    v[:, 32:48] = T16(p["norm_ffn"][1])
    v[:, 48:64] = T16(p["norm_mix"][1])
    v[:, 64:80] = T16(p["norm_final"])
    for l in range(2):
        cw = np.concatenate([p["ffn_conv_w"][l], p["ffn_conv_b"][l][None]], 0)
        v[:, 80 + l * 176:80 + (l + 1) * 176] = cw.reshape(4, 44, 128).transpose(2, 1, 0).reshape(128, 176)
    cwv = np.concatenate([p["ab_conv_w"][0], p["ab_conv_b"][0][None]], 0)
    v[:, 432:512] = cwv.reshape(5, 16, 128).transpose(2, 1, 0).reshape(128, 80)
    v[:, 512:528] = p["ab_gate_b"][0][None, :]
    v[:, 544:608] = p["cd_cmp_pos"][0].transpose(2, 0, 1).reshape(128, 64)
    v[:, 608:616] = p["hgrn_gamma"][0].reshape(8, 128).T
    v[:, 616:624] = p["hgrn_gamma"][1].reshape(8, 128).T
    v[:, 624:632] = p["cd_head_norm"][0].reshape(8, 128).T
    return v


def kernel(**p):
    p = {k: np.asarray(v) for k, v in p.items()}
    hc = _host_consts()
    vec = _vec(p)
    hn0 = np.ascontiguousarray(np.broadcast_to(p["ab_head_norm"][0][None, :], (128, 1024))).astype(np.float32)
    crow = np.concatenate([p["ab_conv_w"][0][3], p["ab_conv_b"][0]])[None, :].astype(np.float32)
    grow = np.concatenate([p["hgrn_gamma"][0], p["hgrn_gamma"][1]])[None, :].astype(np.float32)
    shared = {"crow": np.ascontiguousarray(crow), "grow": np.ascontiguousarray(grow), "cst": hc["cst"], "cstf": hc["cstf"], "vec": vec, "c1": hc["c1"], "Ef": hc["Ef"], "vis": hc["vis"], "cover": hc["cover"],
              "hn0": hn0, "ab_win": p["ab_w_in"][0], "ab_wout": p["ab_w_out"][0], "cd_win": p["cd_w_in"][0], "cd_wout": p["cd_w_out"][0],
              "up0": p["ffn_w_up"][0], "up1": p["ffn_w_up"][1], "dn0": p["ffn_w_down"][0], "dn1": p["ffn_w_down"][1],
              "w1": np.ascontiguousarray(p["cd_cmp_w1"][0].reshape(8192, 128)), "w2": np.ascontiguousarray(p["cd_cmp_w2"][0].reshape(256, 128))}
    nc = _build()
    in_maps = []
    for c in range(8):
        m = dict(shared)
        m["x"] = np.ascontiguousarray(p["x"][2 * c:2 * c + 2].reshape(NSEQ * S, D))
        in_maps.append(m)
    res = run_bass_kernel_spmd(nc, in_maps, core_ids=list(range(8)))
    outs = [np.asarray(r["out"]).reshape(NSEQ, S, D) for r in res.results]
    return np.concatenate(outs, 0).astype(np.float32)
```
